# Optimizing a Trainium2 kernel written in Bass

```python
import math
import jax
import jax.numpy as jnp
from jax import lax
import numpy as np

D_MODEL = 1024
BATCH = 8
SEQ = 8192
DEPTH = 1
DEC_BATCH = 16
DEC_SEQ = 32
PAST_LEN = 1024

CHUNK = 64
EPS = 1e-6
D_INNER = 2 * D_MODEL
M_HEAD_DIM = 64
M_HEADS = D_INNER // M_HEAD_DIM
M_GROUPS = 4
M_HPG = M_HEADS // M_GROUPS
D_STATE = 128
CONV_W = 4
CONV_DIM = D_INNER + 2 * M_GROUPS * D_STATE
A_HEADS = 16
A_KV = 4
A_REP = A_HEADS // A_KV
A_HEAD_DIM = 64
A_WIDTH = A_HEADS * A_HEAD_DIM
WINDOW = 128
BAND = WINDOW // CHUNK
ROT_DIM = A_HEAD_DIM // 4
ROPE_THETA = 500000.0
ATTN_ROWS = min(WINDOW, PAST_LEN)
D_FF = 2816
IN_SIZES = (D_INNER, CONV_DIM, M_HEADS, A_WIDTH, A_KV * A_HEAD_DIM, A_KV * A_HEAD_DIM, D_MODEL, D_MODEL)
D_IN_PROJ = sum(IN_SIZES)
IN_SPLITS = tuple(int(s) for s in np.cumsum(IN_SIZES)[:-1])

kernel_name = 'macaron_ssd_swa_sink_gated_hybrid_stream_step'


def _rmsnorm(x, g):
    xf = x.astype(jnp.float32)
    y = xf * lax.rsqrt(jnp.mean(xf * xf, axis=-1, keepdims=True) + EPS)
    return (y * g.astype(jnp.float32)).astype(x.dtype)


def _swiglu(x, w_gu, w_down):
    gate, up = jnp.split(x @ w_gu, 2, axis=-1)
    return (jax.nn.silu(gate) * up) @ w_down


def _rope(x, pos):
    half = ROT_DIM // 2
    inv_freq = ROPE_THETA ** (-jnp.arange(0, ROT_DIM, 2, dtype=jnp.float32) / ROT_DIM)
    ang = pos.astype(jnp.float32)[:, None] * inv_freq[None, :]
    cos = jnp.cos(ang)[:, None, :]
    sin = jnp.sin(ang)[:, None, :]
    xf = x.astype(jnp.float32)
    x1, x2 = xf[..., :half], xf[..., half:ROT_DIM]
    out = jnp.concatenate([x1 * cos - x2 * sin, x2 * cos + x1 * sin, xf[..., ROT_DIM:]], axis=-1)
    return out.astype(x.dtype)


def _causal_conv(xbc, buf, w, b):
    L = xbc.shape[1]
    full = jnp.concatenate([buf.astype(xbc.dtype), xbc], axis=1)
    out = sum((full[:, j:j + L] * w[j] for j in range(CONV_W)), b)
    return jax.nn.silu(out), full[:, -(CONV_W - 1):]


def _ssd_chunk(xh, dt, Bm, Cm, A, h0):
    L = xh.shape[1]
    acs = jnp.cumsum(dt * A, axis=1)
    causal = jnp.tril(jnp.ones((L, L), dtype=bool))[None, :, :, None, None]
    seg = jnp.exp(jnp.where(causal, acs[:, :, None] - acs[:, None, :], -jnp.inf))
    xdt = xh * dt[..., None]
    cb = jnp.einsum('blgn,bsgn->blsg', Cm, Bm)
    y = jnp.einsum('blsgr,bsgrp->blgrp', cb[..., None] * seg, xdt)
    y = y + jnp.einsum('blgn,bgrpn->blgrp', Cm, h0) * jnp.exp(acs)[..., None]
    to_end = jnp.exp(acs[:, -1:] - acs)
    h = h0 * jnp.exp(acs[:, -1])[..., None, None] + jnp.einsum('blgrp,blgn->bgrpn', xdt * to_end[..., None], Bm)
    return y, h


def _mamba(z, xbc, dt_raw, conv_buf, h0, p):
    f32 = jnp.float32
    xbc, conv_out = _causal_conv(xbc, conv_buf, p['conv_w'], p['conv_b'])
    bsz, L = xbc.shape[:2]
    xs, Bm, Cm = jnp.split(xbc, [D_INNER, D_INNER + M_GROUPS * D_STATE], axis=-1)
    xh = xs.reshape(bsz, L, M_GROUPS, M_HPG, M_HEAD_DIM).astype(f32)
    Bm = Bm.reshape(bsz, L, M_GROUPS, D_STATE).astype(f32)
    Cm = Cm.reshape(bsz, L, M_GROUPS, D_STATE).astype(f32)
    dt = jax.nn.softplus(dt_raw.astype(f32) + p['dt_bias'].astype(f32)).reshape(bsz, L, M_GROUPS, M_HPG)
    A = -jnp.exp(p['a_log'].astype(f32)).reshape(M_GROUPS, M_HPG)
    h = h0.astype(f32).reshape(bsz, M_GROUPS, M_HPG, M_HEAD_DIM, D_STATE)
    cl = min(L, CHUNK)
    nc = L // cl

    def to_chunks(t):
        return jnp.moveaxis(t.reshape(bsz, nc, cl, *t.shape[2:]), 1, 0)

    def step(hc, inp):
        yc, hn = _ssd_chunk(*inp, A, hc)
        return hn, yc

    h_last, ys = lax.scan(step, h, (to_chunks(xh), to_chunks(dt), to_chunks(Bm), to_chunks(Cm)))
    y = jnp.moveaxis(ys, 0, 1).reshape(bsz, L, M_GROUPS, M_HPG, M_HEAD_DIM)
    y = y + p['d_skip'].astype(f32).reshape(M_GROUPS, M_HPG)[..., None] * xh
    y = y.reshape(bsz, L, D_INNER).astype(z.dtype)
    y = _rmsnorm(y * jax.nn.silu(z), p['m_norm_g'])
    ssm_out = h_last.reshape(bsz, M_HEADS, M_HEAD_DIM, D_STATE).astype(z.dtype)
    return y, conv_out, ssm_out


def _sink_attention(q, k, v, sink, key_valid):
    s = jnp.einsum('...qkrd,...skd->...krqs', q, k, preferred_element_type=jnp.float32) * (A_HEAD_DIM ** -0.5)
    if key_valid is not None:
        s = jnp.where(key_valid[..., None, None, None, :], s, -jnp.inf)
    sk = sink.astype(jnp.float32).reshape(A_KV, A_REP)[:, :, None, None]
    m = jnp.maximum(jnp.max(s, axis=-1, keepdims=True), sk)
    e = jnp.exp(s - m)
    prob = e / (jnp.sum(e, axis=-1, keepdims=True) + jnp.exp(sk - m))
    return jnp.einsum('...krqs,...skd->...qkrd', prob.astype(v.dtype), v)


def _attn_prompt(q, k, v, sink):
    bsz, L = q.shape[:2]
    nc = L // CHUNK
    qc = q.reshape(bsz, nc, CHUNK, A_KV, A_REP, A_HEAD_DIM)

    def band(t):
        tc = t.reshape(bsz, nc, CHUNK, A_KV, A_HEAD_DIM)
        tp = jnp.pad(tc, ((0, 0), (BAND, 0), (0, 0), (0, 0), (0, 0)))
        return jnp.concatenate([tp[:, j:j + nc] for j in range(BAND + 1)], axis=2)

    key_chunk = jnp.arange(nc)[:, None] - BAND + jnp.arange(BAND + 1)[None, :]
    valid = jnp.repeat(key_chunk >= 0, CHUNK, axis=1)
    o = _sink_attention(qc, band(k), band(v), sink, valid)
    return o.reshape(bsz, L, A_WIDTH)


def _attn_sample(q, k, v, k_cache, v_cache, sink):
    bsz, L = q.shape[:2]
    kk = jnp.concatenate([k_cache.astype(k.dtype), k], axis=1)
    vv = jnp.concatenate([v_cache.astype(v.dtype), v], axis=1)
    o = _sink_attention(q.reshape(bsz, L, A_KV, A_REP, A_HEAD_DIM), kk, vv, sink, None)
    return o.reshape(bsz, L, A_WIDTH)


def _layer(x, pos, state, p):
    x = x + 0.5 * _rmsnorm(_swiglu(_rmsnorm(x, p['ffn1_pre_g']), p['ffn1_w_gu'], p['ffn1_w_down']), p['ffn1_post_g'])
    u = _rmsnorm(x, p['mix_pre_g'])
    bsz, L = u.shape[:2]
    z, xbc, dt_raw, q, k, v, g_m, g_a = jnp.split(u @ p['w_in'], IN_SPLITS, axis=-1)
    q = _rope(q.reshape(bsz, L, A_HEADS, A_HEAD_DIM), pos)
    k = _rope(k.reshape(bsz, L, A_KV, A_HEAD_DIM), pos)
    v = v.reshape(bsz, L, A_KV, A_HEAD_DIM)
    if state is None:
        conv_buf = jnp.zeros((bsz, CONV_W - 1, CONV_DIM), u.dtype)
        h0 = jnp.zeros((bsz, M_HEADS, M_HEAD_DIM, D_STATE), jnp.float32)
        attn = _attn_prompt(q, k, v, p['attn_sink'])
        rows = min(WINDOW, L)
        k_out, v_out = k[:, L - rows:], v[:, L - rows:]
    else:
        conv_buf, h0, k_cache, v_cache = state
        attn = _attn_sample(q, k, v, k_cache, v_cache, p['attn_sink'])
        k_out, v_out = k, v
    y_m, conv_out, ssm_out = _mamba(z, xbc, dt_raw, conv_buf, h0, p)
    mixed = jax.nn.sigmoid(g_m) * (y_m @ p['w_br_m']) + jax.nn.sigmoid(g_a) * (attn @ p['w_br_a'])
    x = x + _rmsnorm(mixed @ p['w_o'], p['mix_post_g'])
    x = x + 0.5 * _rmsnorm(_swiglu(_rmsnorm(x, p['ffn2_pre_g']), p['ffn2_w_gu'], p['ffn2_w_down']), p['ffn2_post_g'])
    return x, (conv_out, ssm_out, k_out, v_out)


def setup_inputs(seed: int = 0) -> dict:
    key = jax.random.key(seed)
    ks = iter(jax.random.split(key, 32))
    f32 = jnp.float32

    def nrm(shape, scale):
        return jax.random.normal(next(ks), shape, f32) * scale

    def gain(shape):
        return 1.0 + nrm(shape, 0.02)

    dt_u = jax.random.uniform(next(ks), (DEPTH, M_HEADS), f32)
    dt0 = jnp.exp(dt_u * (math.log(0.1) - math.log(0.001)) + math.log(0.001))
    dt_bias = dt0 + jnp.log(-jnp.expm1(-dt0))
    a_log = jnp.log(jax.random.uniform(next(ks), (DEPTH, M_HEADS), f32, minval=1.0, maxval=16.0))
    return {
        'x_prompt': nrm((BATCH, SEQ, D_MODEL), 1.0),
        'x_sample': nrm((DEC_BATCH, DEC_SEQ, D_MODEL), 1.0),
        'state_conv': nrm((DEPTH, DEC_BATCH, CONV_W - 1, CONV_DIM), 1.0),
        'state_ssm': nrm((DEPTH, DEC_BATCH, M_HEADS, M_HEAD_DIM, D_STATE), 0.1),
        'cache_k': nrm((DEPTH, DEC_BATCH, ATTN_ROWS, A_KV, A_HEAD_DIM), 1.0),
        'cache_v': nrm((DEPTH, DEC_BATCH, ATTN_ROWS, A_KV, A_HEAD_DIM), 1.0),
        'ffn1_pre_g': gain((DEPTH, D_MODEL)),
        'ffn1_w_gu': nrm((DEPTH, D_MODEL, 2 * D_FF), D_MODEL ** -0.5),
        'ffn1_w_down': nrm((DEPTH, D_FF, D_MODEL), D_FF ** -0.5),
        'ffn1_post_g': gain((DEPTH, D_MODEL)),
        'mix_pre_g': gain((DEPTH, D_MODEL)),
        'w_in': nrm((DEPTH, D_MODEL, D_IN_PROJ), D_MODEL ** -0.5),
        'conv_w': nrm((DEPTH, CONV_W, CONV_DIM), CONV_W ** -0.5),
        'conv_b': nrm((DEPTH, CONV_DIM), 0.02),
        'dt_bias': dt_bias,
        'a_log': a_log,
        'd_skip': gain((DEPTH, M_HEADS)),
        'm_norm_g': gain((DEPTH, D_INNER)),
        'attn_sink': nrm((DEPTH, A_HEADS), 1.0),
        'w_br_m': nrm((DEPTH, D_INNER, D_MODEL), D_INNER ** -0.5),
        'w_br_a': nrm((DEPTH, A_WIDTH, D_MODEL), A_WIDTH ** -0.5),
        'w_o': nrm((DEPTH, D_MODEL, D_MODEL), D_MODEL ** -0.5),
        'mix_post_g': gain((DEPTH, D_MODEL)),
        'ffn2_pre_g': gain((DEPTH, D_MODEL)),
        'ffn2_w_gu': nrm((DEPTH, D_MODEL, 2 * D_FF), D_MODEL ** -0.5),
        'ffn2_w_down': nrm((DEPTH, D_FF, D_MODEL), D_FF ** -0.5),
        'ffn2_post_g': gain((DEPTH, D_MODEL)),
    }


def reference(x_prompt, x_sample, state_conv, state_ssm, cache_k, cache_v,
              ffn1_pre_g, ffn1_w_gu, ffn1_w_down, ffn1_post_g,
              mix_pre_g, w_in, conv_w, conv_b, dt_bias, a_log, d_skip, m_norm_g,
              attn_sink, w_br_m, w_br_a, w_o, mix_post_g,
              ffn2_pre_g, ffn2_w_gu, ffn2_w_down, ffn2_post_g):
    pos_p = jnp.arange(x_prompt.shape[1], dtype=jnp.int32)
    pos_s = PAST_LEN + jnp.arange(x_sample.shape[1], dtype=jnp.int32)
    yp, ys = x_prompt, x_sample
    new_p, new_s = [], []
    for l in range(DEPTH):
        p = {
            'ffn1_pre_g': ffn1_pre_g[l], 'ffn1_w_gu': ffn1_w_gu[l], 'ffn1_w_down': ffn1_w_down[l],
            'ffn1_post_g': ffn1_post_g[l], 'mix_pre_g': mix_pre_g[l], 'w_in': w_in[l],
            'conv_w': conv_w[l], 'conv_b': conv_b[l], 'dt_bias': dt_bias[l], 'a_log': a_log[l],
            'd_skip': d_skip[l], 'm_norm_g': m_norm_g[l], 'attn_sink': attn_sink[l],
            'w_br_m': w_br_m[l], 'w_br_a': w_br_a[l], 'w_o': w_o[l], 'mix_post_g': mix_post_g[l],
            'ffn2_pre_g': ffn2_pre_g[l], 'ffn2_w_gu': ffn2_w_gu[l], 'ffn2_w_down': ffn2_w_down[l],
            'ffn2_post_g': ffn2_post_g[l],
        }
        yp, st_p = _layer(yp, pos_p, None, p)
        ys, st_s = _layer(ys, pos_s, (state_conv[l], state_ssm[l], cache_k[l], cache_v[l]), p)
        new_p.append(st_p)
        new_s.append(st_s)
    conv_p, ssm_p, k_p, v_p = [jnp.stack(t) for t in zip(*new_p)]
    conv_s, ssm_s, k_s, v_s = [jnp.stack(t) for t in zip(*new_s)]
    return (yp, ys, conv_p, ssm_p, k_p, v_p, conv_s, ssm_s, k_s, v_s)
```

```python
import math
import threading
import numpy as np
import ml_dtypes
import concourse.bass as bass
import concourse.mybir as mybir
from concourse.bass_utils import run_bass_kernel_spmd

F32 = mybir.dt.float32
BF16 = mybir.dt.bfloat16
AF = mybir.ActivationFunctionType
ALU = mybir.AluOpType
AX = mybir.AxisListType
P = 128

D = 1024
DFF = 2816
DI = 2048
CONV = 3072
NH = 32
HP = 64
NS = 128
NG = 4
KC = 8
FC = 22
DINP = 8736
EPS = 1e-6
PAST_LEN = 1024
ROT = 16
THETA = 500000.0
OFF_Z, OFF_XBC, OFF_DT, OFF_Q, OFF_K, OFF_V, OFF_GM, OFF_GA = 0, 2048, 5120, 5152, 6176, 6432, 6688, 7712

ENGS = ("pe", "act", "dve", "pool", "sp")


class Buf:
    __slots__ = ("name", "last_w", "readers", "excl")

    def __init__(self, name="", pending=None):
        self.name = name
        self.excl = False
        self.last_w = None
        self.readers = list(pending) if pending else []


class Op:
    __slots__ = ("eng", "fn", "deps", "sig", "count", "key", "ord", "is_dma", "idx")


class DmaKey:
    def __init__(self, sem, grouped=False):
        self.sem = sem
        self.n = 0
        self.grouped = grouped


class Rec:
    def __init__(self, nc):
        self.nc = nc
        self.streams = {e: [] for e in ENGS}
        self.sems = {e: nc.alloc_semaphore("sem_" + e) for e in ENGS}
        self.final_keys = []
        self.sched = None

    def key(self, name, grouped=False):
        self.nk = getattr(self, "nk", 0) + 1
        return DmaKey(self.nc.alloc_semaphore("dk%d_%s" % (self.nk, name)), grouped)

    def op(self, eng, method, reads, writes, *args, key=None, **kwargs):
        o = Op()
        o.eng = eng
        o.fn = (method, args, kwargs)
        o.key = key
        o.is_dma = key is not None
        o.sig = False
        o.count = None
        o.idx = len(self.streams[eng])
        if key is not None:
            key.n += 1
            o.ord = key.n
        best = {}
        dmas = []
        ex = [b for b in reads if b.excl]
        if ex:
            writes = list(writes) + [b for b in ex if b not in writes]

        dk = {}

        def add(d, kind):
            if d is o:
                return
            if d.is_dma:
                c = dk.get(id(d.key))
                if c is None or d.ord > c.ord:
                    dk[id(d.key)] = d
                return
            if d.eng == eng and not o.is_dma:
                if eng == "pe" or kind != "raw":
                    return
            c = best.get(d.eng)
            if c is None or d.idx > c.idx:
                best[d.eng] = d

        for b in reads:
            if b.last_w is not None:
                add(b.last_w, "raw")
        for b in writes:
            if b.last_w is not None:
                add(b.last_w, "waw")
            for r in b.readers:
                add(r, "war")
        o.deps = list(best.values()) + list(dk.values())
        for d in best.values():
            d.sig = True
        for b in reads:
            b.readers.append(o)
        for b in writes:
            b.last_w = o
            b.readers = []
        self.streams[eng].append(o)
        if self.sched is not None:
            self.sched.tick(o)
        return o

    @staticmethod
    def pending(bufs):
        best = {}
        dk = {}
        for b in bufs:
            for d in ([b.last_w] if b.last_w is not None else []) + b.readers:
                if d.is_dma:
                    c = dk.get(id(d.key))
                    if c is None or d.ord > c.ord:
                        dk[id(d.key)] = d
                else:
                    c = best.get(d.eng)
                    if c is None or d.idx > c.idx:
                        best[d.eng] = d
        return list(dk.values()) + list(best.values())

    def emit(self):
        nc = self.nc
        for e in ENGS:
            c = 0
            for o in self.streams[e]:
                if o.sig and not o.is_dma:
                    c += 1
                    o.count = c
        sems = self.sems

        def runner(en):
            def f(e):
                waited = {}
                for o in self.streams[en]:
                    need = {}
                    for d in o.deps:
                        if d.is_dma:
                            sem = d.key.sem
                            val = 16 * (d.key.n if d.key.grouped else d.ord)
                        else:
                            sem = sems[d.eng]
                            val = d.count
                        k = id(sem)
                        if k not in need or need[k][1] < val:
                            need[k] = (sem, val)
                    items = [(s, v) for k, (s, v) in need.items() if waited.get(k, 0) < v]
                    for k, (s, v) in need.items():
                        if waited.get(k, 0) < v:
                            waited[k] = v
                    attach = None
                    if items and not o.is_dma:
                        attach = items.pop()
                    for s, v in items:
                        e.wait_ge(s, v)
                    m, a, kw = o.fn
                    ins = getattr(e, m)(*a, **kw)
                    if attach is not None:
                        ins._wait_ge(attach[0], attach[1])
                    if o.is_dma:
                        ins.then_inc(o.key.sem, 16)
                    elif o.sig:
                        ins.then_inc(sems[en], 1)
                if en == "sp":
                    for k in self.final_keys:
                        if k.n > 0:
                            e.wait_ge(k.sem, 16 * k.n)
            return f

        with nc.Block() as blk:
            blk.tensor(runner("pe"))
            blk.scalar(runner("act"))
            blk.vector(runner("dve"))
            blk.gpsimd(runner("pool"))
            blk.sync(runner("sp"))


class Sched:
    def __init__(self, ka, kb):
        self.cv = threading.Condition()
        self.turn = None
        self.alive = {"A": False, "B": False}
        self.cnt = {"A": 0, "B": 0}
        self.k = {"A": ka, "B": kb}
        self.local = threading.local()
        self.exc = None

    def me(self):
        return getattr(self.local, "name", None)

    def run(self, fa, fb):
        self.alive = {"A": True, "B": True}
        self.turn = "A"
        self.cnt = {"A": 0, "B": 0}

        def wrap(name, f):
            other = "B" if name == "A" else "A"

            def g():
                self.local.name = name
                with self.cv:
                    while self.turn != name:
                        self.cv.wait()
                try:
                    f()
                except BaseException as e:
                    self.exc = e
                finally:
                    with self.cv:
                        self.alive[name] = False
                        self.turn = other
                        self.cv.notify_all()
            return g

        ta = threading.Thread(target=wrap("A", fa))
        tb = threading.Thread(target=wrap("B", fb))
        ta.start()
        tb.start()
        ta.join()
        tb.join()
        self.turn = None
        if self.exc is not None:
            e = self.exc
            self.exc = None
            raise e

    def yield_(self):
        name = self.me()
        if name is None:
            return False
        other = "B" if name == "A" else "A"
        if not self.alive[other]:
            return False
        with self.cv:
            self.turn = other
            self.cv.notify_all()
            while self.turn != name:
                self.cv.wait()
        return True

    def tick(self, o):
        name = self.me()
        if name is None or o.is_dma or o.eng == "sp":
            return
        if name == "A":
            if o.eng != "pe":
                self.cnt["A"] += 1
        elif o.eng == "pe":
            self.cnt["B"] += 1
        if self.cnt[name] >= self.k[name]:
            self.cnt[name] = 0
            self.yield_()


class TT:
    def __init__(self, ap, nb=1, name=""):
        self.ap = ap
        self.b = [Buf(name) for _ in range(nb)]

    def renew(self, pending):
        self.b = [Buf("", pending) for _ in self.b]


def prod(xs):
    r = 1
    for x in xs:
        r *= x
    return r


class Blk:
    pass


class Tile:
    pass


class Seq:
    pass


DN_GROUPS = ((0, 8), (8, 8), (16, 6))
INTERLEAVE = True
KA = 6
KB = 8
VW = 72
DBG = set("m1,m2,m3,st,m4,m4a,m5".split(","))


def build(n_prompt, NB, n_samp=2, samp_len=32, stages="all"):
    nc = bass.Bass("TRN2", target_bir_lowering=False)
    R = Rec(nc)
    TMAX = 128 * NB
    nt = n_prompt // TMAX
    assert nt * TMAX == n_prompt
    NPOSX = n_prompt + n_samp * samp_len

    def din(name, shape, dt=F32):
        return nc.dram_tensor(name, list(shape), dt, kind="ExternalInput").ap()

    def dout(name, shape, dt=F32):
        return nc.dram_tensor(name, list(shape), dt, kind="ExternalOutput").ap()

    xp = din("xp", [n_prompt, D])
    xs = din("xs", [n_samp * samp_len, D])
    st_conv = din("st_conv", [n_samp, 3, CONV])
    st_ssm = din("st_ssm", [n_samp, DI, NS])
    ck = din("ck", [n_samp, 128, 256])
    cv = din("cv", [n_samp, 128, 256])
    w_gu = [din("w_gu1", [D, 2 * DFF]), din("w_gu2", [D, 2 * DFF])]
    w_dn = [din("w_dn1", [DFF, D]), din("w_dn2", [DFF, D])]
    w_in = din("w_in", [D, DINP])
    w_brm = din("w_brm", [DI, D])
    w_bra = din("w_bra", [D, D])
    w_o = din("w_o", [D, D])
    NPAR = PAR["_n"]
    params_d = din("params", [P, NPAR])
    NCB = CB["_n"]
    cbf_d = din("cbf", [P, NCB], BF16)
    cs_d = din("cs_t", [P, 2, NPOSX])

    yp = dout("yp", [n_prompt, D])
    ys = dout("ys", [n_samp * samp_len, D])
    conv_p = dout("conv_p", [3, CONV])
    ssm_p = dout("ssm_p", [DI, NS])
    rows_p = min(128, n_prompt)
    k_p = dout("k_p", [rows_p, 256])
    v_p = dout("v_p", [rows_p, 256])
    conv_s = dout("conv_s", [n_samp, 3, CONV])
    ssm_s = dout("ssm_s", [n_samp, DI, NS])
    k_s = dout("k_s", [n_samp * samp_len, 256])
    v_s = dout("v_s", [n_samp * samp_len, 256])

    def out_key(name):
        k = R.key(name)
        R.final_keys.append(k)
        return k

    def sb(name, shape, dt=F32):
        return nc.alloc_sbuf_tensor(name, list(shape), dt)

    def tt(name, shape, dt=F32, nb=1):
        t = sb(name, shape, dt)
        return TT(t[:], nb, name)

    class Arena:
        def __init__(self, name, nbytes):
            self.t = sb(name, [P, nbytes // 2], BF16)
            self.nbytes = nbytes
            self.off = 0
            self.hi = 0

        def carve(self, shape, dt, nb=1, name=""):
            nel = prod(shape[1:])
            esz = 2 if dt == BF16 else 4
            nbytes = (nel * esz + 31) // 32 * 32
            assert self.off + nbytes <= self.nbytes, (name, self.off, nbytes, self.nbytes)
            a = self.t[:, self.off // 2:(self.off + nel * esz) // 2]
            self.off += nbytes
            self.hi = max(self.hi, self.off)
            if dt == F32:
                a = a.bitcast(F32)
            if len(shape) == 3:
                a = a.rearrange("p (a b) -> p a b", b=shape[2])
            elif len(shape) == 4:
                a = a.rearrange("p (a b c) -> p a b c", b=shape[2], c=shape[3])
            return TT(a, nb, name)

    par = tt("par_sb", [P, NPAR])
    cbf = tt("cbf_sb", [P, NCB], BF16)
    ld_key = R.key("ld", grouped=True)
    R.op("sp", "dma_start", [], par.b, out=par.ap, in_=params_d, key=ld_key)
    R.op("sp", "dma_start", [], cbf.b, out=cbf.ap, in_=cbf_d, key=ld_key)

    def pv(name):
        o, n = PAR[name]
        return par.ap[:, o:o + n]

    def cv_(name):
        o, n = CB[name]
        return cbf.ap[:, o:o + n]

    ident = cv_("ident")
    tri = cv_("tri")
    m2 = cv_("m2")
    ones = cv_("ones")
    rm = cv_("rm")
    negm = cv_("negm").rearrange("p (h l) -> p h l", l=128)
    maskA = cv_("maskA").rearrange("p (h l) -> p h l", l=128)
    maskB = cv_("maskB").rearrange("p (h l) -> p h l", l=128)
    _oA = CB["maskA"][0]
    assert CB["maskB"][0] == _oA + 512
    maskAB = cbf.ap[:, _oA + 256:_oA + 768].rearrange("p (h l) -> p h l", l=128)
    identf = pv("identf")
    gpre = {0: pv("gpre1"), 1: pv("gpre2"), "m": pv("gmix")}
    gpost = {0: pv("gpost1"), 1: pv("gpost2"), "m": pv("gpostm")}
    gmn = pv("gmn")
    convw = pv("convw")
    convb = pv("convb")
    dtb = pv("dtb")
    dskip = pv("dskip")
    negcol = pv("negcol")

    dconst = tt("dconst", [P, 48])
    Aneg = dconst.ap[:, 0:32]
    esink = dconst.ap[:, 32:48]
    R.op("act", "activation", par.b, dconst.b, out=dconst.ap[:, 0:32], in_=pv("alog"), func=AF.Exp)
    R.op("act", "activation", par.b, dconst.b, out=dconst.ap[:, 32:48], in_=pv("sink"), func=AF.Exp)
    R.op("dve", "tensor_scalar", dconst.b, dconst.b, out=dconst.ap[:, 0:32], in0=dconst.ap[:, 0:32], scalar1=-1.0, scalar2=None, op0=ALU.mult)

    class PsPool:
        def __init__(self):
            self.banks = []
            for i in range(8):
                t = nc.alloc_psum_tensor("ps%d" % i, [P, 512], F32)
                tb = TT(t[:], 1, "ps%d" % i)
                tb.bf = t[:].bitcast(BF16)
                tb.b[0].excl = True
                tb.live = False
                self.banks.append(tb)
            self.i = 0

        def next(self):
            for attempt in range(10000):
                for _ in range(8):
                    b = self.banks[self.i]
                    self.i = (self.i + 1) % 8
                    if not b.live:
                        b.live = True
                        return b
                if R.sched is None or not R.sched.yield_():
                    break
            raise RuntimeError("psum exhausted")

        def rel(self, *bs):
            for b in bs:
                b.live = False

    psum = PsPool()

    SLOT = 4096
    units = {}
    unit_defs = {}

    NCK = 12
    cast_keys = [R.key("cast%d" % i) for i in range(NCK)]
    cast_dummy = [Buf("castd%d" % i) for i in range(NCK)]
    cast_i = [0]
    unit_order = []

    def def_unit(name, n, srcs, grp):
        unit_defs[name] = (n, srcs, grp)
        unit_order.append(name)

    def make_unit(name):
        n, srcs, grp = unit_defs[name]
        d = nc.dram_tensor("wb_" + name, [P, n], BF16, kind="Internal").ap()
        bs = []
        for dv, src in srcs:
            b = Buf(name)
            ci = cast_i[0] % NCK
            cast_i[0] += 1
            R.op("pool", "dma_start", [], [b, cast_dummy[ci]], out=dv(d), in_=src, key=cast_keys[ci])
            bs.append(b)
        units[name] = (d, bs, n)

    def get_unit(name):
        return units[name]

    def kcview(w, c0, ncols):
        return w.rearrange("(kc p) n -> p kc n", p=P)[:, :, c0:c0 + ncols]

    def v3(k):
        return lambda d: d.rearrange("p (k n) -> p k n", k=k)

    for f in range(2):
        wg = w_gu[f].rearrange("(kc p) (gu x) -> p kc gu x", p=P, gu=2)
        for u in range(11):
            srcs = []
            for gu in range(2):
                srcs.append(((lambda gu: lambda d: d.rearrange("p (kc gu x) -> p kc gu x", kc=KC, gu=2)[:, :, gu, :])(gu),
                             wg[:, :, gu, 2 * u * 128:2 * u * 128 + 256]))
            def_unit("gu%d_%d" % (f, u), KC * 512, srcs, "gu%d" % f)
        wd = w_dn[f].rearrange("(kc p) n -> p kc n", p=P)
        for half in range(2):
            for gi, (k0, nk) in enumerate(DN_GROUPS):
                def_unit("dn%d_%d_%d" % (f, half, gi), nk * 512, [(v3(nk), wd[:, k0:k0 + nk, half * 512:(half + 1) * 512])], "dn%d" % f)
    for u in range(6):
        def_unit("xbc%d" % u, KC * 512, [(v3(KC), kcview(w_in, OFF_XBC + u * 512, 512))], "win_a")
    def_unit("dtv", KC * 288,
             [(lambda d: d.rearrange("p (k n) -> p k n", k=KC)[:, :, 0:32], kcview(w_in, OFF_DT, 32)),
              (lambda d: d.rearrange("p (k n) -> p k n", k=KC)[:, :, 32:288], kcview(w_in, OFF_V, 256))], "win_a")
    for u in range(4):
        def_unit("z%d" % u, KC * 512, [(v3(KC), kcview(w_in, OFF_Z + u * 512, 512))], "win_a")
    for u in range(2):
        def_unit("q%d" % u, KC * 512, [(v3(KC), kcview(w_in, OFF_Q + u * 512, 512))], "win_b")
    def_unit("kdup", KC * 512,
             [((lambda j, t: lambda d: d.rearrange("p (k x) -> p k x", k=KC)[:, :, j * 128 + t * 64:j * 128 + t * 64 + 64])(j, t),
               kcview(w_in, OFF_K + j * 64, 64)) for j in range(4) for t in range(2)], "win_b")
    for u in range(2):
        def_unit("gm%d" % u, KC * 512, [(v3(KC), kcview(w_in, OFF_GM + u * 512, 512))], "win_b")
        def_unit("ga%d" % u, KC * 512, [(v3(KC), kcview(w_in, OFF_GA + u * 512, 512))], "win_b")
    wm = w_brm.rearrange("(kc p) n -> p kc n", p=P)
    for u in range(4):
        def_unit("brm%d" % u, 16 * 256, [(v3(16), wm[:, :, u * 256:(u + 1) * 256])], "brm")
    for u in range(2):
        def_unit("bra%d" % u, KC * 512, [(v3(KC), kcview(w_bra, u * 512, 512))], "bra")
    for u in range(2):
        def_unit("wo%d" % u, KC * 512, [(v3(KC), kcview(w_o, u * 512, 512))], "wo")

    def prepass(seq):
        for name in seq:
            if name not in units:
                make_unit(name)

    def ffn_unit_seq(f):
        s_ = []
        for u in range(11):
            s_.append("gu%d_%d" % (f, u))
        for half in range(2):
            for gi in range(3):
                s_.append("dn%d_%d_%d" % (f, half, gi))
        return s_

    def mix_unit_seq():
        s_ = []
        for u in range(6):
            s_.append("xbc%d" % u)
        s_.append("dtv")
        for u in range(4):
            s_.append("z%d" % u)
        for u in range(2):
            s_.append("q%d" % u)
        s_.append("kdup")
        for u in range(4):
            s_.append(("gm%d" if u % 2 == 0 else "ga%d") % (u // 2))
        for u in range(4):
            s_.append("brm%d" % u)
        for u in range(2):
            s_.append("bra%d" % u)
        for u in range(2):
            s_.append("wo%d" % u)
        return s_

    class WStream:
        def __init__(self, name, nslot):
            self.seq = []
            self.issued = 0
            self.cons = 0
            self.nslot = nslot
            self.look = nslot - 1
            self.ring = [tt("wslot%s%d" % (name, i), [P, SLOT], BF16) for i in range(nslot)]
            self.keys = [R.key("ring%s%d" % (name, i)) for i in range(nslot)]

        def _issue(self):
            i = self.issued
            d, bs, n = get_unit(self.seq[i])
            slot = self.ring[i % self.nslot]
            R.op("sp", "dma_start", bs, slot.b, out=slot.ap[:, 0:n], in_=d, key=self.keys[i % self.nslot])
            self.issued += 1

        def acquire(self, name):
            i = self.cons
            assert self.seq[i] == name, (self.seq[i], name)
            while self.issued < min(len(self.seq), i + 1 + self.look):
                self._issue()
            self.cons += 1
            return self.ring[i % self.nslot]

    tiles = []
    prompt = Seq()
    samp = [Seq() for _ in range(n_samp)]
    for t in range(nt):
        tl = Tile()
        tl.blocks = []
        for b in range(NB):
            k = Blk()
            k.n = 128
            k.seq = prompt
            k.pos0 = (t * NB + b) * 128
            k.first = (k.pos0 == 0)
            k.last = (k.pos0 + 128 == n_prompt)
            k.is_samp = False
            tl.blocks.append(k)
        r0 = t * TMAX
        tl.src = xp[r0:r0 + TMAX, :].rearrange("(b p) d -> p b d", p=128)
        tl.dst = yp[r0:r0 + TMAX, :].rearrange("(b p) d -> p b d", p=128)
        tl.runs = [(prompt, 0, TMAX)]
        tl.cscol = r0
        tl.is_samp = False
        tiles.append(tl)
    tl = Tile()
    tl.blocks = []
    for b in range(n_samp):
        k = Blk()
        k.n = samp_len
        k.seq = samp[b]
        k.pos0 = PAST_LEN
        k.first = True
        k.last = True
        k.is_samp = True
        k.sidx = b
        tl.blocks.append(k)
    tl.src = xs.rearrange("(b p) d -> p b d", p=samp_len)
    tl.dst = ys.rearrange("(b p) d -> p b d", p=samp_len)
    tl.runs = [(samp[b], b * samp_len, samp_len) for b in range(n_samp)]
    tl.cscol = n_prompt
    tl.is_samp = True
    tiles.append(tl)
    for tl in tiles:
        c = 0
        for bi, k in enumerate(tl.blocks):
            k.col0 = c
            k.bi = bi
            c += k.n
        tl.T = c
        tl.n = tl.blocks[0].n
        tl.nb = len(tl.blocks)

    prepass(ffn_unit_seq(0) + mix_unit_seq() + ffn_unit_seq(1))
    wsA = WStream("A", 3)
    wsB = WStream("B", 2)

    NBX = max(NB, n_samp)
    x_bufs = [tt("x%d" % i, [P, NBX, D], F32, nb=1) for i in range(2)]
    x_keys = [R.key("x%d" % i) for i in range(2)]
    xst_keys = [out_key("xst%d" % i) for i in range(2)]
    xnT = tt("xnT", [P, KC, TMAX], BF16, nb=NBX)
    uT = tt("uT", [P, KC, TMAX], BF16, nb=NBX)
    hT = tt("hT", [P, FC, TMAX], BF16, nb=FC)
    sg = [tt("sg%d" % i, [P, TMAX], F32) for i in range(2)]

    class Scr:
        def __init__(self, nm):
            self.junk = tt("junk" + nm, [P, 2048], BF16)
            self.etmp = tt("etmp" + nm, [P, 512], F32)
            self.stat = [tt("stat%s%d" % (nm, i), [P, 8], F32) for i in range(4)]
            self.i = 0

        def new_stat(self):
            s_ = self.stat[self.i % 4]
            self.i += 1
            return s_

    scrA = Scr("A")
    scrB = Scr("B")

    def load_x(ti):
        tl = tiles[ti]
        xb = x_bufs[ti % 2]
        R.op("sp", "dma_start", [], xb.b, out=xb.ap[0:tl.n, 0:tl.nb, :], in_=tl.src, key=x_keys[ti % 2])

    def store_x(ti):
        tl = tiles[ti]
        xb = x_bufs[ti % 2]
        R.op("sp", "dma_start", xb.b, [], out=tl.dst, in_=xb.ap[0:tl.n, 0:tl.nb, :], key=xst_keys[ti % 2])

    def rstd_from(ssq_ap, n, st, col, dim, extra):
        e2 = float(extra) ** 2
        R.op("act", "activation", st.b, st.b, out=st.ap[0:n, col:col + 1], in_=ssq_ap, func=AF.Sqrt, scale=1.0 / (dim * e2), bias=EPS / e2)
        R.op("dve", "reciprocal", st.b, st.b, out=st.ap[0:n, col:col + 1], in_=st.ap[0:n, col:col + 1])

    def norm_T(tl, xb, g_ap, dst, scr):
        for k in tl.blocks:
            n = k.n
            st = scr.new_stat()
            junk = scr.junk
            xin = xb.ap[0:n, k.bi, :]
            R.op("act", "activation", xb.b, junk.b + st.b, out=junk.ap[0:n, 0:D], in_=xin, func=AF.Square, accum_out=st.ap[0:n, 0:1])
            rstd_from(st.ap[0:n, 0:1], n, st, 1, D, 1.0)
            xn = junk.ap[0:n, 1024:2048]
            R.op("act", "activation", xb.b + st.b, junk.b, out=xn, in_=xin, func=AF.Copy, scale=st.ap[0:n, 1:2])
            ps = psum.next()
            ptv = ps.bf.rearrange("p (a b) -> p a b", b=128)
            for kc in range(KC):
                R.op("pe", "transpose", junk.b + cbf.b, ps.b, ptv[:, kc, 0:n], junk.ap[0:n, 1024 + kc * 128:1024 + (kc + 1) * 128], ident[0:n, 0:n])
            R.op("dve", "tensor_tensor", ps.b + par.b, [dst.b[k.bi]], out=dst.ap[:, :, k.col0:k.col0 + n], in0=ptv[:, :, 0:n],
                 in1=g_ap.unsqueeze(2).to_broadcast([P, KC, n]), op=ALU.mult)
            psum.rel(ps)

    def post_norm_add(xb, k, accs, g_ap, scale, scr):
        n = k.n
        st = scr.new_stat()
        junk = scr.junk
        for h in range(2):
            R.op("act", "activation", accs[h].b, junk.b + st.b, out=junk.ap[0:n, h * 512:(h + 1) * 512], in_=accs[h].ap[0:n, :], func=AF.Square,
                 accum_out=st.ap[0:n, h:h + 1])
        R.op("dve", "tensor_tensor", st.b, st.b, out=st.ap[0:n, 2:3], in0=st.ap[0:n, 0:1], in1=st.ap[0:n, 1:2], op=ALU.add)
        rstd_from(st.ap[0:n, 2:3], n, st, 3, D, scale)
        for h in range(2):
            et = scr.etmp
            R.op("dve", "scalar_tensor_tensor", accs[h].b + st.b + par.b, et.b, out=et.ap[0:n, :], in0=accs[h].ap[0:n, :], scalar=st.ap[0:n, 3:4],
                 in1=g_ap[0:n, h * 512:(h + 1) * 512], op0=ALU.mult, op1=ALU.mult)
            xsl = xb.ap[0:n, k.bi, h * 512:(h + 1) * 512]
            R.op("pool", "tensor_tensor", et.b + xb.b, xb.b, out=xsl, in0=et.ap[0:n, :], in1=xsl, op=ALU.add)

    def ffn(tl, xb, f):
        T = tl.T
        norm_T(tl, xb, gpre[f], xnT, scrB)
        for u in range(11):
            slot = wsB.acquire("gu%d_%d" % (f, u))
            wv = slot.ap.rearrange("p (kc gu jj c) -> p kc gu jj c", kc=KC, gu=2, jj=2)
            for jj in range(2):
                j = 2 * u + jj
                gps = psum.next()
                ups = psum.next()
                for gu, ps in ((0, gps), (1, ups)):
                    for kc in range(KC):
                        R.op("pe", "matmul", slot.b + xnT.b[0:tl.nb], ps.b, ps.ap[:, 0:T], lhsT=wv[:, kc, gu, jj, :], rhs=xnT.ap[:, kc, 0:T],
                             start=(kc == 0), stop=(kc == KC - 1))
                s = sg[j % 2]
                R.op("act", "activation", gps.b, s.b, out=s.ap[:, 0:T], in_=gps.ap[:, 0:T], func=AF.Tanh, scale=0.5)
                R.op("dve", "scalar_tensor_tensor", gps.b + s.b, s.b, out=s.ap[:, 0:T], in0=s.ap[:, 0:T], scalar=1.0, in1=gps.ap[:, 0:T],
                     op0=ALU.add, op1=ALU.mult)
                R.op("dve", "scalar_tensor_tensor", ups.b + s.b, [hT.b[j]], out=hT.ap[:, j, 0:T], in0=s.ap[:, 0:T], scalar=0.5, in1=ups.ap[:, 0:T],
                     op0=ALU.mult, op1=ALU.mult)
                psum.rel(gps, ups)
        accs = [[psum.next() for _ in tl.blocks] for _ in range(2)]
        for half in range(2):
            for gi, (k0, nk) in enumerate(DN_GROUPS):
                slot = wsB.acquire("dn%d_%d_%d" % (f, half, gi))
                wv = slot.ap[:, 0:nk * 512].rearrange("p (k n) -> p k n", k=nk)
                for kk in range(nk):
                    kc = k0 + kk
                    for k in tl.blocks:
                        R.op("pe", "matmul", slot.b + [hT.b[kc]], accs[half][k.bi].b, accs[half][k.bi].ap[0:k.n, :],
                             lhsT=hT.ap[:, kc, k.col0:k.col0 + k.n], rhs=wv[:, kk, :], start=(kc == 0), stop=(kc == FC - 1))
        for k in tl.blocks:
            a2 = [accs[0][k.bi], accs[1][k.bi]]
            post_norm_add(xb, k, a2, gpost[f], 0.5, scrB)
            psum.rel(*a2)

    do_mix = stages in ("all", "mix")
    if do_mix:
        for si, s in enumerate([prompt] + samp):
            s.halo = tt("halo%d" % si, [P, 24, 3], F32)
        hbufs = [(tt("hTa", [P, DI], F32, nb=4), tt("hTa_bf", [P, DI], BF16, nb=4))]
        prompt.h, prompt.hbf = hbufs[0]
        for i, s in enumerate(samp):
            s.h, s.hbf = hbufs[0]
        kT_all = tt("kT_all", [P, 4, (NB + 1) * 128], BF16, nb=NB + 1)
        v_all = tt("v_all", [P, NB + 1, 4, VW], BF16, nb=NB + 1)
        cacheT = [tt("cacheT%d" % i, [P, 4, 128], BF16) for i in range(n_samp)]
        cachev = [tt("cachev%d" % i, [P, 4, VW], BF16) for i in range(n_samp)]
        R.op("pool", "memset", [], prompt.halo.b, prompt.halo.ap, 0.0)
        R.op("pool", "memset", [], prompt.h.b, prompt.h.ap, 0.0)
        R.op("pool", "memset", [], prompt.hbf.b, prompt.hbf.ap, 0.0)
        R.op("pool", "memset", [], v_all.b, v_all.ap, 1.0)
        for c in cachev:
            R.op("pool", "memset", [], c.b, c.ap, 1.0)

        mx = Arena("mx", MX_BYTES[NB])
        ymT = mx.carve([P, 16, TMAX], BF16, nb=NBX, name="ymT")
        base = mx.off
        PREW = max(3 + TMAX, n_samp * (3 + samp_len))
        pre = [mx.carve([P, PREW], F32, name="pre") for _ in range(2)]
        acc = [mx.carve([P, TMAX], F32, name="acc") for _ in range(2)]
        xc = [mx.carve([P, TMAX], BF16, name="xc") for _ in range(2)]
        xh_tok = mx.carve([P, NBX, DI], BF16, nb=NBX, name="xh_tok")
        B_tok = mx.carve([P, NBX, 512], BF16, nb=NBX, name="B_tok")
        BT = mx.carve([P, 4, TMAX], BF16, name="BT")
        CT = mx.carve([P, 4, TMAX], BF16, name="CT")
        zs = mx.carve([P, NBX, DI], BF16, nb=NBX, name="zs")
        dtw = mx.carve([P, NBX, 32], F32, nb=NBX, name="dtw")
        dtA = mx.carve([P, NBX, 32], BF16, nb=NBX, name="dtA")
        xdt = mx.carve([P, DI], BF16, nb=4, name="xdt")
        xdtw = mx.carve([P, DI], BF16, nb=4, name="xdtw")
        rhs1 = [mx.carve([P, 8, 128], BF16, name="rhs1") for _ in range(2)]
        xd = mx.carve([P, DI], BF16, nb=4, name="xd")
        Eb = [mx.carve([P, 8, 128], BF16, name="E") for _ in range(2)]
        Gm = [mx.carve([P, 8, 128], BF16, name="G") for _ in range(2)]
        cbm = mx.carve([P, 4, 128], BF16, name="cbm")
        t1 = [mx.carve([P, 512], F32, name="t1") for _ in range(1)] * 2
        t2 = [mx.carve([P, 512], F32, name="t2") for _ in range(2)]
        yg = tt("yg", [P, DI], F32, nb=4)
        yn = scrA.junk
        eacs = mx.carve([P, 64], F32, name="eacs")
        hstage_ap = yg.ap.rearrange("p (c n) -> p c n", n=128)
        kvout0 = tt("kvout0", [P, 256], F32)
        M123 = pre + acc + xc + [xh_tok, B_tok, BT, CT, zs, dtw, dtA, xdt, xdtw] + rhs1 + [xd] + Eb + Gm + [cbm, t1[0]] + t2 + [yg, eacs, kvout0]
        mx.off = base
        cs = tt("cs_sb", [P, 2, TMAX], F32)
        qb = [mx.carve([P, TMAX], BF16, name="qb") for _ in range(2)]
        rt1 = [mx.carve([P, TMAX], F32, name="rt1") for _ in range(2)]
        rt2 = [mx.carve([P, TMAX], F32, name="rt2") for _ in range(2)]
        qT = mx.carve([P, 8, TMAX], BF16, nb=8, name="qT")
        PT = [mx.carve([P, 4, 128], BF16, name="PT") for _ in range(4)]
        attn_tok = [mx.carve([P, 1024], BF16, name="attn_tok") for _ in range(2)]
        den = [mx.carve([P, 8], F32, name="den") for _ in range(2)]
        attnT = mx.carve([P, 8, TMAX], BF16, nb=NBX, name="attnT")
        bt = [mx.carve([P, TMAX], F32, name="bt") for _ in range(2)]
        mixedT = mx.carve([P, 8, TMAX], BF16, nb=8, name="mixedT")
        krot_f = mx.carve([P, 4, 128], F32, name="krot_f")
        kvout1 = tt("kvout1", [P, 256], F32)
        gates = [mx.carve([P, 8, TMAX], BF16, nb=8, name="gate%d" % i) for i in range(2)]
        mixf = mx.carve([P, 8, TMAX], F32, nb=8, name="mixf")
        M456 = [cs] + qb + rt1 + rt2 + [qT] + PT + attn_tok + den + [attnT] + bt + [mixedT, krot_f, kvout1] + gates + [mixf]
        cs_key = R.key("cs")
        misc_keys = {}

        def misc_out(name, reads, **kw):
            k = out_key("o_" + name)
            R.op("sp", "dma_start", reads, [], key=k, **kw)

        def switch(frm, to):
            pend = Rec.pending([b for t_ in frm for b in t_.b])
            for t_ in to:
                t_.renew(pend)

        def bc3(ap2, a, b_):
            return ap2.unsqueeze(2).to_broadcast([ap2.shape[0], a, b_])

        def bcm(ap2, a, b_):
            return ap2.unsqueeze(1).to_broadcast([ap2.shape[0], a, b_])

        def lkey(nm):
            return R.key(nm)

        def load_halo(k):
            s = k.seq
            for r in range(3):
                R.op("sp", "dma_start", [], s.halo.b, out=s.halo.ap[:, :, r], in_=st_conv[k.sidx, r].rearrange("(c p) -> p c", p=P),
                     key=lkey("lh%d_%d" % (k.sidx, r)), allow_slow_non_contiguous=True)

        def load_seq_state_ssm(k):
            s = k.seq
            b = k.sidx
            R.op("sp", "dma_start", [], yg.b, out=hstage_ap, in_=st_ssm[b].rearrange("(c p) n -> p c n", p=P), key=lkey("ls%d" % b))
            for c4 in range(4):
                ps = psum.next()
                pv4 = ps.ap.rearrange("p (a b) -> p a b", b=128)
                for cc in range(4):
                    c = c4 * 4 + cc
                    R.op("pe", "transpose", [yg.b[c4]] + par.b, ps.b, pv4[:, cc, :], hstage_ap[:, c, :], identf)
                R.op("act", "activation", ps.b, [s.h.b[c4]], out=s.h.ap[:, c4 * 512:(c4 + 1) * 512], in_=ps.ap, func=AF.Copy)
                R.op("dve", "tensor_scalar", ps.b, [s.hbf.b[c4]], out=s.hbf.ap[:, c4 * 512:(c4 + 1) * 512], in0=ps.ap, scalar1=1.0, scalar2=None, op0=ALU.mult)
                psum.rel(ps)

        def load_seq_state_kv(k):
            b = k.sidx
            ckf = rt1[b % 2]
            cvf = rt2[b % 2]
            assert TMAX >= 256
            R.op("sp", "dma_start", [], ckf.b, out=ckf.ap[:, 0:256], in_=ck[b], key=lkey("lk%d" % b))
            R.op("sp", "dma_start", [], cvf.b, out=cvf.ap[:, 0:256], in_=cv[b], key=lkey("lv%d" % b))
            kd = attn_tok[b % 2]
            kdv = kd.ap[:, 0:512].rearrange("p (j t c) -> p j t c", j=4, t=2)
            for dd in range(2):
                R.op("dve", "tensor_copy", ckf.b, kd.b, out=kdv[:, :, dd, :], in_=ckf.ap[:, 0:256].rearrange("p (j c) -> p j c", c=64))
            ps = psum.next()
            ptv = ps.bf.rearrange("p (a b) -> p a b", b=128)
            for j in range(4):
                R.op("pe", "transpose", kd.b + cbf.b, ps.b, ptv[:, j, :], kd.ap[:, j * 128:(j + 1) * 128], ident)
            R.op("act", "activation", ps.b, cacheT[b].b, out=cacheT[b].ap, in_=ptv[:, 0:4, :], func=AF.Copy)
            psum.rel(ps)
            R.op("act", "activation", cvf.b, cachev[b].b, out=cachev[b].ap[:, :, 0:64], in_=cvf.ap[:, 0:256].rearrange("p (j c) -> p j c", c=64),
                 func=AF.Copy)

        def store_seq_state(k):
            s = k.seq
            if k.is_samp:
                cdst = conv_s[k.sidx]
                sdst = ssm_s[k.sidx]
            else:
                cdst = conv_p
                sdst = ssm_p
            for r in range(3):
                misc_out("conv%d" % r, s.halo.b, out=cdst[r].rearrange("(c p) -> p c", p=P), in_=s.halo.ap[:, :, r], allow_slow_non_contiguous=True)
            for c4 in range(4):
                ps = psum.next()
                pv4 = ps.ap.rearrange("p (a b) -> p a b", b=128)
                for cc in range(4):
                    c = c4 * 4 + cc
                    R.op("pe", "transpose", [s.h.b[c4]] + par.b, ps.b, pv4[:, cc, :], s.h.ap[:, c * 128:(c + 1) * 128], identf)
                R.op("act", "activation", ps.b, [yg.b[c4]], out=hstage_ap[:, c4 * 4:(c4 + 1) * 4, :], in_=pv4, func=AF.Copy)
                psum.rel(ps)
            misc_out("ssm", yg.b, out=sdst.rearrange("(c p) n -> p c n", p=P), in_=hstage_ap)

        def m1_xbc(tl):
            T = tl.T
            nruns = len(tl.runs)
            ln = tl.runs[0][2]
            W3 = 3 + ln
            n = tl.n
            slots = {}

            def views(c):
                pr = pre[c % 2]
                prv = pr.ap[:, 0:nruns * W3].rearrange("p (r w) -> p r w", w=W3)
                ac = acc[c % 2]
                acv = ac.ap[:, 0:T].rearrange("p (r w) -> p r w", w=ln)
                if c < 16:
                    dap, dbufs = xc[c % 2].ap[:, 0:T], xc[c % 2].b
                elif c < 20:
                    dap, dbufs = BT.ap[:, c - 16, 0:T], BT.b
                else:
                    dap, dbufs = CT.ap[:, c - 20, 0:T], CT.b
                return pr, prv, ac, acv, dap, dbufs

            def s1(c):
                u, cc = divmod(c, 4)
                if cc == 0:
                    slots[u] = wsA.acquire("xbc%d" % u)
                slot = slots[u]
                wv = slot.ap.rearrange("p (k cc c) -> p k cc c", k=KC, cc=4)
                pr, prv, ac, acv, dap, dbufs = views(c)
                ps = psum.next()
                for kc in range(KC):
                    R.op("pe", "matmul", slot.b + uT.b[0:tl.nb], ps.b, ps.ap[:, 0:T], lhsT=wv[:, kc, cc, :], rhs=uT.ap[:, kc, 0:T],
                         start=(kc == 0), stop=(kc == KC - 1))
                for r, (s_, col0, l_) in enumerate(tl.runs):
                    R.op("pool", "tensor_copy", s_.halo.b, pr.b, out=prv[:, r, 0:3], in_=s_.halo.ap[:, c, :])
                R.op("act", "activation", ps.b, pr.b, out=prv[:, :, 3:W3], in_=ps.ap[:, 0:T].rearrange("p (r w) -> p r w", w=ln), func=AF.Copy)
                psum.rel(ps)
                for r, (s_, col0, l_) in enumerate(tl.runs):
                    R.op("pool", "tensor_copy", pr.b, s_.halo.b, out=s_.halo.ap[:, c, :], in_=prv[:, r, ln:ln + 3])

            def s2(c):
                pr, prv, ac, acv, dap, dbufs = views(c)
                R.op("dve", "tensor_scalar", pr.b + par.b, ac.b, out=acv, in0=prv[:, :, 0:ln], scalar1=convw[:, c * 4:c * 4 + 1], scalar2=None, op0=ALU.mult)
                for j in range(1, 4):
                    R.op("dve", "scalar_tensor_tensor", pr.b + par.b + ac.b, ac.b, out=acv, in0=prv[:, :, j:j + ln],
                         scalar=convw[:, c * 4 + j:c * 4 + j + 1], in1=acv, op0=ALU.mult, op1=ALU.add)
                R.op("act", "activation", ac.b + par.b, dbufs, out=dap, in_=ac.ap[:, 0:T], func=AF.Silu, bias=convb[:, c:c + 1])

            def s3(c):
                if c >= 20:
                    return
                pr, prv, ac, acv, dap, dbufs = views(c)
                ps2 = psum.next()
                ptv = ps2.bf.rearrange("p (a b) -> p a b", b=128)
                for k in tl.blocks:
                    R.op("pe", "transpose", dbufs + cbf.b, ps2.b, ptv[0:n, k.bi, :], dap[:, k.col0:k.col0 + n], ident)
                if c < 16:
                    R.op("act", "activation", ps2.b, xh_tok.b[0:tl.nb], out=xh_tok.ap[0:n, 0:tl.nb, c * 128:(c + 1) * 128], in_=ptv[0:n, 0:tl.nb, :], func=AF.Copy)
                else:
                    g = c - 16
                    R.op("act", "activation", ps2.b, B_tok.b[0:tl.nb], out=B_tok.ap[0:n, 0:tl.nb, g * 128:(g + 1) * 128], in_=ptv[0:n, 0:tl.nb, :], func=AF.Copy)
                psum.rel(ps2)

            for c in range(24 + 2):
                if c < 24:
                    s1(c)
                if 0 <= c - 1 < 24:
                    s2(c - 1)
                if 0 <= c - 2 < 24:
                    s3(c - 2)

        def m2_dtv_z(tl):
            n = tl.n
            slot = wsA.acquire("dtv")
            wv = slot.ap[:, 0:KC * 288].rearrange("p (k n) -> p k n", k=KC)
            for k in tl.blocks:
                ps = psum.next()
                for kc in range(KC):
                    R.op("pe", "matmul", slot.b + [uT.b[k.bi]], ps.b, ps.ap[0:n, 0:288], lhsT=uT.ap[:, kc, k.col0:k.col0 + n], rhs=wv[:, kc, :],
                         start=(kc == 0), stop=(kc == KC - 1))
                dw = dtw.ap[0:n, k.bi, :]
                if "z1" in DBG:
                    psum.rel(ps)
                    continue
                R.op("dve", "tensor_tensor", ps.b + par.b, [dtw.b[k.bi]], out=dw, in0=ps.ap[0:n, 0:32], in1=dtb[0:n, :], op=ALU.add)
                R.op("act", "activation", [dtw.b[k.bi]], [dtw.b[k.bi]], out=dw, in_=dw, func=AF.Exp)
                R.op("act", "activation", [dtw.b[k.bi]], [dtw.b[k.bi]], out=dw, in_=dw, func=AF.Ln, bias=1.0)
                R.op("dve", "tensor_tensor", [dtw.b[k.bi]] + dconst.b, [dtA.b[k.bi]], out=dtA.ap[0:n, k.bi, :], in0=dw, in1=Aneg[0:n, :], op=ALU.mult)
                if "z2" in DBG:
                    psum.rel(ps)
                    continue
                vb = k.bi + 1
                if "z4" not in DBG:
                    R.op("act", "activation", ps.b, [v_all.b[vb]], out=v_all.ap[0:n, vb, :, 0:64], in_=ps.ap[0:n, 32:288].rearrange("p (j c) -> p j c", c=64),
                         func=AF.Copy)
                if k.last and "z5" not in DBG:
                    ko = kvout0
                    R.op("dve", "tensor_scalar", ps.b, ko.b, out=ko.ap[0:n, :], in0=ps.ap[0:n, 32:288], scalar1=1.0, scalar2=None, op0=ALU.mult)
                    if "z6" in DBG:
                        pass
                    elif k.is_samp:
                        misc_out("v", ko.b, out=v_s[k.sidx * samp_len:(k.sidx + 1) * samp_len, :], in_=ko.ap[0:n, :])
                    else:
                        misc_out("v", ko.b, out=v_p, in_=ko.ap[0:n, :])
                psum.rel(ps)
            for u in range(4):
                slot = wsA.acquire("z%d" % u)
                if "z3" in DBG:
                    continue
                wv = slot.ap.rearrange("p (k n) -> p k n", k=KC)
                for k in tl.blocks:
                    ps = psum.next()
                    for kc in range(KC):
                        R.op("pe", "matmul", slot.b + [uT.b[k.bi]], ps.b, ps.ap[0:n, :], lhsT=uT.ap[:, kc, k.col0:k.col0 + n], rhs=wv[:, kc, :],
                             start=(kc == 0), stop=(kc == KC - 1))
                    R.op("act", "activation", ps.b, [zs.b[k.bi]], out=zs.ap[0:n, k.bi, u * 512:(u + 1) * 512], in_=ps.ap[0:n, :], func=AF.Silu)
                    psum.rel(ps)

        def m3_ssd(tl, k, mid=None):
            L = k.n
            c0 = k.col0
            s = k.seq
            bi = k.bi
            dA = dtA.ap[0:L, bi, :]
            v3_ = lambda ap: ap.rearrange("p (h c) -> p h c", c=64)
            ps_s = psum.next()
            R.op("pe", "matmul", [dtA.b[bi]] + cbf.b, ps_s.b, ps_s.ap[0:L, 0:32], lhsT=tri[0:L, 0:L], rhs=dA, start=True, stop=True)
            R.op("pe", "matmul", [dtA.b[bi]] + cbf.b, ps_s.b, ps_s.ap[:, 32:64], lhsT=ones[0:L, :], rhs=dA, start=True, stop=True)
            R.op("act", "activation", ps_s.b, eacs.b, out=eacs.ap[0:L, 0:32], in_=ps_s.ap[0:L, 0:32], func=AF.Exp)
            R.op("act", "activation", ps_s.b, eacs.b, out=eacs.ap[:, 32:64], in_=ps_s.ap[:, 32:64], func=AF.Exp)
            psum.rel(ps_s)
            decay = eacs.ap[:, 32:64]
            ps_cb = psum.next()
            pcb = ps_cb.ap.rearrange("p (g l) -> p g l", l=128)
            for g in range(4):
                R.op("pe", "matmul", BT.b + CT.b, ps_cb.b, pcb[0:L, g, 0:L], lhsT=BT.ap[:, g, c0:c0 + L], rhs=CT.ap[:, g, c0:c0 + L], start=True, stop=True)
            R.op("dve", "tensor_tensor", ps_cb.b + cbf.b, cbm.b, out=cbm.ap[0:L, :, 0:L], in0=pcb[0:L, :, 0:L], in1=bcm(tri[0:L, 0:L], 4, L), op=ALU.mult)
            psum.rel(ps_cb)

            def stA(g):
                r1 = rhs1[g % 2]
                E = Eb[g % 2]
                gs = slice(g * 512, (g + 1) * 512)
                R.op("pool", "tensor_tensor", [dtA.b[bi]] + cbf.b, r1.b, out=r1.ap[0:L, :, 0:L], in0=bc3(dA[:, 8 * g:8 * g + 8], 8, L),
                     in1=bcm(tri[0:L, 0:L], 8, L), op=ALU.mult)
                R.op("pool", "tensor_tensor", [xh_tok.b[bi], dtw.b[bi]], [xdt.b[g]], out=v3_(xdt.ap[0:L, gs]),
                     in0=v3_(xh_tok.ap[0:L, bi, gs]), in1=bc3(dtw.ap[0:L, bi, 8 * g:8 * g + 8], 8, 64), op=ALU.mult)
                R.op("dve", "tensor_tensor", [xh_tok.b[bi]] + par.b, [xd.b[g]], out=v3_(xd.ap[0:L, gs]),
                     in0=v3_(xh_tok.ap[0:L, bi, gs]), in1=bc3(dskip[0:L, 8 * g:8 * g + 8], 8, 64), op=ALU.mult)
                for hh in range(2):
                    ps = psum.next()
                    pv_ = ps.ap.rearrange("p (h l) -> p h l", l=128)[0:L, :, 0:L]
                    h0 = 8 * g + 4 * hh
                    if L == 128:
                        R.op("pe", "matmul", r1.b + cbf.b, ps.b, ps.ap[:, :], lhsT=ones[0:L, 0:L], rhs=r1.ap[0:L, 4 * hh:4 * hh + 4, :].rearrange("p h l -> p (h l)"),
                             start=True, stop=False)
                        R.op("pe", "matmul", [dtA.b[bi]] + cbf.b, ps.b, ps.ap[:, :], lhsT=m2[0:L, 0:L], rhs=bc3(dA[:, h0:h0 + 4], 4, L),
                             start=False, stop=False)
                        R.op("pe", "matmul", cbf.b, ps.b, ps.ap[:, :], lhsT=ident[0:L, 0:L], rhs=negm[0:L, 0:4, :].rearrange("p h l -> p (h l)"), start=False, stop=True)
                    else:
                        for h4 in range(4):
                            o2 = pv_[:, h4, :]
                            R.op("pe", "matmul", r1.b + cbf.b, ps.b, o2, lhsT=ones[0:L, 0:L], rhs=r1.ap[0:L, 4 * hh + h4, 0:L], start=True, stop=False)
                            R.op("pe", "matmul", [dtA.b[bi]] + cbf.b, ps.b, o2, lhsT=m2[0:L, 0:L], rhs=dA[:, h0 + h4:h0 + h4 + 1].to_broadcast([L, L]),
                                 start=False, stop=False)
                            R.op("pe", "matmul", cbf.b, ps.b, o2, lhsT=ident[0:L, 0:L], rhs=negm[0:L, h4, 0:L], start=False, stop=True)
                    R.op("act", "activation", ps.b, E.b, out=E.ap[0:L, 4 * hh:4 * hh + 4, 0:L], in_=pv_, func=AF.Exp)
                    psum.rel(ps)
                R.op("pool", "tensor_tensor", [xdt.b[g]] + E.b, [xdtw.b[g]], out=v3_(xdtw.ap[0:L, gs]), in0=v3_(xdt.ap[0:L, gs]),
                     in1=E.ap[0:L, :, L - 1:L].to_broadcast([L, 8, 64]), op=ALU.mult)

            def stB(g):
                E = Eb[g % 2]
                G_ = Gm[g % 2]
                R.op("dve", "tensor_tensor", E.b + cbm.b, G_.b, out=G_.ap[0:L, :, 0:L], in0=E.ap[0:L, :, 0:L], in1=bcm(cbm.ap[0:L, g, 0:L], 8, L), op=ALU.mult)
                y_ps = psum.next()
                R.op("pe", "matmul", [xd.b[g]] + cbf.b, y_ps.b, y_ps.ap[0:L, :], lhsT=ident[0:L, 0:L], rhs=xd.ap[0:L, g * 512:(g + 1) * 512], start=True, stop=False)
                for h in range(8):
                    hh_ = 8 * g + h
                    R.op("pe", "matmul", G_.b + [xdt.b[g]], y_ps.b, y_ps.ap[0:L, h * 64:(h + 1) * 64], lhsT=G_.ap[0:L, h, 0:L], rhs=xdt.ap[0:L, hh_ * 64:(hh_ + 1) * 64],
                         start=False, stop=(h == 7))
                yi_ps = psum.next()
                R.op("pe", "matmul", CT.b + [s.hbf.b[g]], yi_ps.b, yi_ps.ap[0:L, :], lhsT=CT.ap[:, g, c0:c0 + L], rhs=s.hbf.ap[:, g * 512:(g + 1) * 512],
                     start=True, stop=True)
                a1 = t1[0]
                a2 = t2[g % 2]
                R.op("dve", "tensor_tensor", yi_ps.b + eacs.b, a1.b, out=v3_(a1.ap[0:L, :]), in0=v3_(yi_ps.ap[0:L, :]), in1=bc3(eacs.ap[0:L, 8 * g:8 * g + 8], 8, 64),
                     op=ALU.mult)
                R.op("dve", "tensor_tensor", y_ps.b + a1.b, a2.b, out=a2.ap[0:L, :], in0=y_ps.ap[0:L, :], in1=a1.ap[0:L, :], op=ALU.add)
                psum.rel(y_ps, yi_ps)
                R.op("pool", "tensor_tensor", a2.b + [zs.b[bi]], [yg.b[g]], out=yg.ap[0:L, g * 512:(g + 1) * 512], in0=a2.ap[0:L, :],
                     in1=zs.ap[0:L, bi, g * 512:(g + 1) * 512], op=ALU.mult)

            def stC(g):
                S_ps = psum.next()
                R.op("pe", "matmul", [B_tok.b[bi], xdtw.b[g]], S_ps.b, S_ps.ap[:, :], lhsT=B_tok.ap[0:L, bi, g * 128:(g + 1) * 128],
                     rhs=xdtw.ap[0:L, g * 512:(g + 1) * 512], start=True, stop=True)
                hsl = s.h.ap[:, g * 512:(g + 1) * 512]
                R.op("dve", "tensor_tensor", [s.h.b[g]] + eacs.b, [s.h.b[g]], out=v3_(hsl), in0=v3_(hsl), in1=bc3(decay[:, 8 * g:8 * g + 8], 8, 64), op=ALU.mult)
                R.op("dve", "tensor_tensor", [s.h.b[g]] + S_ps.b, [s.h.b[g]], out=hsl, in0=S_ps.ap[:, :], in1=hsl, op=ALU.add)
                psum.rel(S_ps)
                R.op("act", "activation", [s.h.b[g]], [s.hbf.b[g]], out=s.hbf.ap[:, g * 512:(g + 1) * 512], in_=hsl, func=AF.Copy)

            stA(0)
            if mid is not None:
                mid()
            for g in range(4):
                if g + 1 < 4:
                    stA(g + 1)
                stB(g)
                stC(g)

        def m3_norm(tl, k):
            L = k.n
            c0 = k.col0
            bi = k.bi
            st = scrA.new_stat()
            R.op("act", "activation", yg.b, scrA.junk.b + st.b, out=scrA.junk.ap[0:L, 0:DI], in_=yg.ap[0:L, :], func=AF.Square, accum_out=st.ap[0:L, 0:1])
            rstd_from(st.ap[0:L, 0:1], L, st, 1, DI, 1.0)
            R.op("act", "activation", yg.b + st.b, yn.b, out=yn.ap[0:L, :], in_=yg.ap[0:L, :], func=AF.Copy, scale=st.ap[0:L, 1:2])
            for i in range(2):
                ps = psum.next()
                ptv = ps.bf.rearrange("p (a b) -> p a b", b=128)
                for cc in range(8):
                    c = 8 * i + cc
                    R.op("pe", "transpose", yn.b + cbf.b, ps.b, ptv[:, cc, 0:L], yn.ap[0:L, c * 128:(c + 1) * 128], ident[0:L, 0:L])
                R.op("dve", "tensor_tensor", ps.b + par.b, [ymT.b[bi]], out=ymT.ap[:, 8 * i:8 * i + 8, c0:c0 + L], in0=ptv[:, :, 0:L],
                     in1=bc3(gmn[:, 8 * i:8 * i + 8], 8, L), op=ALU.mult)
                psum.rel(ps)

        def rope(ps, T, i, dst_ap, dst_bufs, kout_j=None):
            q_b = qb[i % 2]
            a1 = rt1[i % 2]
            a2 = rt2[i % 2]
            R.op("act", "activation", ps.b, q_b.b, out=q_b.ap[:, 0:T], in_=ps.ap[:, 0:T], func=AF.Copy)
            sw = psum.next()
            R.op("pe", "matmul", q_b.b + cbf.b, sw.b, sw.ap[:, 0:T], lhsT=rm, rhs=q_b.ap[:, 0:T], start=True, stop=True)
            R.op("dve", "tensor_tensor", ps.b + cs.b, a1.b, out=a1.ap[:, 0:T], in0=ps.ap[:, 0:T], in1=cs.ap[:, 0, 0:T], op=ALU.mult)
            R.op("dve", "tensor_tensor", sw.b + cs.b, a2.b, out=a2.ap[:, 0:T], in0=sw.ap[:, 0:T], in1=cs.ap[:, 1, 0:T], op=ALU.mult)
            psum.rel(sw)
            return a1, a2

        def m4_qkv(tl):
            T = tl.T
            n = tl.n
            R.op("sp", "dma_start", [], cs.b, out=cs.ap[:, :, 0:T], in_=cs_d[:, :, tl.cscol:tl.cscol + T], key=cs_key)
            i = 0
            for u in range(2):
                slot = wsA.acquire("q%d" % u)
                wv = slot.ap.rearrange("p (k cc c) -> p k cc c", k=KC, cc=4)
                for cc in range(4):
                    c = 4 * u + cc
                    ps = psum.next()
                    for kc in range(KC):
                        R.op("pe", "matmul", slot.b + uT.b[0:tl.nb], ps.b, ps.ap[:, 0:T], lhsT=wv[:, kc, cc, :], rhs=uT.ap[:, kc, 0:T],
                             start=(kc == 0), stop=(kc == KC - 1))
                    a1, a2 = rope(ps, T, i, None, None)
                    psum.rel(ps)
                    R.op("pool", "tensor_tensor", a1.b + a2.b, [qT.b[c]], out=qT.ap[:, c, 0:T], in0=a1.ap[:, 0:T], in1=a2.ap[:, 0:T], op=ALU.add)
                    i += 1
            slot = wsA.acquire("kdup")
            wv = slot.ap.rearrange("p (k j c) -> p k j c", k=KC, j=4)
            for j in range(4):
                ps = psum.next()
                for kc in range(KC):
                    R.op("pe", "matmul", slot.b + uT.b[0:tl.nb], ps.b, ps.ap[:, 0:T], lhsT=wv[:, kc, j, :], rhs=uT.ap[:, kc, 0:T],
                         start=(kc == 0), stop=(kc == KC - 1))
                a1, a2 = rope(ps, T, i, None, None)
                psum.rel(ps)
                kview = kT_all.ap[:, j, 128:128 + tl.nb * 128].rearrange("p (b w) -> p b w", w=128)[:, :, 0:n]
                R.op("pool", "tensor_tensor", a1.b + a2.b, kT_all.b[1:1 + tl.nb], out=kview, in0=a1.ap[:, 0:T].rearrange("p (b w) -> p b w", w=n),
                     in1=a2.ap[:, 0:T].rearrange("p (b w) -> p b w", w=n), op=ALU.add)
                if tl.blocks[-1].last:
                    if tl.is_samp:
                        lo, wd = 0, T
                    else:
                        lo, wd = T - 128, 128
                    R.op("dve", "tensor_tensor", a1.b + a2.b, krot_f.b, out=krot_f.ap[:, j, 0:wd], in0=a1.ap[:, lo:lo + wd], in1=a2.ap[:, lo:lo + wd], op=ALU.add)
                i += 1
            if tl.blocks[-1].last:
                wd = T if tl.is_samp else 128
                ps = psum.next()
                pv4 = ps.ap[:, 0:256].rearrange("p (j c) -> p j c", c=64)
                for j in range(4):
                    R.op("pe", "transpose", krot_f.b + par.b, ps.b, pv4[0:wd, j, :], krot_f.ap[0:64, j, 0:wd], identf[0:64, 0:64])
                ko = kvout1
                R.op("act", "activation", ps.b, ko.b, out=ko.ap[0:wd, :], in_=ps.ap[0:wd, 0:256], func=AF.Copy)
                psum.rel(ps)
                misc_out("k", ko.b, out=(k_s if tl.is_samp else k_p), in_=ko.ap[0:wd, :])

        def m4_attn(tl, k):
            n = k.n
            c0 = k.col0
            bi = k.bi
            keysets = []
            if k.is_samp:
                keysets.append((cacheT[k.sidx].ap, cacheT[k.sidx].b, cachev[k.sidx].ap, cachev[k.sidx].b, 128, None, None))
                ownb = (None, None)
            else:
                if not k.first:
                    keysets.append((kT_all.ap[:, :, bi * 128:(bi + 1) * 128], [kT_all.b[bi]], v_all.ap[:, bi, :, :], [v_all.b[bi]], 128, None, 0))
                ownb = (1, None)
            keysets.append((kT_all.ap[:, :, (bi + 1) * 128:(bi + 2) * 128], [kT_all.b[bi + 1]], v_all.ap[:, bi + 1, :, :], [v_all.b[bi + 1]], n, ownb[0], ownb[1]))
            nks = len(keysets)
            at = attn_tok[bi % 2]

            def scores(j):
                sbk = [psum.next(), psum.next()]
                svs = [b_.ap.rearrange("p (s q) -> p s q", q=128) for b_ in sbk]
                ptb = [PT[2 * (j % 2)], PT[2 * (j % 2) + 1]]
                for ks, ke in enumerate(keysets):
                    kap, kbufs, nk = ke[0], ke[1], ke[4]
                    for hl in range(4):
                        hf = hl % 2
                        hp = hl // 2
                        ch = (4 * j + hl) // 2
                        R.op("pe", "matmul", kbufs + [qT.b[ch]], sbk[hf].b, svs[hf][0:nk, ks * 2 + hp, 0:n], lhsT=kap[hf * 64:(hf + 1) * 64, j, 0:nk],
                             rhs=qT.ap[hf * 64:(hf + 1) * 64, ch, c0:c0 + n], start=True, stop=True)
                for hf in range(2):
                    pt = ptb[hf]
                    for ks, ke in enumerate(keysets):
                        nk, b0, b1 = ke[4], ke[5], ke[6]
                        sl = slice(2 * ks, 2 * ks + 2)
                        if b0 is None and b1 is None:
                            R.op("act", "activation", sbk[hf].b, pt.b, out=pt.ap[0:nk, sl, 0:n], in_=svs[hf][0:nk, sl, 0:n], func=AF.Exp, scale=0.125)
                        else:
                            for q0, bcol in ((0, b0), (64, b1)):
                                kw = {} if bcol is None else {"bias": negcol[0:nk, bcol:bcol + 1]}
                                R.op("act", "activation", sbk[hf].b + par.b, pt.b, out=pt.ap[0:nk, sl, q0:q0 + 64], in_=svs[hf][0:nk, sl, q0:q0 + 64],
                                     func=AF.Exp, scale=0.125, **kw)
                psum.rel(*sbk)
                return ptb

            def pv_out(j, ptb):
                o_ps = psum.next()
                ov = o_ps.ap[:, 0:260].rearrange("p (h c) -> p h c", c=65)
                for hl in range(4):
                    hf = hl % 2
                    hp = hl // 2
                    for ks, ke in enumerate(keysets):
                        vap, vbufs, nk = ke[2], ke[3], ke[4]
                        R.op("pe", "matmul", ptb[hf].b + vbufs, o_ps.b, ov[0:n, hl, :], lhsT=ptb[hf].ap[0:nk, ks * 2 + hp, 0:n], rhs=vap[0:nk, j, 0:65],
                             start=(ks == 0), stop=(ks == nks - 1))
                dn = den[j % 2]
                R.op("dve", "tensor_tensor", o_ps.b + dconst.b, dn.b, out=dn.ap[0:n, 0:4], in0=ov[0:n, :, 64], in1=esink[0:n, 4 * j:4 * j + 4], op=ALU.add)
                R.op("dve", "reciprocal", dn.b, dn.b, out=dn.ap[0:n, 4:8], in_=dn.ap[0:n, 0:4])
                R.op("dve", "tensor_tensor", o_ps.b + dn.b, at.b, out=at.ap[0:n, j * 256:(j + 1) * 256].rearrange("p (h c) -> p h c", c=64),
                     in0=ov[0:n, :, 0:64], in1=bc3(dn.ap[0:n, 4:8], 4, 64), op=ALU.mult)
                psum.rel(o_ps)

            pts = scores(0)
            for j in range(4):
                nxt = scores(j + 1) if j + 1 < 4 else None
                pv_out(j, pts)
                pts = nxt
            ps = psum.next()
            ptv = ps.bf.rearrange("p (a b) -> p a b", b=128)
            for c in range(8):
                R.op("pe", "transpose", at.b + cbf.b, ps.b, ptv[:, c, 0:n], at.ap[0:n, c * 128:(c + 1) * 128], ident[0:n, 0:n])
            R.op("act", "activation", ps.b, [attnT.b[bi]], out=attnT.ap[:, :, c0:c0 + n], in_=ptv[:, :, 0:n], func=AF.Copy)
            psum.rel(ps)

        def m4_save_prev(tl):
            if tl.is_samp:
                return
            R.op("act", "activation", [kT_all.b[NB]], [kT_all.b[0]], out=kT_all.ap[:, :, 0:128], in_=kT_all.ap[:, :, NB * 128:(NB + 1) * 128], func=AF.Copy)
            R.op("pool", "tensor_copy", [v_all.b[NB]], [v_all.b[0]], out=v_all.ap[:, 0, :, :], in_=v_all.ap[:, NB, :, :])

        def m5_branches(tl, xb):
            T = tl.T
            for u in range(4):
                nm = ("gm%d" if u % 2 == 0 else "ga%d") % (u // 2)
                slot = wsA.acquire(nm)
                wv = slot.ap.rearrange("p (k cc c) -> p k cc c", k=KC, cc=4)
                for cc in range(4):
                    fo = 4 * (u // 2) + cc
                    ps = psum.next()
                    for kc in range(KC):
                        R.op("pe", "matmul", slot.b + uT.b[0:tl.nb], ps.b, ps.ap[:, 0:T], lhsT=wv[:, kc, cc, :], rhs=uT.ap[:, kc, 0:T],
                             start=(kc == 0), stop=(kc == KC - 1))
                    gsb = gates[(u % 2)]
                    R.op("act", "activation", ps.b, [gsb.b[fo]], out=gsb.ap[:, fo, 0:T], in_=ps.ap[:, 0:T], func=AF.Sigmoid)
                    psum.rel(ps)
            for u in range(4):
                slot = wsA.acquire("brm%d" % u)
                wv = slot.ap.rearrange("p (k cc c) -> p k cc c", k=16, cc=2)
                for cc in range(2):
                    fo = 2 * u + cc
                    ps = psum.next()
                    for kc in range(16):
                        R.op("pe", "matmul", slot.b + ymT.b[0:tl.nb], ps.b, ps.ap[:, 0:T], lhsT=wv[:, kc, cc, :], rhs=ymT.ap[:, kc, 0:T],
                             start=(kc == 0), stop=(kc == 15))
                    R.op("dve", "tensor_tensor", ps.b + [gates[0].b[fo]], [mixf.b[fo]], out=mixf.ap[:, fo, 0:T], in0=ps.ap[:, 0:T], in1=gates[0].ap[:, fo, 0:T],
                         op=ALU.mult)
                    psum.rel(ps)
            for u in range(2):
                slot = wsA.acquire("bra%d" % u)
                wv = slot.ap.rearrange("p (k cc c) -> p k cc c", k=KC, cc=4)
                for cc in range(4):
                    fo = 4 * u + cc
                    ps = psum.next()
                    for kc in range(KC):
                        R.op("pe", "matmul", slot.b + attnT.b[0:tl.nb], ps.b, ps.ap[:, 0:T], lhsT=wv[:, kc, cc, :], rhs=attnT.ap[:, kc, 0:T],
                             start=(kc == 0), stop=(kc == KC - 1))
                    b_ = bt[fo % 2]
                    R.op("dve", "tensor_tensor", ps.b + [gates[1].b[fo]], b_.b, out=b_.ap[:, 0:T], in0=ps.ap[:, 0:T], in1=gates[1].ap[:, fo, 0:T], op=ALU.mult)
                    psum.rel(ps)
                    R.op("pool", "tensor_tensor", b_.b + [mixf.b[fo]], [mixedT.b[fo]], out=mixedT.ap[:, fo, 0:T], in0=b_.ap[:, 0:T], in1=mixf.ap[:, fo, 0:T],
                         op=ALU.add)
            accs = [[psum.next() for _ in tl.blocks] for _ in range(2)]
            for half in range(2):
                slot = wsA.acquire("wo%d" % half)
                wv = slot.ap.rearrange("p (k n) -> p k n", k=KC)
                for kc in range(KC):
                    for k in tl.blocks:
                        R.op("pe", "matmul", slot.b + [mixedT.b[kc]], accs[half][k.bi].b, accs[half][k.bi].ap[0:k.n, :],
                             lhsT=mixedT.ap[:, kc, k.col0:k.col0 + k.n], rhs=wv[:, kc, :], start=(kc == 0), stop=(kc == KC - 1))
            for k in tl.blocks:
                a2 = [accs[0][k.bi], accs[1][k.bi]]
                post_norm_add(xb, k, a2, gpost["m"], 1.0, scrA)
                psum.rel(*a2)


        def mixer(tl, xb):
            switch(M456, M123)
            norm_T(tl, xb, gpre["m"], uT, scrA)
            for k in tl.blocks:
                if k.is_samp and "st" in DBG:
                    load_halo(k)
            if "m1" in DBG:
                m1_xbc(tl)
            if "m2" in DBG:
                m2_dtv_z(tl)
            prevk = [None]
            for k in tl.blocks:
                if k.is_samp and "st" in DBG:
                    load_seq_state_ssm(k)

                def mid(pk=prevk[0]):
                    if pk is not None:
                        m3_norm(tl, pk)
                if k.is_samp or k.last:
                    mid()
                    m3_ssd(tl, k)
                    m3_norm(tl, k)
                    prevk[0] = None
                else:
                    m3_ssd(tl, k, mid)
                    prevk[0] = k
                if k.last and "st" in DBG:
                    store_seq_state(k)
            if prevk[0] is not None:
                m3_norm(tl, prevk[0])
            switch(M123, M456)
            for k in tl.blocks:
                if k.is_samp and "st" in DBG:
                    load_seq_state_kv(k)
            if "m4" in DBG:
                m4_qkv(tl)
            if "m4a" in DBG:
                for k in tl.blocks:
                    m4_attn(tl, k)
                m4_save_prev(tl)
            if "m5" in DBG:
                m5_branches(tl, xb)

    NT = len(tiles)
    for t in range(NT):
        wsA.seq += mix_unit_seq() if do_mix else []
    wsB.seq += ffn_unit_seq(0)
    for t in range(NT):
        if t > 0:
            wsB.seq += ffn_unit_seq(1)
        if t + 1 < NT:
            wsB.seq += ffn_unit_seq(0)
    wsB.seq += ffn_unit_seq(1)
    sched = Sched(KA, KB)
    load_x(0)
    ffn(tiles[0], x_bufs[0], 0)
    if NT > 1:
        load_x(1)
    for t in range(NT):
        tl = tiles[t]

        def fa(t=t, tl=tl):
            if do_mix:
                mixer(tl, x_bufs[t % 2])

        def fb(t=t):
            if t > 0:
                ffn(tiles[t - 1], x_bufs[(t - 1) % 2], 1)
                store_x(t - 1)
                if t + 1 < NT:
                    load_x(t + 1)
            if t + 1 < NT:
                ffn(tiles[t + 1], x_bufs[(t + 1) % 2], 0)

        if INTERLEAVE:
            R.sched = sched
            sched.run(fa, fb)
            R.sched = None
        else:
            fa()
            fb()
    ffn(tiles[NT - 1], x_bufs[(NT - 1) % 2], 1)
    store_x(NT - 1)
    R.emit()
    return nc


MX_BYTES = {2: 68352, 4: 150 * 1024}

def _layout(entries):
    d = {}
    o = 0
    for name, n in entries:
        d[name] = (o, n)
        o += n
    d["_n"] = o
    return d


PAR = _layout([("gpre1", 8), ("gpre2", 8), ("gmix", 8), ("gmn", 16), ("convw", 96), ("convb", 24),
               ("gpost1", 1024), ("gpost2", 1024), ("gpostm", 1024), ("dtb", 32), ("alog", 32), ("dskip", 32),
               ("sink", 16), ("identf", 128), ("negcol", 2)])
CB = _layout([("ident", 128), ("tri", 128), ("m2", 128), ("ones", 128), ("rm", 128), ("negm", 1024), ("maskA", 512), ("maskB", 512)])


def pack_params(inp):
    a = np.zeros((P, PAR["_n"]), np.float32)

    def put(name, v):
        o, n = PAR[name]
        a[:, o:o + n] = v

    def chunked(v):
        return np.asarray(v, np.float32).reshape(-1, P).T

    def bc(v):
        return np.broadcast_to(np.asarray(v, np.float32).reshape(1, -1), (P, np.asarray(v).size))

    put("gpre1", chunked(inp["ffn1_pre_g"][0]))
    put("gpre2", chunked(inp["ffn2_pre_g"][0]))
    put("gmix", chunked(inp["mix_pre_g"][0]))
    put("gmn", chunked(inp["m_norm_g"][0]))
    cw = np.asarray(inp["conv_w"][0], np.float32)
    put("convw", cw.reshape(4, 24, P).transpose(2, 1, 0).reshape(P, 96))
    put("convb", chunked(inp["conv_b"][0]))
    put("gpost1", bc(inp["ffn1_post_g"][0]))
    put("gpost2", bc(inp["ffn2_post_g"][0]))
    put("gpostm", bc(inp["mix_post_g"][0]))
    put("dtb", bc(inp["dt_bias"][0]))
    put("alog", bc(inp["a_log"][0]))
    put("dskip", bc(inp["d_skip"][0]))
    put("sink", bc(inp["attn_sink"][0]))
    put("identf", np.eye(P, dtype=np.float32))
    ncol = np.zeros((P, 2), np.float32)
    ncol[:64, 0] = -30000.0
    ncol[64:, 1] = -30000.0
    put("negcol", ncol)
    return a


def pack_consts():
    a = np.zeros((P, CB["_n"]), np.float32)

    def put(name, v):
        o, n = CB[name]
        a[:, o:o + n] = v.reshape(P, n)

    k = np.arange(P)[:, None]
    l = np.arange(P)[None, :]
    put("ident", np.eye(P))
    put("tri", (k <= l).astype(np.float32))
    put("m2", -(k <= l).astype(np.float32))
    put("ones", np.ones((P, P)))
    rmm = np.zeros((P, P), np.float32)
    for hb in (0, 64):
        for d2 in range(8):
            rmm[hb + d2 + 8, hb + d2] = 1.0
            rmm[hb + d2, hb + d2 + 8] = 1.0
    put("rm", rmm)
    ng = np.where(l < k, -30000.0, 0.0).astype(np.float32)
    put("negm", np.broadcast_to(ng[:, None, :], (P, 8, P)))
    mA = np.ones((P, P), np.float32)
    mA[:64, 64:] = 0.0
    mB = np.ones((P, P), np.float32)
    mB[64:, :64] = 0.0
    put("maskA", np.broadcast_to(mA[:, None, :], (P, 4, P)))
    put("maskB", np.broadcast_to(mB[:, None, :], (P, 4, P)))
    return a.astype(ml_dtypes.bfloat16)


def rope_tables(n_prompt, samp_len):
    pos = np.concatenate([np.arange(n_prompt), PAST_LEN + np.arange(samp_len)]).astype(np.float32)
    inv = (np.float32(THETA) ** (-np.arange(0, ROT, 2, dtype=np.float32) / np.float32(ROT))).astype(np.float32)
    ang = (pos[:, None] * inv[None, :]).astype(np.float32)
    c = np.cos(ang).astype(np.float32).T
    s = np.sin(ang).astype(np.float32).T
    cos_t = np.ones((P, pos.size), np.float32)
    sin_t = np.zeros((P, pos.size), np.float32)
    for hb in (0, 64):
        cos_t[hb:hb + 8] = c
        cos_t[hb + 8:hb + 16] = c
        sin_t[hb:hb + 8] = -s
        sin_t[hb + 8:hb + 16] = s
    return cos_t, sin_t


N_CORES = 8
NB_DEFAULT = 2
_cache = {}


def make_in_maps(inp, n_prompt, n_cores):
    f = lambda a: np.ascontiguousarray(np.asarray(a, dtype=np.float32))
    params = pack_params(inp)
    cbf = pack_consts()
    cos_t, sin_t = rope_tables(n_prompt, 32)
    cs = np.stack([np.concatenate([cos_t, cos_t[:, n_prompt:]], axis=1), np.concatenate([sin_t, sin_t[:, n_prompt:]], axis=1)], axis=1)
    cs = np.ascontiguousarray(cs, dtype=np.float32)
    shared = {
        "w_gu1": f(inp["ffn1_w_gu"][0]), "w_gu2": f(inp["ffn2_w_gu"][0]),
        "w_dn1": f(inp["ffn1_w_down"][0]), "w_dn2": f(inp["ffn2_w_down"][0]),
        "w_in": f(inp["w_in"][0]), "w_brm": f(inp["w_br_m"][0]), "w_bra": f(inp["w_br_a"][0]), "w_o": f(inp["w_o"][0]),
        "params": params, "cbf": cbf, "cs_t": cs,
    }
    maps = []
    for c in range(n_cores):
        m = dict(shared)
        m["xp"] = f(inp["x_prompt"][c])
        sl = slice(2 * c, 2 * c + 2)
        m["xs"] = f(inp["x_sample"][sl]).reshape(64, D)
        m["st_conv"] = f(inp["state_conv"][0][sl])
        m["st_ssm"] = f(inp["state_ssm"][0][sl]).reshape(2, DI, NS)
        m["ck"] = f(inp["cache_k"][0][sl]).reshape(2, 128, 256)
        m["cv"] = f(inp["cache_v"][0][sl]).reshape(2, 128, 256)
        maps.append(m)
    return maps


def gather(results, n_prompt, n_cores):
    g = lambda k: [np.asarray(r[k], dtype=np.float32) for r in results]
    rows = min(128, n_prompt)
    yp = np.stack(g("yp")).reshape(n_cores, n_prompt, D)
    ys = np.concatenate([a.reshape(2, 32, D) for a in g("ys")], 0)
    conv_p = np.stack(g("conv_p")).reshape(1, n_cores, 3, CONV)
    ssm_p = np.stack(g("ssm_p")).reshape(1, n_cores, NH, HP, NS)
    k_p = np.stack(g("k_p")).reshape(1, n_cores, rows, 4, 64)
    v_p = np.stack(g("v_p")).reshape(1, n_cores, rows, 4, 64)
    conv_s = np.concatenate(g("conv_s"), 0).reshape(1, 2 * n_cores, 3, CONV)
    ssm_s = np.concatenate(g("ssm_s"), 0).reshape(1, 2 * n_cores, NH, HP, NS)
    k_s = np.concatenate([a.reshape(2, 32, 4, 64) for a in g("k_s")], 0).reshape(1, 2 * n_cores, 32, 4, 64)
    v_s = np.concatenate([a.reshape(2, 32, 4, 64) for a in g("v_s")], 0).reshape(1, 2 * n_cores, 32, 4, 64)
    return (yp, ys, conv_p, ssm_p, k_p, v_p, conv_s, ssm_s, k_s, v_s)


def kernel(**inputs):
    n_prompt = int(np.asarray(inputs["x_prompt"]).shape[1])
    n_cores = int(np.asarray(inputs["x_prompt"]).shape[0])
    nc = build(n_prompt, NB_DEFAULT)
    maps = make_in_maps(inputs, n_prompt, n_cores)
    res = run_bass_kernel_spmd(nc, maps, core_ids=list(range(n_cores)))
    return gather(res.results, n_prompt, n_cores)
```

```python
import math
import threading
import numpy as np
import ml_dtypes
import concourse.bass as bass
import concourse.mybir as mybir
from concourse.bass_utils import run_bass_kernel_spmd

F32 = mybir.dt.float32
BF16 = mybir.dt.bfloat16
AF = mybir.ActivationFunctionType
ALU = mybir.AluOpType
AX = mybir.AxisListType
P = 128

D = 1024
DFF = 2816
DI = 2048
CONV = 3072
NH = 32
HP = 64
NS = 128
NG = 4
KC = 8
FC = 22
DINP = 8736
EPS = 1e-6
PAST_LEN = 1024
ROT = 16
THETA = 500000.0
OFF_Z, OFF_XBC, OFF_DT, OFF_Q, OFF_K, OFF_V, OFF_GM, OFF_GA = 0, 2048, 5120, 5152, 6176, 6432, 6688, 7712

ENGS = ("pe", "act", "dve", "pool", "sp")


class Buf:
    __slots__ = ("name", "last_w", "readers", "excl")

    def __init__(self, name="", pending=None):
        self.name = name
        self.excl = False
        self.last_w = None
        self.readers = list(pending) if pending else []


class Op:
    __slots__ = ("eng", "fn", "deps", "sig", "count", "key", "ord", "is_dma", "idx")


class DmaKey:
    def __init__(self, sem, grouped=False):
        self.sem = sem
        self.n = 0
        self.grouped = grouped


class Rec:
    def __init__(self, nc):
        self.nc = nc
        self.streams = {e: [] for e in ENGS}
        self.sems = {e: nc.alloc_semaphore("sem_" + e) for e in ENGS}
        self.final_keys = []
        self.sched = None

    def key(self, name, grouped=False):
        self.nk = getattr(self, "nk", 0) + 1
        return DmaKey(self.nc.alloc_semaphore("dk%d_%s" % (self.nk, name)), grouped)

    def op(self, eng, method, reads, writes, *args, key=None, **kwargs):
        o = Op()
        o.eng = eng
        o.fn = (method, args, kwargs)
        o.key = key
        o.is_dma = key is not None
        o.sig = False
        o.count = None
        o.idx = len(self.streams[eng])
        if key is not None:
            key.n += 1
            o.ord = key.n
        best = {}
        dmas = []
        ex = [b for b in reads if b.excl]
        if ex:
            writes = list(writes) + [b for b in ex if b not in writes]

        dk = {}

        def add(d, kind):
            if d is o:
                return
            if d.is_dma:
                c = dk.get(id(d.key))
                if c is None or d.ord > c.ord:
                    dk[id(d.key)] = d
                return
            if d.eng == eng and not o.is_dma:
                if eng == "pe" or kind != "raw":
                    return
            c = best.get(d.eng)
            if c is None or d.idx > c.idx:
                best[d.eng] = d

        for b in reads:
            if b.last_w is not None:
                add(b.last_w, "raw")
        for b in writes:
            if b.last_w is not None:
                add(b.last_w, "waw")
            for r in b.readers:
                add(r, "war")
        o.deps = list(best.values()) + list(dk.values())
        for d in best.values():
            d.sig = True
        for b in reads:
            b.readers.append(o)
        for b in writes:
            b.last_w = o
            b.readers = []
        self.streams[eng].append(o)
        if self.sched is not None:
            self.sched.tick(o)
        return o

    @staticmethod
    def pending(bufs):
        best = {}
        dk = {}
        for b in bufs:
            for d in ([b.last_w] if b.last_w is not None else []) + b.readers:
                if d.is_dma:
                    c = dk.get(id(d.key))
                    if c is None or d.ord > c.ord:
                        dk[id(d.key)] = d
                else:
                    c = best.get(d.eng)
                    if c is None or d.idx > c.idx:
                        best[d.eng] = d
        return list(dk.values()) + list(best.values())

    def emit(self):
        nc = self.nc
        for e in ENGS:
            c = 0
            for o in self.streams[e]:
                if o.sig and not o.is_dma:
                    c += 1
                    o.count = c
        sems = self.sems

        def runner(en):
            def f(e):
                waited = {}
                for o in self.streams[en]:
                    need = {}
                    for d in o.deps:
                        if d.is_dma:
                            sem = d.key.sem
                            val = 16 * (d.key.n if d.key.grouped else d.ord)
                        else:
                            sem = sems[d.eng]
                            val = d.count
                        k = id(sem)
                        if k not in need or need[k][1] < val:
                            need[k] = (sem, val)
                    items = [(s, v) for k, (s, v) in need.items() if waited.get(k, 0) < v]
                    for k, (s, v) in need.items():
                        if waited.get(k, 0) < v:
                            waited[k] = v
                    attach = None
                    if items and not o.is_dma:
                        attach = items.pop()
                    for s, v in items:
                        e.wait_ge(s, v)
                    m, a, kw = o.fn
                    ins = getattr(e, m)(*a, **kw)
                    if attach is not None:
                        ins._wait_ge(attach[0], attach[1])
                    if o.is_dma:
                        ins.then_inc(o.key.sem, 16)
                    elif o.sig:
                        ins.then_inc(sems[en], 1)
                if en == "sp":
                    for k in self.final_keys:
                        if k.n > 0:
                            e.wait_ge(k.sem, 16 * k.n)
            return f

        with nc.Block() as blk:
            blk.tensor(runner("pe"))
            blk.scalar(runner("act"))
            blk.vector(runner("dve"))
            blk.gpsimd(runner("pool"))
            blk.sync(runner("sp"))


class Sched:
    def __init__(self, ka, kb):
        self.cv = threading.Condition()
        self.turn = None
        self.alive = {"A": False, "B": False}
        self.cnt = {"A": 0, "B": 0}
        self.k = {"A": ka, "B": kb}
        self.local = threading.local()
        self.exc = None

    def me(self):
        return getattr(self.local, "name", None)

    def run(self, fa, fb):
        self.alive = {"A": True, "B": True}
        self.turn = "A"
        self.cnt = {"A": 0, "B": 0}

        def wrap(name, f):
            other = "B" if name == "A" else "A"

            def g():
                self.local.name = name
                with self.cv:
                    while self.turn != name:
                        self.cv.wait()
                try:
                    f()
                except BaseException as e:
                    self.exc = e
                finally:
                    with self.cv:
                        self.alive[name] = False
                        self.turn = other
                        self.cv.notify_all()
            return g

        ta = threading.Thread(target=wrap("A", fa))
        tb = threading.Thread(target=wrap("B", fb))
        ta.start()
        tb.start()
        ta.join()
        tb.join()
        self.turn = None
        if self.exc is not None:
            e = self.exc
            self.exc = None
            raise e

    def yield_(self):
        name = self.me()
        if name is None:
            return False
        other = "B" if name == "A" else "A"
        if not self.alive[other]:
            return False
        with self.cv:
            self.turn = other
            self.cv.notify_all()
            while self.turn != name:
                self.cv.wait()
        return True

    def tick(self, o):
        name = self.me()
        if name is None or o.is_dma or o.eng == "sp":
            return
        if name == "A":
            if o.eng != "pe":
                self.cnt["A"] += 1
        elif o.eng == "pe":
            self.cnt["B"] += 1
        if self.cnt[name] >= self.k[name]:
            self.cnt[name] = 0
            self.yield_()


class TT:
    def __init__(self, ap, nb=1, name=""):
        self.ap = ap
        self.b = [Buf(name) for _ in range(nb)]

    def renew(self, pending):
        self.b = [Buf("", pending) for _ in self.b]


def prod(xs):
    r = 1
    for x in xs:
        r *= x
    return r


class Blk:
    pass


class Tile:
    pass


class Seq:
    pass


DN_GROUPS = ((0, 8), (8, 8), (16, 6))
INTERLEAVE = True
KA = 6
KB = 8
PAUSE1 = 4
PAUSE2 = 10
VW = 72
DBG = set("m1,m2,m3,st,m4,m4a,m5".split(","))


def build(n_prompt, NB, n_samp=2, samp_len=32, stages="all"):
    nc = bass.Bass("TRN2", target_bir_lowering=False)
    R = Rec(nc)
    TMAX = 128 * NB
    nt = n_prompt // TMAX
    assert nt * TMAX == n_prompt
    NPOSX = n_prompt + n_samp * samp_len

    def din(name, shape, dt=F32):
        return nc.dram_tensor(name, list(shape), dt, kind="ExternalInput").ap()

    def dout(name, shape, dt=F32):
        return nc.dram_tensor(name, list(shape), dt, kind="ExternalOutput").ap()

    xp = din("xp", [n_prompt, D])
    xs = din("xs", [n_samp * samp_len, D])
    st_conv = din("st_conv", [n_samp, 3, CONV])
    st_ssm = din("st_ssm", [n_samp, DI, NS])
    ck = din("ck", [n_samp, 128, 256])
    cv = din("cv", [n_samp, 128, 256])
    w_gu = [din("w_gu1", [D, 2 * DFF]), din("w_gu2", [D, 2 * DFF])]
    w_dn = [din("w_dn1", [DFF, D]), din("w_dn2", [DFF, D])]
    w_in = din("w_in", [D, DINP])
    w_brm = din("w_brm", [DI, D])
    w_bra = din("w_bra", [D, D])
    w_o = din("w_o", [D, D])
    NPAR = PAR["_n"]
    params_d = din("params", [P, NPAR])
    NCB = CB["_n"]
    cbf_d = din("cbf", [P, NCB], BF16)
    cs_d = din("cs_t", [P, 2, NPOSX])

    yp = dout("yp", [n_prompt, D])
    ys = dout("ys", [n_samp * samp_len, D])
    conv_p = dout("conv_p", [3, CONV])
    ssm_p = dout("ssm_p", [DI, NS])
    rows_p = min(128, n_prompt)
    k_p = dout("k_p", [rows_p, 256])
    v_p = dout("v_p", [rows_p, 256])
    conv_s = dout("conv_s", [n_samp, 3, CONV])
    ssm_s = dout("ssm_s", [n_samp, DI, NS])
    k_s = dout("k_s", [n_samp * samp_len, 256])
    v_s = dout("v_s", [n_samp * samp_len, 256])

    def out_key(name):
        k = R.key(name)
        R.final_keys.append(k)
        return k

    def sb(name, shape, dt=F32):
        return nc.alloc_sbuf_tensor(name, list(shape), dt)

    def tt(name, shape, dt=F32, nb=1):
        t = sb(name, shape, dt)
        return TT(t[:], nb, name)

    class Arena:
        def __init__(self, name, nbytes):
            self.t = sb(name, [P, nbytes // 2], BF16)
            self.nbytes = nbytes
            self.off = 0
            self.hi = 0

        def carve(self, shape, dt, nb=1, name=""):
            nel = prod(shape[1:])
            esz = 2 if dt == BF16 else 4
            nbytes = (nel * esz + 31) // 32 * 32
            assert self.off + nbytes <= self.nbytes, (name, self.off, nbytes, self.nbytes)
            a = self.t[:, self.off // 2:(self.off + nel * esz) // 2]
            self.off += nbytes
            self.hi = max(self.hi, self.off)
            if dt == F32:
                a = a.bitcast(F32)
            if len(shape) == 3:
                a = a.rearrange("p (a b) -> p a b", b=shape[2])
            elif len(shape) == 4:
                a = a.rearrange("p (a b c) -> p a b c", b=shape[2], c=shape[3])
            return TT(a, nb, name)

    par = tt("par_sb", [P, NPAR])
    cbf = tt("cbf_sb", [P, NCB], BF16)
    ld_key = R.key("ld", grouped=True)
    R.op("sp", "dma_start", [], par.b, out=par.ap, in_=params_d, key=ld_key)
    R.op("sp", "dma_start", [], cbf.b, out=cbf.ap, in_=cbf_d, key=ld_key)

    def pv(name):
        o, n = PAR[name]
        return par.ap[:, o:o + n]

    def cv_(name):
        o, n = CB[name]
        return cbf.ap[:, o:o + n]

    ident = cv_("ident")
    tri = cv_("tri")
    m2 = cv_("m2")
    ones = cv_("ones")
    rm = cv_("rm")
    negm = cv_("negm").rearrange("p (h l) -> p h l", l=128)
    maskA = cv_("maskA").rearrange("p (h l) -> p h l", l=128)
    maskB = cv_("maskB").rearrange("p (h l) -> p h l", l=128)
    _oA = CB["maskA"][0]
    assert CB["maskB"][0] == _oA + 512
    maskAB = cbf.ap[:, _oA + 256:_oA + 768].rearrange("p (h l) -> p h l", l=128)
    identf = pv("identf")
    gpre = {0: pv("gpre1"), 1: pv("gpre2"), "m": pv("gmix")}
    gpost = {0: pv("gpost1"), 1: pv("gpost2"), "m": pv("gpostm")}
    gmn = pv("gmn")
    convw = pv("convw")
    convb = pv("convb")
    dtb = pv("dtb")
    dskip = pv("dskip")
    negcol = pv("negcol")

    dconst = tt("dconst", [P, 48])
    Aneg = dconst.ap[:, 0:32]
    esink = dconst.ap[:, 32:48]
    R.op("act", "activation", par.b, dconst.b, out=dconst.ap[:, 0:32], in_=pv("alog"), func=AF.Exp)
    R.op("act", "activation", par.b, dconst.b, out=dconst.ap[:, 32:48], in_=pv("sink"), func=AF.Exp)
    R.op("dve", "tensor_scalar", dconst.b, dconst.b, out=dconst.ap[:, 0:32], in0=dconst.ap[:, 0:32], scalar1=-1.0, scalar2=None, op0=ALU.mult)

    class PsPool:
        def __init__(self):
            self.banks = []
            for i in range(8):
                t = nc.alloc_psum_tensor("ps%d" % i, [P, 512], F32)
                tb = TT(t[:], 1, "ps%d" % i)
                tb.bf = t[:].bitcast(BF16)
                tb.b[0].excl = True
                tb.live = False
                self.banks.append(tb)
            self.i = 0

        def next(self):
            for attempt in range(10000):
                for _ in range(8):
                    b = self.banks[self.i]
                    self.i = (self.i + 1) % 8
                    if not b.live:
                        b.live = True
                        return b
                if R.sched is None or not R.sched.yield_():
                    break
            raise RuntimeError("psum exhausted")

        def rel(self, *bs):
            for b in bs:
                b.live = False

    psum = PsPool()

    SLOT = 4096
    units = {}
    unit_defs = {}

    NCK = 12
    cast_keys = [R.key("cast%d" % i) for i in range(NCK)]
    cast_dummy = [Buf("castd%d" % i) for i in range(NCK)]
    cast_i = [0]
    unit_order = []

    def def_unit(name, n, srcs, grp):
        unit_defs[name] = (n, srcs, grp)
        unit_order.append(name)

    def make_unit(name):
        n, srcs, grp = unit_defs[name]
        d = nc.dram_tensor("wb_" + name, [P, n], BF16, kind="Internal").ap()
        bs = []
        for dv, src in srcs:
            b = Buf(name)
            ci = cast_i[0] % NCK
            cast_i[0] += 1
            R.op("pool", "dma_start", [], [b, cast_dummy[ci]], out=dv(d), in_=src, key=cast_keys[ci])
            bs.append(b)
        units[name] = (d, bs, n)

    def get_unit(name):
        return units[name]

    def kcview(w, c0, ncols):
        return w.rearrange("(kc p) n -> p kc n", p=P)[:, :, c0:c0 + ncols]

    def v3(k):
        return lambda d: d.rearrange("p (k n) -> p k n", k=k)

    for f in range(2):
        wg = w_gu[f].rearrange("(kc p) (gu x) -> p kc gu x", p=P, gu=2)
        for u in range(11):
            srcs = []
            for gu in range(2):
                srcs.append(((lambda gu: lambda d: d.rearrange("p (kc gu x) -> p kc gu x", kc=KC, gu=2)[:, :, gu, :])(gu),
                             wg[:, :, gu, 2 * u * 128:2 * u * 128 + 256]))
            def_unit("gu%d_%d" % (f, u), KC * 512, srcs, "gu%d" % f)
        wd = w_dn[f].rearrange("(kc p) n -> p kc n", p=P)
        for half in range(2):
            for gi, (k0, nk) in enumerate(DN_GROUPS):
                def_unit("dn%d_%d_%d" % (f, half, gi), nk * 512, [(v3(nk), wd[:, k0:k0 + nk, half * 512:(half + 1) * 512])], "dn%d" % f)
    for u in range(6):
        def_unit("xbc%d" % u, KC * 512, [(v3(KC), kcview(w_in, OFF_XBC + u * 512, 512))], "win_a")
    def_unit("dtv", KC * 288,
             [(lambda d: d.rearrange("p (k n) -> p k n", k=KC)[:, :, 0:32], kcview(w_in, OFF_DT, 32)),
              (lambda d: d.rearrange("p (k n) -> p k n", k=KC)[:, :, 32:288], kcview(w_in, OFF_V, 256))], "win_a")
    for u in range(4):
        def_unit("z%d" % u, KC * 512, [(v3(KC), kcview(w_in, OFF_Z + u * 512, 512))], "win_a")
    for u in range(2):
        def_unit("q%d" % u, KC * 512, [(v3(KC), kcview(w_in, OFF_Q + u * 512, 512))], "win_b")
    def_unit("kdup", KC * 512,
             [((lambda j, t: lambda d: d.rearrange("p (k x) -> p k x", k=KC)[:, :, j * 128 + t * 64:j * 128 + t * 64 + 64])(j, t),
               kcview(w_in, OFF_K + j * 64, 64)) for j in range(4) for t in range(2)], "win_b")
    for u in range(2):
        def_unit("gm%d" % u, KC * 512, [(v3(KC), kcview(w_in, OFF_GM + u * 512, 512))], "win_b")
        def_unit("ga%d" % u, KC * 512, [(v3(KC), kcview(w_in, OFF_GA + u * 512, 512))], "win_b")
    wm = w_brm.rearrange("(kc p) n -> p kc n", p=P)
    for u in range(4):
        def_unit("brm%d" % u, 16 * 256, [(v3(16), wm[:, :, u * 256:(u + 1) * 256])], "brm")
    for u in range(2):
        def_unit("bra%d" % u, KC * 512, [(v3(KC), kcview(w_bra, u * 512, 512))], "bra")
    for u in range(2):
        def_unit("wo%d" % u, KC * 512, [(v3(KC), kcview(w_o, u * 512, 512))], "wo")

    def prepass(seq):
        for name in seq:
            if name not in units:
                make_unit(name)

    def ffn_unit_seq(f):
        s_ = []
        for u in range(11):
            s_.append("gu%d_%d" % (f, u))
        for half in range(2):
            for gi in range(3):
                s_.append("dn%d_%d_%d" % (f, half, gi))
        return s_

    def mix_unit_seq():
        s_ = []
        for u in range(6):
            s_.append("xbc%d" % u)
        s_.append("dtv")
        for u in range(4):
            s_.append("z%d" % u)
        for u in range(2):
            s_.append("q%d" % u)
        s_.append("kdup")
        for u in range(4):
            s_.append(("gm%d" if u % 2 == 0 else "ga%d") % (u // 2))
        for u in range(4):
            s_.append("brm%d" % u)
        for u in range(2):
            s_.append("bra%d" % u)
        for u in range(2):
            s_.append("wo%d" % u)
        return s_

    class WStream:
        def __init__(self, name, nslot):
            self.seq = []
            self.issued = 0
            self.cons = 0
            self.nslot = nslot
            self.look = nslot - 1
            self.ring = [tt("wslot%s%d" % (name, i), [P, SLOT], BF16) for i in range(nslot)]
            self.keys = [R.key("ring%s%d" % (name, i)) for i in range(nslot)]

        def _issue(self):
            i = self.issued
            d, bs, n = get_unit(self.seq[i])
            slot = self.ring[i % self.nslot]
            R.op("sp", "dma_start", bs, slot.b, out=slot.ap[:, 0:n], in_=d, key=self.keys[i % self.nslot])
            self.issued += 1

        def acquire(self, name):
            i = self.cons
            assert self.seq[i] == name, (self.seq[i], name)
            while self.issued < min(len(self.seq), i + 1 + self.look):
                self._issue()
            self.cons += 1
            return self.ring[i % self.nslot]

    tiles = []
    prompt = Seq()
    samp = [Seq() for _ in range(n_samp)]
    for t in range(nt):
        tl = Tile()
        tl.blocks = []
        for b in range(NB):
            k = Blk()
            k.n = 128
            k.seq = prompt
            k.pos0 = (t * NB + b) * 128
            k.first = (k.pos0 == 0)
            k.last = (k.pos0 + 128 == n_prompt)
            k.is_samp = False
            tl.blocks.append(k)
        r0 = t * TMAX
        tl.src = xp[r0:r0 + TMAX, :].rearrange("(b p) d -> p b d", p=128)
        tl.dst = yp[r0:r0 + TMAX, :].rearrange("(b p) d -> p b d", p=128)
        tl.runs = [(prompt, 0, TMAX)]
        tl.cscol = r0
        tl.is_samp = False
        tiles.append(tl)
    tl = Tile()
    tl.blocks = []
    for b in range(n_samp):
        k = Blk()
        k.n = samp_len
        k.seq = samp[b]
        k.pos0 = PAST_LEN
        k.first = True
        k.last = True
        k.is_samp = True
        k.sidx = b
        tl.blocks.append(k)
    tl.src = xs.rearrange("(b p) d -> p b d", p=samp_len)
    tl.dst = ys.rearrange("(b p) d -> p b d", p=samp_len)
    tl.runs = [(samp[b], b * samp_len, samp_len) for b in range(n_samp)]
    tl.cscol = n_prompt
    tl.is_samp = True
    tiles.append(tl)
    for tl in tiles:
        c = 0
        for bi, k in enumerate(tl.blocks):
            k.col0 = c
            k.bi = bi
            c += k.n
        tl.T = c
        tl.n = tl.blocks[0].n
        tl.nb = len(tl.blocks)

    prepass(ffn_unit_seq(0) + mix_unit_seq() + ffn_unit_seq(1))
    wsA = WStream("A", 3)
    wsB = WStream("B", 2)

    NBX = max(NB, n_samp)
    x_bufs = [tt("x%d" % i, [P, NBX, D], F32, nb=1) for i in range(2)]
    x_keys = [R.key("x%d" % i) for i in range(2)]
    xst_keys = [out_key("xst%d" % i) for i in range(2)]
    xnT = tt("xnT", [P, KC, TMAX], BF16, nb=NBX)
    uT = tt("uT", [P, KC, TMAX], BF16, nb=NBX)
    hT = tt("hT", [P, FC, TMAX], BF16, nb=FC)
    sg = [tt("sg%d" % i, [P, TMAX], F32) for i in range(2)]

    class Scr:
        def __init__(self, nm):
            self.junk = tt("junk" + nm, [P, 2048], BF16)
            self.etmp = tt("etmp" + nm, [P, 512], F32)
            self.stat = [tt("stat%s%d" % (nm, i), [P, 8], F32) for i in range(4)]
            self.i = 0

        def new_stat(self):
            s_ = self.stat[self.i % 4]
            self.i += 1
            return s_

    scrA = Scr("A")
    scrB = Scr("B")

    def load_x(ti):
        tl = tiles[ti]
        xb = x_bufs[ti % 2]
        R.op("sp", "dma_start", [], xb.b, out=xb.ap[0:tl.n, 0:tl.nb, :], in_=tl.src, key=x_keys[ti % 2])

    def store_x(ti):
        tl = tiles[ti]
        xb = x_bufs[ti % 2]
        R.op("sp", "dma_start", xb.b, [], out=tl.dst, in_=xb.ap[0:tl.n, 0:tl.nb, :], key=xst_keys[ti % 2])

    def rstd_from(ssq_ap, n, st, col, dim, extra):
        e2 = float(extra) ** 2
        R.op("act", "activation", st.b, st.b, out=st.ap[0:n, col:col + 1], in_=ssq_ap, func=AF.Sqrt, scale=1.0 / (dim * e2), bias=EPS / e2)
        R.op("dve", "reciprocal", st.b, st.b, out=st.ap[0:n, col:col + 1], in_=st.ap[0:n, col:col + 1])

    def norm_T(tl, xb, g_ap, dst, scr):
        for k in tl.blocks:
            n = k.n
            st = scr.new_stat()
            junk = scr.junk
            xin = xb.ap[0:n, k.bi, :]
            R.op("act", "activation", xb.b, junk.b + st.b, out=junk.ap[0:n, 0:D], in_=xin, func=AF.Square, accum_out=st.ap[0:n, 0:1])
            rstd_from(st.ap[0:n, 0:1], n, st, 1, D, 1.0)
            xn = junk.ap[0:n, 1024:2048]
            R.op("act", "activation", xb.b + st.b, junk.b, out=xn, in_=xin, func=AF.Copy, scale=st.ap[0:n, 1:2])
            ps = psum.next()
            ptv = ps.bf.rearrange("p (a b) -> p a b", b=128)
            for kc in range(KC):
                R.op("pe", "transpose", junk.b + cbf.b, ps.b, ptv[:, kc, 0:n], junk.ap[0:n, 1024 + kc * 128:1024 + (kc + 1) * 128], ident[0:n, 0:n])
            R.op("dve", "tensor_tensor", ps.b + par.b, [dst.b[k.bi]], out=dst.ap[:, :, k.col0:k.col0 + n], in0=ptv[:, :, 0:n],
                 in1=g_ap.unsqueeze(2).to_broadcast([P, KC, n]), op=ALU.mult)
            psum.rel(ps)

    def post_norm_add(xb, k, accs, g_ap, scale, scr):
        n = k.n
        st = scr.new_stat()
        junk = scr.junk
        for h in range(2):
            R.op("act", "activation", accs[h].b, junk.b + st.b, out=junk.ap[0:n, h * 512:(h + 1) * 512], in_=accs[h].ap[0:n, :], func=AF.Square,
                 accum_out=st.ap[0:n, h:h + 1])
        R.op("dve", "tensor_tensor", st.b, st.b, out=st.ap[0:n, 2:3], in0=st.ap[0:n, 0:1], in1=st.ap[0:n, 1:2], op=ALU.add)
        rstd_from(st.ap[0:n, 2:3], n, st, 3, D, scale)
        for h in range(2):
            et = scr.etmp
            R.op("dve", "scalar_tensor_tensor", accs[h].b + st.b + par.b, et.b, out=et.ap[0:n, :], in0=accs[h].ap[0:n, :], scalar=st.ap[0:n, 3:4],
                 in1=g_ap[0:n, h * 512:(h + 1) * 512], op0=ALU.mult, op1=ALU.mult)
            xsl = xb.ap[0:n, k.bi, h * 512:(h + 1) * 512]
            R.op("pool", "tensor_tensor", et.b + xb.b, xb.b, out=xsl, in0=et.ap[0:n, :], in1=xsl, op=ALU.add)

    def ffn(tl, xb, f):
        T = tl.T
        norm_T(tl, xb, gpre[f], xnT, scrB)
        for u in range(11):
            slot = wsB.acquire("gu%d_%d" % (f, u))
            wv = slot.ap.rearrange("p (kc gu jj c) -> p kc gu jj c", kc=KC, gu=2, jj=2)
            for jj in range(2):
                j = 2 * u + jj
                gps = psum.next()
                ups = psum.next()
                for gu, ps in ((0, gps), (1, ups)):
                    for kc in range(KC):
                        R.op("pe", "matmul", slot.b + xnT.b[0:tl.nb], ps.b, ps.ap[:, 0:T], lhsT=wv[:, kc, gu, jj, :], rhs=xnT.ap[:, kc, 0:T],
                             start=(kc == 0), stop=(kc == KC - 1))
                s = sg[j % 2]
                R.op("act", "activation", gps.b, s.b, out=s.ap[:, 0:T], in_=gps.ap[:, 0:T], func=AF.Tanh, scale=0.5)
                R.op("dve", "scalar_tensor_tensor", gps.b + s.b, s.b, out=s.ap[:, 0:T], in0=s.ap[:, 0:T], scalar=1.0, in1=gps.ap[:, 0:T],
                     op0=ALU.add, op1=ALU.mult)
                R.op("dve", "scalar_tensor_tensor", ups.b + s.b, [hT.b[j]], out=hT.ap[:, j, 0:T], in0=s.ap[:, 0:T], scalar=0.5, in1=ups.ap[:, 0:T],
                     op0=ALU.mult, op1=ALU.mult)
                psum.rel(gps, ups)
        accs = [[psum.next() for _ in tl.blocks] for _ in range(2)]
        for half in range(2):
            for gi, (k0, nk) in enumerate(DN_GROUPS):
                slot = wsB.acquire("dn%d_%d_%d" % (f, half, gi))
                wv = slot.ap[:, 0:nk * 512].rearrange("p (k n) -> p k n", k=nk)
                for kk in range(nk):
                    kc = k0 + kk
                    for k in tl.blocks:
                        R.op("pe", "matmul", slot.b + [hT.b[kc]], accs[half][k.bi].b, accs[half][k.bi].ap[0:k.n, :],
                             lhsT=hT.ap[:, kc, k.col0:k.col0 + k.n], rhs=wv[:, kk, :], start=(kc == 0), stop=(kc == FC - 1))
        for k in tl.blocks:
            a2 = [accs[0][k.bi], accs[1][k.bi]]
            post_norm_add(xb, k, a2, gpost[f], 0.5, scrB)
            psum.rel(*a2)

    do_mix = stages in ("all", "mix")
    if do_mix:
        for si, s in enumerate([prompt] + samp):
            s.halo = tt("halo%d" % si, [P, 24, 3], F32)
        hbufs = [(tt("hTa", [P, DI], F32, nb=4), tt("hTa_bf", [P, DI], BF16, nb=4))]
        prompt.h, prompt.hbf = hbufs[0]
        for i, s in enumerate(samp):
            s.h, s.hbf = hbufs[0]
        kT_all = tt("kT_all", [P, 4, (NB + 1) * 128], BF16, nb=NB + 1)
        v_all = tt("v_all", [P, NB + 1, 4, VW], BF16, nb=NB + 1)
        cacheT = [tt("cacheT%d" % i, [P, 4, 128], BF16) for i in range(n_samp)]
        cachev = [tt("cachev%d" % i, [P, 4, VW], BF16) for i in range(n_samp)]
        R.op("pool", "memset", [], prompt.halo.b, prompt.halo.ap, 0.0)
        R.op("pool", "memset", [], prompt.h.b, prompt.h.ap, 0.0)
        R.op("pool", "memset", [], prompt.hbf.b, prompt.hbf.ap, 0.0)
        R.op("pool", "memset", [], v_all.b, v_all.ap, 1.0)
        for c in cachev:
            R.op("pool", "memset", [], c.b, c.ap, 1.0)

        mx = Arena("mx", MX_BYTES[NB])
        ymT = mx.carve([P, 16, TMAX], BF16, nb=NBX, name="ymT")
        base = mx.off
        PREW = max(3 + TMAX, n_samp * (3 + samp_len))
        pre = [mx.carve([P, PREW], F32, name="pre") for _ in range(2)]
        acc = [mx.carve([P, TMAX], F32, name="acc") for _ in range(2)]
        xc = [mx.carve([P, TMAX], BF16, name="xc") for _ in range(2)]
        xh_tok = mx.carve([P, NBX, DI], BF16, nb=NBX, name="xh_tok")
        B_tok = mx.carve([P, NBX, 512], BF16, nb=NBX, name="B_tok")
        BT = mx.carve([P, 4, TMAX], BF16, name="BT")
        CT = mx.carve([P, 4, TMAX], BF16, name="CT")
        zs = mx.carve([P, NBX, DI], BF16, nb=NBX, name="zs")
        dtw = mx.carve([P, NBX, 32], F32, nb=NBX, name="dtw")
        dtA = mx.carve([P, NBX, 32], BF16, nb=NBX, name="dtA")
        xdt = mx.carve([P, DI], BF16, nb=4, name="xdt")
        xdtw = mx.carve([P, DI], BF16, nb=4, name="xdtw")
        rhs1 = [mx.carve([P, 8, 128], BF16, name="rhs1") for _ in range(2)]
        xd = mx.carve([P, DI], BF16, nb=4, name="xd")
        Eb = [mx.carve([P, 8, 128], BF16, name="E") for _ in range(2)]
        Gm = [mx.carve([P, 8, 128], BF16, name="G") for _ in range(2)]
        cbm = mx.carve([P, 4, 128], BF16, name="cbm")
        t1 = [mx.carve([P, 512], F32, name="t1") for _ in range(1)] * 2
        t2 = [mx.carve([P, 512], F32, name="t2") for _ in range(2)]
        yg = tt("yg", [P, DI], F32, nb=4)
        yn = scrA.junk
        eacs = mx.carve([P, 64], F32, name="eacs")
        hstage_ap = yg.ap.rearrange("p (c n) -> p c n", n=128)
        kvout0 = tt("kvout0", [P, 256], F32)
        M123 = pre + acc + xc + [xh_tok, B_tok, BT, CT, zs, dtw, dtA, xdt, xdtw] + rhs1 + [xd] + Eb + Gm + [cbm, t1[0]] + t2 + [yg, eacs, kvout0]
        mx.off = base
        cs = tt("cs_sb", [P, 2, TMAX], F32)
        qb = [mx.carve([P, TMAX], BF16, name="qb") for _ in range(2)]
        rt1 = [mx.carve([P, TMAX], F32, name="rt1") for _ in range(2)]
        rt2 = [mx.carve([P, TMAX], F32, name="rt2") for _ in range(2)]
        qT = mx.carve([P, 8, TMAX], BF16, nb=8, name="qT")
        PT = [mx.carve([P, 4, 128], BF16, name="PT") for _ in range(4)]
        attn_tok = [mx.carve([P, 1024], BF16, name="attn_tok") for _ in range(2)]
        den = [mx.carve([P, 8], F32, name="den") for _ in range(2)]
        attnT = mx.carve([P, 8, TMAX], BF16, nb=NBX, name="attnT")
        bt = [mx.carve([P, TMAX], F32, name="bt") for _ in range(2)]
        mixedT = mx.carve([P, 8, TMAX], BF16, nb=8, name="mixedT")
        krot_f = mx.carve([P, 4, 128], F32, name="krot_f")
        kvout1 = tt("kvout1", [P, 256], F32)
        gates = [mx.carve([P, 8, TMAX], BF16, nb=8, name="gate%d" % i) for i in range(2)]
        mixf = mx.carve([P, 8, TMAX], F32, nb=8, name="mixf")
        M456 = [cs] + qb + rt1 + rt2 + [qT] + PT + attn_tok + den + [attnT] + bt + [mixedT, krot_f, kvout1] + gates + [mixf]
        cs_key = R.key("cs")
        misc_keys = {}

        def misc_out(name, reads, **kw):
            k = out_key("o_" + name)
            R.op("sp", "dma_start", reads, [], key=k, **kw)

        def switch(frm, to):
            pend = Rec.pending([b for t_ in frm for b in t_.b])
            for t_ in to:
                t_.renew(pend)

        def bc3(ap2, a, b_):
            return ap2.unsqueeze(2).to_broadcast([ap2.shape[0], a, b_])

        def bcm(ap2, a, b_):
            return ap2.unsqueeze(1).to_broadcast([ap2.shape[0], a, b_])

        def lkey(nm):
            return R.key(nm)

        def load_halo(k):
            s = k.seq
            for r in range(3):
                R.op("sp", "dma_start", [], s.halo.b, out=s.halo.ap[:, :, r], in_=st_conv[k.sidx, r].rearrange("(c p) -> p c", p=P),
                     key=lkey("lh%d_%d" % (k.sidx, r)), allow_slow_non_contiguous=True)

        def load_seq_state_ssm(k):
            s = k.seq
            b = k.sidx
            R.op("sp", "dma_start", [], yg.b, out=hstage_ap, in_=st_ssm[b].rearrange("(c p) n -> p c n", p=P), key=lkey("ls%d" % b))
            for c4 in range(4):
                ps = psum.next()
                pv4 = ps.ap.rearrange("p (a b) -> p a b", b=128)
                for cc in range(4):
                    c = c4 * 4 + cc
                    R.op("pe", "transpose", [yg.b[c4]] + par.b, ps.b, pv4[:, cc, :], hstage_ap[:, c, :], identf)
                R.op("act", "activation", ps.b, [s.h.b[c4]], out=s.h.ap[:, c4 * 512:(c4 + 1) * 512], in_=ps.ap, func=AF.Copy)
                R.op("dve", "tensor_scalar", ps.b, [s.hbf.b[c4]], out=s.hbf.ap[:, c4 * 512:(c4 + 1) * 512], in0=ps.ap, scalar1=1.0, scalar2=None, op0=ALU.mult)
                psum.rel(ps)

        def load_seq_state_kv(k):
            b = k.sidx
            ckf = rt1[b % 2]
            cvf = rt2[b % 2]
            assert TMAX >= 256
            R.op("sp", "dma_start", [], ckf.b, out=ckf.ap[:, 0:256], in_=ck[b], key=lkey("lk%d" % b))
            R.op("sp", "dma_start", [], cvf.b, out=cvf.ap[:, 0:256], in_=cv[b], key=lkey("lv%d" % b))
            kd = attn_tok[b % 2]
            kdv = kd.ap[:, 0:512].rearrange("p (j t c) -> p j t c", j=4, t=2)
            for dd in range(2):
                R.op("dve", "tensor_copy", ckf.b, kd.b, out=kdv[:, :, dd, :], in_=ckf.ap[:, 0:256].rearrange("p (j c) -> p j c", c=64))
            ps = psum.next()
            ptv = ps.bf.rearrange("p (a b) -> p a b", b=128)
            for j in range(4):
                R.op("pe", "transpose", kd.b + cbf.b, ps.b, ptv[:, j, :], kd.ap[:, j * 128:(j + 1) * 128], ident)
            R.op("act", "activation", ps.b, cacheT[b].b, out=cacheT[b].ap, in_=ptv[:, 0:4, :], func=AF.Copy)
            psum.rel(ps)
            R.op("act", "activation", cvf.b, cachev[b].b, out=cachev[b].ap[:, :, 0:64], in_=cvf.ap[:, 0:256].rearrange("p (j c) -> p j c", c=64),
                 func=AF.Copy)

        def store_seq_state(k):
            s = k.seq
            if k.is_samp:
                cdst = conv_s[k.sidx]
                sdst = ssm_s[k.sidx]
            else:
                cdst = conv_p
                sdst = ssm_p
            for r in range(3):
                misc_out("conv%d" % r, s.halo.b, out=cdst[r].rearrange("(c p) -> p c", p=P), in_=s.halo.ap[:, :, r], allow_slow_non_contiguous=True)
            for c4 in range(4):
                ps = psum.next()
                pv4 = ps.ap.rearrange("p (a b) -> p a b", b=128)
                for cc in range(4):
                    c = c4 * 4 + cc
                    R.op("pe", "transpose", [s.h.b[c4]] + par.b, ps.b, pv4[:, cc, :], s.h.ap[:, c * 128:(c + 1) * 128], identf)
                R.op("act", "activation", ps.b, [yg.b[c4]], out=hstage_ap[:, c4 * 4:(c4 + 1) * 4, :], in_=pv4, func=AF.Copy)
                psum.rel(ps)
            misc_out("ssm", yg.b, out=sdst.rearrange("(c p) n -> p c n", p=P), in_=hstage_ap)

        def m1_xbc(tl):
            T = tl.T
            nruns = len(tl.runs)
            ln = tl.runs[0][2]
            W3 = 3 + ln
            n = tl.n
            slots = {}

            def views(c):
                pr = pre[c % 2]
                prv = pr.ap[:, 0:nruns * W3].rearrange("p (r w) -> p r w", w=W3)
                ac = acc[c % 2]
                acv = ac.ap[:, 0:T].rearrange("p (r w) -> p r w", w=ln)
                if c < 16:
                    dap, dbufs = xc[c % 2].ap[:, 0:T], xc[c % 2].b
                elif c < 20:
                    dap, dbufs = BT.ap[:, c - 16, 0:T], BT.b
                else:
                    dap, dbufs = CT.ap[:, c - 20, 0:T], CT.b
                return pr, prv, ac, acv, dap, dbufs

            def s1(c):
                u, cc = divmod(c, 4)
                if cc == 0:
                    slots[u] = wsA.acquire("xbc%d" % u)
                slot = slots[u]
                wv = slot.ap.rearrange("p (k cc c) -> p k cc c", k=KC, cc=4)
                pr, prv, ac, acv, dap, dbufs = views(c)
                ps = psum.next()
                for kc in range(KC):
                    R.op("pe", "matmul", slot.b + uT.b[0:tl.nb], ps.b, ps.ap[:, 0:T], lhsT=wv[:, kc, cc, :], rhs=uT.ap[:, kc, 0:T],
                         start=(kc == 0), stop=(kc == KC - 1))
                for r, (s_, col0, l_) in enumerate(tl.runs):
                    R.op("pool", "tensor_copy", s_.halo.b, pr.b, out=prv[:, r, 0:3], in_=s_.halo.ap[:, c, :])
                R.op("act", "activation", ps.b, pr.b, out=prv[:, :, 3:W3], in_=ps.ap[:, 0:T].rearrange("p (r w) -> p r w", w=ln), func=AF.Copy)
                psum.rel(ps)
                for r, (s_, col0, l_) in enumerate(tl.runs):
                    R.op("pool", "tensor_copy", pr.b, s_.halo.b, out=s_.halo.ap[:, c, :], in_=prv[:, r, ln:ln + 3])

            def s2(c):
                pr, prv, ac, acv, dap, dbufs = views(c)
                R.op("dve", "tensor_scalar", pr.b + par.b, ac.b, out=acv, in0=prv[:, :, 0:ln], scalar1=convw[:, c * 4:c * 4 + 1], scalar2=None, op0=ALU.mult)
                for j in range(1, 4):
                    R.op("dve", "scalar_tensor_tensor", pr.b + par.b + ac.b, ac.b, out=acv, in0=prv[:, :, j:j + ln],
                         scalar=convw[:, c * 4 + j:c * 4 + j + 1], in1=acv, op0=ALU.mult, op1=ALU.add)
                R.op("act", "activation", ac.b + par.b, dbufs, out=dap, in_=ac.ap[:, 0:T], func=AF.Silu, bias=convb[:, c:c + 1])

            def s3(c):
                if c >= 20:
                    return
                pr, prv, ac, acv, dap, dbufs = views(c)
                ps2 = psum.next()
                ptv = ps2.bf.rearrange("p (a b) -> p a b", b=128)
                for k in tl.blocks:
                    R.op("pe", "transpose", dbufs + cbf.b, ps2.b, ptv[0:n, k.bi, :], dap[:, k.col0:k.col0 + n], ident)
                if c < 16:
                    R.op("act", "activation", ps2.b, xh_tok.b[0:tl.nb], out=xh_tok.ap[0:n, 0:tl.nb, c * 128:(c + 1) * 128], in_=ptv[0:n, 0:tl.nb, :], func=AF.Copy)
                else:
                    g = c - 16
                    R.op("act", "activation", ps2.b, B_tok.b[0:tl.nb], out=B_tok.ap[0:n, 0:tl.nb, g * 128:(g + 1) * 128], in_=ptv[0:n, 0:tl.nb, :], func=AF.Copy)
                psum.rel(ps2)

            for c in range(24 + 2):
                if c < 24:
                    s1(c)
                if 0 <= c - 1 < 24:
                    s2(c - 1)
                if 0 <= c - 2 < 24:
                    s3(c - 2)

        def m2_dtv_z(tl):
            n = tl.n
            slot = wsA.acquire("dtv")
            wv = slot.ap[:, 0:KC * 288].rearrange("p (k n) -> p k n", k=KC)
            for k in tl.blocks:
                ps = psum.next()
                for kc in range(KC):
                    R.op("pe", "matmul", slot.b + [uT.b[k.bi]], ps.b, ps.ap[0:n, 0:288], lhsT=uT.ap[:, kc, k.col0:k.col0 + n], rhs=wv[:, kc, :],
                         start=(kc == 0), stop=(kc == KC - 1))
                dw = dtw.ap[0:n, k.bi, :]
                if "z1" in DBG:
                    psum.rel(ps)
                    continue
                R.op("dve", "tensor_tensor", ps.b + par.b, [dtw.b[k.bi]], out=dw, in0=ps.ap[0:n, 0:32], in1=dtb[0:n, :], op=ALU.add)
                R.op("act", "activation", [dtw.b[k.bi]], [dtw.b[k.bi]], out=dw, in_=dw, func=AF.Exp)
                R.op("act", "activation", [dtw.b[k.bi]], [dtw.b[k.bi]], out=dw, in_=dw, func=AF.Ln, bias=1.0)
                R.op("dve", "tensor_tensor", [dtw.b[k.bi]] + dconst.b, [dtA.b[k.bi]], out=dtA.ap[0:n, k.bi, :], in0=dw, in1=Aneg[0:n, :], op=ALU.mult)
                if "z2" in DBG:
                    psum.rel(ps)
                    continue
                vb = k.bi + 1
                if "z4" not in DBG:
                    R.op("act", "activation", ps.b, [v_all.b[vb]], out=v_all.ap[0:n, vb, :, 0:64], in_=ps.ap[0:n, 32:288].rearrange("p (j c) -> p j c", c=64),
                         func=AF.Copy)
                if k.last and "z5" not in DBG:
                    ko = kvout0
                    R.op("dve", "tensor_scalar", ps.b, ko.b, out=ko.ap[0:n, :], in0=ps.ap[0:n, 32:288], scalar1=1.0, scalar2=None, op0=ALU.mult)
                    if "z6" in DBG:
                        pass
                    elif k.is_samp:
                        misc_out("v", ko.b, out=v_s[k.sidx * samp_len:(k.sidx + 1) * samp_len, :], in_=ko.ap[0:n, :])
                    else:
                        misc_out("v", ko.b, out=v_p, in_=ko.ap[0:n, :])
                psum.rel(ps)
            for u in range(4):
                slot = wsA.acquire("z%d" % u)
                if "z3" in DBG:
                    continue
                wv = slot.ap.rearrange("p (k n) -> p k n", k=KC)
                for k in tl.blocks:
                    ps = psum.next()
                    for kc in range(KC):
                        R.op("pe", "matmul", slot.b + [uT.b[k.bi]], ps.b, ps.ap[0:n, :], lhsT=uT.ap[:, kc, k.col0:k.col0 + n], rhs=wv[:, kc, :],
                             start=(kc == 0), stop=(kc == KC - 1))
                    R.op("act", "activation", ps.b, [zs.b[k.bi]], out=zs.ap[0:n, k.bi, u * 512:(u + 1) * 512], in_=ps.ap[0:n, :], func=AF.Silu)
                    psum.rel(ps)

        def m3_ssd(tl, k, mid=None):
            L = k.n
            c0 = k.col0
            s = k.seq
            bi = k.bi
            dA = dtA.ap[0:L, bi, :]
            v3_ = lambda ap: ap.rearrange("p (h c) -> p h c", c=64)
            ps_s = psum.next()
            R.op("pe", "matmul", [dtA.b[bi]] + cbf.b, ps_s.b, ps_s.ap[0:L, 0:32], lhsT=tri[0:L, 0:L], rhs=dA, start=True, stop=True)
            R.op("pe", "matmul", [dtA.b[bi]] + cbf.b, ps_s.b, ps_s.ap[:, 32:64], lhsT=ones[0:L, :], rhs=dA, start=True, stop=True)
            R.op("act", "activation", ps_s.b, eacs.b, out=eacs.ap[0:L, 0:32], in_=ps_s.ap[0:L, 0:32], func=AF.Exp)
            R.op("act", "activation", ps_s.b, eacs.b, out=eacs.ap[:, 32:64], in_=ps_s.ap[:, 32:64], func=AF.Exp)
            psum.rel(ps_s)
            decay = eacs.ap[:, 32:64]
            ps_cb = psum.next()
            pcb = ps_cb.ap.rearrange("p (g l) -> p g l", l=128)
            for g in range(4):
                R.op("pe", "matmul", BT.b + CT.b, ps_cb.b, pcb[0:L, g, 0:L], lhsT=BT.ap[:, g, c0:c0 + L], rhs=CT.ap[:, g, c0:c0 + L], start=True, stop=True)
            R.op("dve", "tensor_tensor", ps_cb.b + cbf.b, cbm.b, out=cbm.ap[0:L, :, 0:L], in0=pcb[0:L, :, 0:L], in1=bcm(tri[0:L, 0:L], 4, L), op=ALU.mult)
            psum.rel(ps_cb)

            def stA(g):
                r1 = rhs1[g % 2]
                E = Eb[g % 2]
                gs = slice(g * 512, (g + 1) * 512)
                R.op("pool", "tensor_tensor", [dtA.b[bi]] + cbf.b, r1.b, out=r1.ap[0:L, :, 0:L], in0=bc3(dA[:, 8 * g:8 * g + 8], 8, L),
                     in1=bcm(tri[0:L, 0:L], 8, L), op=ALU.mult)
                R.op("pool", "tensor_tensor", [xh_tok.b[bi], dtw.b[bi]], [xdt.b[g]], out=v3_(xdt.ap[0:L, gs]),
                     in0=v3_(xh_tok.ap[0:L, bi, gs]), in1=bc3(dtw.ap[0:L, bi, 8 * g:8 * g + 8], 8, 64), op=ALU.mult)
                R.op("dve", "tensor_tensor", [xh_tok.b[bi]] + par.b, [xd.b[g]], out=v3_(xd.ap[0:L, gs]),
                     in0=v3_(xh_tok.ap[0:L, bi, gs]), in1=bc3(dskip[0:L, 8 * g:8 * g + 8], 8, 64), op=ALU.mult)
                for hh in range(2):
                    ps = psum.next()
                    pv_ = ps.ap.rearrange("p (h l) -> p h l", l=128)[0:L, :, 0:L]
                    h0 = 8 * g + 4 * hh
                    if L == 128:
                        R.op("pe", "matmul", r1.b + cbf.b, ps.b, ps.ap[:, :], lhsT=ones[0:L, 0:L], rhs=r1.ap[0:L, 4 * hh:4 * hh + 4, :].rearrange("p h l -> p (h l)"),
                             start=True, stop=False)
                        R.op("pe", "matmul", [dtA.b[bi]] + cbf.b, ps.b, ps.ap[:, :], lhsT=m2[0:L, 0:L], rhs=bc3(dA[:, h0:h0 + 4], 4, L),
                             start=False, stop=False)
                        R.op("pe", "matmul", cbf.b, ps.b, ps.ap[:, :], lhsT=ident[0:L, 0:L], rhs=negm[0:L, 0:4, :].rearrange("p h l -> p (h l)"), start=False, stop=True)
                    else:
                        for h4 in range(4):
                            o2 = pv_[:, h4, :]
                            R.op("pe", "matmul", r1.b + cbf.b, ps.b, o2, lhsT=ones[0:L, 0:L], rhs=r1.ap[0:L, 4 * hh + h4, 0:L], start=True, stop=False)
                            R.op("pe", "matmul", [dtA.b[bi]] + cbf.b, ps.b, o2, lhsT=m2[0:L, 0:L], rhs=dA[:, h0 + h4:h0 + h4 + 1].to_broadcast([L, L]),
                                 start=False, stop=False)
                            R.op("pe", "matmul", cbf.b, ps.b, o2, lhsT=ident[0:L, 0:L], rhs=negm[0:L, h4, 0:L], start=False, stop=True)
                    R.op("act", "activation", ps.b, E.b, out=E.ap[0:L, 4 * hh:4 * hh + 4, 0:L], in_=pv_, func=AF.Exp)
                    psum.rel(ps)
                R.op("pool", "tensor_tensor", [xdt.b[g]] + E.b, [xdtw.b[g]], out=v3_(xdtw.ap[0:L, gs]), in0=v3_(xdt.ap[0:L, gs]),
                     in1=E.ap[0:L, :, L - 1:L].to_broadcast([L, 8, 64]), op=ALU.mult)

            def stB(g):
                E = Eb[g % 2]
                G_ = Gm[g % 2]
                R.op("dve", "tensor_tensor", E.b + cbm.b, G_.b, out=G_.ap[0:L, :, 0:L], in0=E.ap[0:L, :, 0:L], in1=bcm(cbm.ap[0:L, g, 0:L], 8, L), op=ALU.mult)
                y_ps = psum.next()
                R.op("pe", "matmul", [xd.b[g]] + cbf.b, y_ps.b, y_ps.ap[0:L, :], lhsT=ident[0:L, 0:L], rhs=xd.ap[0:L, g * 512:(g + 1) * 512], start=True, stop=False)
                for h in range(8):
                    hh_ = 8 * g + h
                    R.op("pe", "matmul", G_.b + [xdt.b[g]], y_ps.b, y_ps.ap[0:L, h * 64:(h + 1) * 64], lhsT=G_.ap[0:L, h, 0:L], rhs=xdt.ap[0:L, hh_ * 64:(hh_ + 1) * 64],
                         start=False, stop=(h == 7))
                yi_ps = psum.next()
                R.op("pe", "matmul", CT.b + [s.hbf.b[g]], yi_ps.b, yi_ps.ap[0:L, :], lhsT=CT.ap[:, g, c0:c0 + L], rhs=s.hbf.ap[:, g * 512:(g + 1) * 512],
                     start=True, stop=True)
                a1 = t1[0]
                a2 = t2[g % 2]
                R.op("dve", "tensor_tensor", yi_ps.b + eacs.b, a1.b, out=v3_(a1.ap[0:L, :]), in0=v3_(yi_ps.ap[0:L, :]), in1=bc3(eacs.ap[0:L, 8 * g:8 * g + 8], 8, 64),
                     op=ALU.mult)
                R.op("dve", "tensor_tensor", y_ps.b + a1.b, a2.b, out=a2.ap[0:L, :], in0=y_ps.ap[0:L, :], in1=a1.ap[0:L, :], op=ALU.add)
                psum.rel(y_ps, yi_ps)
                R.op("pool", "tensor_tensor", a2.b + [zs.b[bi]], [yg.b[g]], out=yg.ap[0:L, g * 512:(g + 1) * 512], in0=a2.ap[0:L, :],
                     in1=zs.ap[0:L, bi, g * 512:(g + 1) * 512], op=ALU.mult)

            def stC(g):
                S_ps = psum.next()
                R.op("pe", "matmul", [B_tok.b[bi], xdtw.b[g]], S_ps.b, S_ps.ap[:, :], lhsT=B_tok.ap[0:L, bi, g * 128:(g + 1) * 128],
                     rhs=xdtw.ap[0:L, g * 512:(g + 1) * 512], start=True, stop=True)
                hsl = s.h.ap[:, g * 512:(g + 1) * 512]
                R.op("dve", "tensor_tensor", [s.h.b[g]] + eacs.b, [s.h.b[g]], out=v3_(hsl), in0=v3_(hsl), in1=bc3(decay[:, 8 * g:8 * g + 8], 8, 64), op=ALU.mult)
                R.op("dve", "tensor_tensor", [s.h.b[g]] + S_ps.b, [s.h.b[g]], out=hsl, in0=S_ps.ap[:, :], in1=hsl, op=ALU.add)
                psum.rel(S_ps)
                R.op("act", "activation", [s.h.b[g]], [s.hbf.b[g]], out=s.hbf.ap[:, g * 512:(g + 1) * 512], in_=hsl, func=AF.Copy)

            stA(0)
            if mid is not None:
                mid()
            for g in range(4):
                if g + 1 < 4:
                    stA(g + 1)
                stB(g)
                stC(g)

        def m3_norm(tl, k):
            L = k.n
            c0 = k.col0
            bi = k.bi
            st = scrA.new_stat()
            R.op("act", "activation", yg.b, scrA.junk.b + st.b, out=scrA.junk.ap[0:L, 0:DI], in_=yg.ap[0:L, :], func=AF.Square, accum_out=st.ap[0:L, 0:1])
            rstd_from(st.ap[0:L, 0:1], L, st, 1, DI, 1.0)
            R.op("act", "activation", yg.b + st.b, yn.b, out=yn.ap[0:L, :], in_=yg.ap[0:L, :], func=AF.Copy, scale=st.ap[0:L, 1:2])
            for i in range(2):
                ps = psum.next()
                ptv = ps.bf.rearrange("p (a b) -> p a b", b=128)
                for cc in range(8):
                    c = 8 * i + cc
                    R.op("pe", "transpose", yn.b + cbf.b, ps.b, ptv[:, cc, 0:L], yn.ap[0:L, c * 128:(c + 1) * 128], ident[0:L, 0:L])
                R.op("dve", "tensor_tensor", ps.b + par.b, [ymT.b[bi]], out=ymT.ap[:, 8 * i:8 * i + 8, c0:c0 + L], in0=ptv[:, :, 0:L],
                     in1=bc3(gmn[:, 8 * i:8 * i + 8], 8, L), op=ALU.mult)
                psum.rel(ps)

        def rope(ps, T, i, dst_ap, dst_bufs, kout_j=None):
            q_b = qb[i % 2]
            a1 = rt1[i % 2]
            a2 = rt2[i % 2]
            R.op("act", "activation", ps.b, q_b.b, out=q_b.ap[:, 0:T], in_=ps.ap[:, 0:T], func=AF.Copy)
            sw = psum.next()
            R.op("pe", "matmul", q_b.b + cbf.b, sw.b, sw.ap[:, 0:T], lhsT=rm, rhs=q_b.ap[:, 0:T], start=True, stop=True)
            R.op("dve", "tensor_tensor", ps.b + cs.b, a1.b, out=a1.ap[:, 0:T], in0=ps.ap[:, 0:T], in1=cs.ap[:, 0, 0:T], op=ALU.mult)
            R.op("dve", "tensor_tensor", sw.b + cs.b, a2.b, out=a2.ap[:, 0:T], in0=sw.ap[:, 0:T], in1=cs.ap[:, 1, 0:T], op=ALU.mult)
            psum.rel(sw)
            return a1, a2

        def m4_qkv(tl):
            T = tl.T
            n = tl.n
            R.op("sp", "dma_start", [], cs.b, out=cs.ap[:, :, 0:T], in_=cs_d[:, :, tl.cscol:tl.cscol + T], key=cs_key)
            i = 0
            for u in range(2):
                slot = wsA.acquire("q%d" % u)
                wv = slot.ap.rearrange("p (k cc c) -> p k cc c", k=KC, cc=4)
                for cc in range(4):
                    c = 4 * u + cc
                    ps = psum.next()
                    for kc in range(KC):
                        R.op("pe", "matmul", slot.b + uT.b[0:tl.nb], ps.b, ps.ap[:, 0:T], lhsT=wv[:, kc, cc, :], rhs=uT.ap[:, kc, 0:T],
                             start=(kc == 0), stop=(kc == KC - 1))
                    a1, a2 = rope(ps, T, i, None, None)
                    psum.rel(ps)
                    R.op("pool", "tensor_tensor", a1.b + a2.b, [qT.b[c]], out=qT.ap[:, c, 0:T], in0=a1.ap[:, 0:T], in1=a2.ap[:, 0:T], op=ALU.add)
                    i += 1
            slot = wsA.acquire("kdup")
            wv = slot.ap.rearrange("p (k j c) -> p k j c", k=KC, j=4)
            for j in range(4):
                ps = psum.next()
                for kc in range(KC):
                    R.op("pe", "matmul", slot.b + uT.b[0:tl.nb], ps.b, ps.ap[:, 0:T], lhsT=wv[:, kc, j, :], rhs=uT.ap[:, kc, 0:T],
                         start=(kc == 0), stop=(kc == KC - 1))
                a1, a2 = rope(ps, T, i, None, None)
                psum.rel(ps)
                kview = kT_all.ap[:, j, 128:128 + tl.nb * 128].rearrange("p (b w) -> p b w", w=128)[:, :, 0:n]
                R.op("pool", "tensor_tensor", a1.b + a2.b, kT_all.b[1:1 + tl.nb], out=kview, in0=a1.ap[:, 0:T].rearrange("p (b w) -> p b w", w=n),
                     in1=a2.ap[:, 0:T].rearrange("p (b w) -> p b w", w=n), op=ALU.add)
                if tl.blocks[-1].last:
                    if tl.is_samp:
                        lo, wd = 0, T
                    else:
                        lo, wd = T - 128, 128
                    R.op("dve", "tensor_tensor", a1.b + a2.b, krot_f.b, out=krot_f.ap[:, j, 0:wd], in0=a1.ap[:, lo:lo + wd], in1=a2.ap[:, lo:lo + wd], op=ALU.add)
                i += 1
            if tl.blocks[-1].last:
                wd = T if tl.is_samp else 128
                ps = psum.next()
                pv4 = ps.ap[:, 0:256].rearrange("p (j c) -> p j c", c=64)
                for j in range(4):
                    R.op("pe", "transpose", krot_f.b + par.b, ps.b, pv4[0:wd, j, :], krot_f.ap[0:64, j, 0:wd], identf[0:64, 0:64])
                ko = kvout1
                R.op("act", "activation", ps.b, ko.b, out=ko.ap[0:wd, :], in_=ps.ap[0:wd, 0:256], func=AF.Copy)
                psum.rel(ps)
                misc_out("k", ko.b, out=(k_s if tl.is_samp else k_p), in_=ko.ap[0:wd, :])

        def m4_attn(tl, k):
            n = k.n
            c0 = k.col0
            bi = k.bi
            keysets = []
            if k.is_samp:
                keysets.append((cacheT[k.sidx].ap, cacheT[k.sidx].b, cachev[k.sidx].ap, cachev[k.sidx].b, 128, None, None))
                ownb = (None, None)
            else:
                if not k.first:
                    keysets.append((kT_all.ap[:, :, bi * 128:(bi + 1) * 128], [kT_all.b[bi]], v_all.ap[:, bi, :, :], [v_all.b[bi]], 128, None, 0))
                ownb = (1, None)
            keysets.append((kT_all.ap[:, :, (bi + 1) * 128:(bi + 2) * 128], [kT_all.b[bi + 1]], v_all.ap[:, bi + 1, :, :], [v_all.b[bi + 1]], n, ownb[0], ownb[1]))
            nks = len(keysets)
            at = attn_tok[bi % 2]

            def scores(j):
                sbk = [psum.next(), psum.next()]
                svs = [b_.ap.rearrange("p (s q) -> p s q", q=128) for b_ in sbk]
                ptb = [PT[2 * (j % 2)], PT[2 * (j % 2) + 1]]
                for ks, ke in enumerate(keysets):
                    kap, kbufs, nk = ke[0], ke[1], ke[4]
                    for hl in range(4):
                        hf = hl % 2
                        hp = hl // 2
                        ch = (4 * j + hl) // 2
                        R.op("pe", "matmul", kbufs + [qT.b[ch]], sbk[hf].b, svs[hf][0:nk, ks * 2 + hp, 0:n], lhsT=kap[hf * 64:(hf + 1) * 64, j, 0:nk],
                             rhs=qT.ap[hf * 64:(hf + 1) * 64, ch, c0:c0 + n], start=True, stop=True)
                for hf in range(2):
                    pt = ptb[hf]
                    for ks, ke in enumerate(keysets):
                        nk, b0, b1 = ke[4], ke[5], ke[6]
                        sl = slice(2 * ks, 2 * ks + 2)
                        if b0 is None and b1 is None:
                            R.op("act", "activation", sbk[hf].b, pt.b, out=pt.ap[0:nk, sl, 0:n], in_=svs[hf][0:nk, sl, 0:n], func=AF.Exp, scale=0.125)
                        else:
                            for q0, bcol in ((0, b0), (64, b1)):
                                kw = {} if bcol is None else {"bias": negcol[0:nk, bcol:bcol + 1]}
                                R.op("act", "activation", sbk[hf].b + par.b, pt.b, out=pt.ap[0:nk, sl, q0:q0 + 64], in_=svs[hf][0:nk, sl, q0:q0 + 64],
                                     func=AF.Exp, scale=0.125, **kw)
                psum.rel(*sbk)
                return ptb

            def pv_out(j, ptb):
                o_ps = psum.next()
                ov = o_ps.ap[:, 0:260].rearrange("p (h c) -> p h c", c=65)
                for hl in range(4):
                    hf = hl % 2
                    hp = hl // 2
                    for ks, ke in enumerate(keysets):
                        vap, vbufs, nk = ke[2], ke[3], ke[4]
                        R.op("pe", "matmul", ptb[hf].b + vbufs, o_ps.b, ov[0:n, hl, :], lhsT=ptb[hf].ap[0:nk, ks * 2 + hp, 0:n], rhs=vap[0:nk, j, 0:65],
                             start=(ks == 0), stop=(ks == nks - 1))
                dn = den[j % 2]
                R.op("dve", "tensor_tensor", o_ps.b + dconst.b, dn.b, out=dn.ap[0:n, 0:4], in0=ov[0:n, :, 64], in1=esink[0:n, 4 * j:4 * j + 4], op=ALU.add)
                R.op("dve", "reciprocal", dn.b, dn.b, out=dn.ap[0:n, 4:8], in_=dn.ap[0:n, 0:4])
                R.op("dve", "tensor_tensor", o_ps.b + dn.b, at.b, out=at.ap[0:n, j * 256:(j + 1) * 256].rearrange("p (h c) -> p h c", c=64),
                     in0=ov[0:n, :, 0:64], in1=bc3(dn.ap[0:n, 4:8], 4, 64), op=ALU.mult)
                psum.rel(o_ps)

            pts = scores(0)
            for j in range(4):
                nxt = scores(j + 1) if j + 1 < 4 else None
                pv_out(j, pts)
                pts = nxt
            ps = psum.next()
            ptv = ps.bf.rearrange("p (a b) -> p a b", b=128)
            for c in range(8):
                R.op("pe", "transpose", at.b + cbf.b, ps.b, ptv[:, c, 0:n], at.ap[0:n, c * 128:(c + 1) * 128], ident[0:n, 0:n])
            R.op("act", "activation", ps.b, [attnT.b[bi]], out=attnT.ap[:, :, c0:c0 + n], in_=ptv[:, :, 0:n], func=AF.Copy)
            psum.rel(ps)

        def m4_save_prev(tl):
            if tl.is_samp:
                return
            R.op("act", "activation", [kT_all.b[NB]], [kT_all.b[0]], out=kT_all.ap[:, :, 0:128], in_=kT_all.ap[:, :, NB * 128:(NB + 1) * 128], func=AF.Copy)
            R.op("pool", "tensor_copy", [v_all.b[NB]], [v_all.b[0]], out=v_all.ap[:, 0, :, :], in_=v_all.ap[:, NB, :, :])

        def m5_branches(tl, xb):
            T = tl.T
            for u in range(4):
                nm = ("gm%d" if u % 2 == 0 else "ga%d") % (u // 2)
                slot = wsA.acquire(nm)
                wv = slot.ap.rearrange("p (k cc c) -> p k cc c", k=KC, cc=4)
                for cc in range(4):
                    fo = 4 * (u // 2) + cc
                    ps = psum.next()
                    for kc in range(KC):
                        R.op("pe", "matmul", slot.b + uT.b[0:tl.nb], ps.b, ps.ap[:, 0:T], lhsT=wv[:, kc, cc, :], rhs=uT.ap[:, kc, 0:T],
                             start=(kc == 0), stop=(kc == KC - 1))
                    gsb = gates[(u % 2)]
                    R.op("act", "activation", ps.b, [gsb.b[fo]], out=gsb.ap[:, fo, 0:T], in_=ps.ap[:, 0:T], func=AF.Sigmoid)
                    psum.rel(ps)
            for u in range(4):
                slot = wsA.acquire("brm%d" % u)
                wv = slot.ap.rearrange("p (k cc c) -> p k cc c", k=16, cc=2)
                for cc in range(2):
                    fo = 2 * u + cc
                    ps = psum.next()
                    for kc in range(16):
                        R.op("pe", "matmul", slot.b + ymT.b[0:tl.nb], ps.b, ps.ap[:, 0:T], lhsT=wv[:, kc, cc, :], rhs=ymT.ap[:, kc, 0:T],
                             start=(kc == 0), stop=(kc == 15))
                    R.op("dve", "tensor_tensor", ps.b + [gates[0].b[fo]], [mixf.b[fo]], out=mixf.ap[:, fo, 0:T], in0=ps.ap[:, 0:T], in1=gates[0].ap[:, fo, 0:T],
                         op=ALU.mult)
                    psum.rel(ps)
            for u in range(2):
                slot = wsA.acquire("bra%d" % u)
                wv = slot.ap.rearrange("p (k cc c) -> p k cc c", k=KC, cc=4)
                for cc in range(4):
                    fo = 4 * u + cc
                    ps = psum.next()
                    for kc in range(KC):
                        R.op("pe", "matmul", slot.b + attnT.b[0:tl.nb], ps.b, ps.ap[:, 0:T], lhsT=wv[:, kc, cc, :], rhs=attnT.ap[:, kc, 0:T],
                             start=(kc == 0), stop=(kc == KC - 1))
                    b_ = bt[fo % 2]
                    R.op("dve", "tensor_tensor", ps.b + [gates[1].b[fo]], b_.b, out=b_.ap[:, 0:T], in0=ps.ap[:, 0:T], in1=gates[1].ap[:, fo, 0:T], op=ALU.mult)
                    psum.rel(ps)
                    R.op("pool", "tensor_tensor", b_.b + [mixf.b[fo]], [mixedT.b[fo]], out=mixedT.ap[:, fo, 0:T], in0=b_.ap[:, 0:T], in1=mixf.ap[:, fo, 0:T],
                         op=ALU.add)
            accs = [[psum.next() for _ in tl.blocks] for _ in range(2)]
            for half in range(2):
                slot = wsA.acquire("wo%d" % half)
                wv = slot.ap.rearrange("p (k n) -> p k n", k=KC)
                for kc in range(KC):
                    for k in tl.blocks:
                        R.op("pe", "matmul", slot.b + [mixedT.b[kc]], accs[half][k.bi].b, accs[half][k.bi].ap[0:k.n, :],
                             lhsT=mixedT.ap[:, kc, k.col0:k.col0 + k.n], rhs=wv[:, kc, :], start=(kc == 0), stop=(kc == KC - 1))
            for k in tl.blocks:
                a2 = [accs[0][k.bi], accs[1][k.bi]]
                post_norm_add(xb, k, a2, gpost["m"], 1.0, scrA)
                psum.rel(*a2)


        def mixer(tl, xb):
            switch(M456, M123)
            norm_T(tl, xb, gpre["m"], uT, scrA)
            for k in tl.blocks:
                if k.is_samp and "st" in DBG:
                    load_halo(k)
            if "m1" in DBG:
                m1_xbc(tl)
            if "m2" in DBG:
                m2_dtv_z(tl)
            prevk = [None]
            for k in tl.blocks:
                if k.is_samp and "st" in DBG:
                    load_seq_state_ssm(k)

                def mid(pk=prevk[0]):
                    if pk is not None:
                        m3_norm(tl, pk)
                if k.is_samp or k.last:
                    mid()
                    m3_ssd(tl, k)
                    m3_norm(tl, k)
                    prevk[0] = None
                else:
                    m3_ssd(tl, k, mid)
                    prevk[0] = k
                if k.last and "st" in DBG:
                    store_seq_state(k)
            if prevk[0] is not None:
                m3_norm(tl, prevk[0])
            switch(M123, M456)
            for k in tl.blocks:
                if k.is_samp and "st" in DBG:
                    load_seq_state_kv(k)
            if "m4" in DBG:
                m4_qkv(tl)
            if "m4a" in DBG:
                for k in tl.blocks:
                    m4_attn(tl, k)
                m4_save_prev(tl)
            if "m5" in DBG:
                m5_branches(tl, xb)

    NT = len(tiles)
    for t in range(NT):
        wsA.seq += mix_unit_seq() if do_mix else []
    wsB.seq += ffn_unit_seq(0)
    for t in range(NT):
        if t > 0:
            wsB.seq += ffn_unit_seq(1)
        if t + 1 < NT:
            wsB.seq += ffn_unit_seq(0)
    wsB.seq += ffn_unit_seq(1)
    sched = Sched(KA, KB)
    load_x(0)
    ffn(tiles[0], x_bufs[0], 0)
    if NT > 1:
        load_x(1)
    prog = {"a": 0, "b1": 1}

    def wait_for(key, val):
        while prog[key] < val:
            if not sched.yield_():
                raise RuntimeError("schedule deadlock on %s>=%d" % (key, val))

    def pause(n):
        for _ in range(n):
            if not sched.yield_():
                break

    def fa():
        for t in range(NT):
            wait_for("b1", t + 1)
            if do_mix:
                mixer(tiles[t], x_bufs[t % 2])
            prog["a"] = t + 1

    def fb():
        for t in range(NT):
            if t > 0:
                wait_for("a", t)
                pause(PAUSE1)
                ffn(tiles[t - 1], x_bufs[(t - 1) % 2], 1)
                store_x(t - 1)
                if t + 1 < NT:
                    load_x(t + 1)
                    pause(PAUSE2)
            if t + 1 < NT:
                ffn(tiles[t + 1], x_bufs[(t + 1) % 2], 0)
                prog["b1"] = t + 2
        wait_for("a", NT)
        ffn(tiles[NT - 1], x_bufs[(NT - 1) % 2], 1)
        store_x(NT - 1)

    if INTERLEAVE:
        R.sched = sched
        sched.run(fa, fb)
        R.sched = None
    else:
        for t in range(NT):
            if do_mix:
                mixer(tiles[t], x_bufs[t % 2])
            if t > 0:
                ffn(tiles[t - 1], x_bufs[(t - 1) % 2], 1)
                store_x(t - 1)
                if t + 1 < NT:
                    load_x(t + 1)
            if t + 1 < NT:
                ffn(tiles[t + 1], x_bufs[(t + 1) % 2], 0)
        ffn(tiles[NT - 1], x_bufs[(NT - 1) % 2], 1)
        store_x(NT - 1)
    R.emit()
    return nc


MX_BYTES = {2: 68352, 4: 150 * 1024}

def _layout(entries):
    d = {}
    o = 0
    for name, n in entries:
        d[name] = (o, n)
        o += n
    d["_n"] = o
    return d


PAR = _layout([("gpre1", 8), ("gpre2", 8), ("gmix", 8), ("gmn", 16), ("convw", 96), ("convb", 24),
               ("gpost1", 1024), ("gpost2", 1024), ("gpostm", 1024), ("dtb", 32), ("alog", 32), ("dskip", 32),
               ("sink", 16), ("identf", 128), ("negcol", 2)])
CB = _layout([("ident", 128), ("tri", 128), ("m2", 128), ("ones", 128), ("rm", 128), ("negm", 1024), ("maskA", 512), ("maskB", 512)])


def pack_params(inp):
    a = np.zeros((P, PAR["_n"]), np.float32)

    def put(name, v):
        o, n = PAR[name]
        a[:, o:o + n] = v

    def chunked(v):
        return np.asarray(v, np.float32).reshape(-1, P).T

    def bc(v):
        return np.broadcast_to(np.asarray(v, np.float32).reshape(1, -1), (P, np.asarray(v).size))

    put("gpre1", chunked(inp["ffn1_pre_g"][0]))
    put("gpre2", chunked(inp["ffn2_pre_g"][0]))
    put("gmix", chunked(inp["mix_pre_g"][0]))
    put("gmn", chunked(inp["m_norm_g"][0]))
    cw = np.asarray(inp["conv_w"][0], np.float32)
    put("convw", cw.reshape(4, 24, P).transpose(2, 1, 0).reshape(P, 96))
    put("convb", chunked(inp["conv_b"][0]))
    put("gpost1", bc(inp["ffn1_post_g"][0]))
    put("gpost2", bc(inp["ffn2_post_g"][0]))
    put("gpostm", bc(inp["mix_post_g"][0]))
    put("dtb", bc(inp["dt_bias"][0]))
    put("alog", bc(inp["a_log"][0]))
    put("dskip", bc(inp["d_skip"][0]))
    put("sink", bc(inp["attn_sink"][0]))
    put("identf", np.eye(P, dtype=np.float32))
    ncol = np.zeros((P, 2), np.float32)
    ncol[:64, 0] = -30000.0
    ncol[64:, 1] = -30000.0
    put("negcol", ncol)
    return a


def pack_consts():
    a = np.zeros((P, CB["_n"]), np.float32)

    def put(name, v):
        o, n = CB[name]
        a[:, o:o + n] = v.reshape(P, n)

    k = np.arange(P)[:, None]
    l = np.arange(P)[None, :]
    put("ident", np.eye(P))
    put("tri", (k <= l).astype(np.float32))
    put("m2", -(k <= l).astype(np.float32))
    put("ones", np.ones((P, P)))
    rmm = np.zeros((P, P), np.float32)
    for hb in (0, 64):
        for d2 in range(8):
            rmm[hb + d2 + 8, hb + d2] = 1.0
            rmm[hb + d2, hb + d2 + 8] = 1.0
    put("rm", rmm)
    ng = np.where(l < k, -30000.0, 0.0).astype(np.float32)
    put("negm", np.broadcast_to(ng[:, None, :], (P, 8, P)))
    mA = np.ones((P, P), np.float32)
    mA[:64, 64:] = 0.0
    mB = np.ones((P, P), np.float32)
    mB[64:, :64] = 0.0
    put("maskA", np.broadcast_to(mA[:, None, :], (P, 4, P)))
    put("maskB", np.broadcast_to(mB[:, None, :], (P, 4, P)))
    return a.astype(ml_dtypes.bfloat16)


def rope_tables(n_prompt, samp_len):
    pos = np.concatenate([np.arange(n_prompt), PAST_LEN + np.arange(samp_len)]).astype(np.float32)
    inv = (np.float32(THETA) ** (-np.arange(0, ROT, 2, dtype=np.float32) / np.float32(ROT))).astype(np.float32)
    ang = (pos[:, None] * inv[None, :]).astype(np.float32)
    c = np.cos(ang).astype(np.float32).T
    s = np.sin(ang).astype(np.float32).T
    cos_t = np.ones((P, pos.size), np.float32)
    sin_t = np.zeros((P, pos.size), np.float32)
    for hb in (0, 64):
        cos_t[hb:hb + 8] = c
        cos_t[hb + 8:hb + 16] = c
        sin_t[hb:hb + 8] = -s
        sin_t[hb + 8:hb + 16] = s
    return cos_t, sin_t


N_CORES = 8
NB_DEFAULT = 2
_cache = {}


def make_in_maps(inp, n_prompt, n_cores):
    f = lambda a: np.ascontiguousarray(np.asarray(a, dtype=np.float32))
    params = pack_params(inp)
    cbf = pack_consts()
    cos_t, sin_t = rope_tables(n_prompt, 32)
    cs = np.stack([np.concatenate([cos_t, cos_t[:, n_prompt:]], axis=1), np.concatenate([sin_t, sin_t[:, n_prompt:]], axis=1)], axis=1)
    cs = np.ascontiguousarray(cs, dtype=np.float32)
    shared = {
        "w_gu1": f(inp["ffn1_w_gu"][0]), "w_gu2": f(inp["ffn2_w_gu"][0]),
        "w_dn1": f(inp["ffn1_w_down"][0]), "w_dn2": f(inp["ffn2_w_down"][0]),
        "w_in": f(inp["w_in"][0]), "w_brm": f(inp["w_br_m"][0]), "w_bra": f(inp["w_br_a"][0]), "w_o": f(inp["w_o"][0]),
        "params": params, "cbf": cbf, "cs_t": cs,
    }
    maps = []
    for c in range(n_cores):
        m = dict(shared)
        m["xp"] = f(inp["x_prompt"][c])
        sl = slice(2 * c, 2 * c + 2)
        m["xs"] = f(inp["x_sample"][sl]).reshape(64, D)
        m["st_conv"] = f(inp["state_conv"][0][sl])
        m["st_ssm"] = f(inp["state_ssm"][0][sl]).reshape(2, DI, NS)
        m["ck"] = f(inp["cache_k"][0][sl]).reshape(2, 128, 256)
        m["cv"] = f(inp["cache_v"][0][sl]).reshape(2, 128, 256)
        maps.append(m)
    return maps


def gather(results, n_prompt, n_cores):
    g = lambda k: [np.asarray(r[k], dtype=np.float32) for r in results]
    rows = min(128, n_prompt)
    yp = np.stack(g("yp")).reshape(n_cores, n_prompt, D)
    ys = np.concatenate([a.reshape(2, 32, D) for a in g("ys")], 0)
    conv_p = np.stack(g("conv_p")).reshape(1, n_cores, 3, CONV)
    ssm_p = np.stack(g("ssm_p")).reshape(1, n_cores, NH, HP, NS)
    k_p = np.stack(g("k_p")).reshape(1, n_cores, rows, 4, 64)
    v_p = np.stack(g("v_p")).reshape(1, n_cores, rows, 4, 64)
    conv_s = np.concatenate(g("conv_s"), 0).reshape(1, 2 * n_cores, 3, CONV)
    ssm_s = np.concatenate(g("ssm_s"), 0).reshape(1, 2 * n_cores, NH, HP, NS)
    k_s = np.concatenate([a.reshape(2, 32, 4, 64) for a in g("k_s")], 0).reshape(1, 2 * n_cores, 32, 4, 64)
    v_s = np.concatenate([a.reshape(2, 32, 4, 64) for a in g("v_s")], 0).reshape(1, 2 * n_cores, 32, 4, 64)
    return (yp, ys, conv_p, ssm_p, k_p, v_p, conv_s, ssm_s, k_s, v_s)


def kernel(**inputs):
    n_prompt = int(np.asarray(inputs["x_prompt"]).shape[1])
    n_cores = int(np.asarray(inputs["x_prompt"]).shape[0])
    nc = build(n_prompt, NB_DEFAULT)
    maps = make_in_maps(inputs, n_prompt, n_cores)
    res = run_bass_kernel_spmd(nc, maps, core_ids=list(range(n_cores)))
    return gather(res.results, n_prompt, n_cores)
```

```python
import math
import threading
import numpy as np
import ml_dtypes
import concourse.bass as bass
import concourse.mybir as mybir
from concourse.bass_utils import run_bass_kernel_spmd

F32 = mybir.dt.float32
BF16 = mybir.dt.bfloat16
AF = mybir.ActivationFunctionType
ALU = mybir.AluOpType
AX = mybir.AxisListType
P = 128

D = 1024
DFF = 2816
DI = 2048
CONV = 3072
NH = 32
HP = 64
NS = 128
NG = 4
KC = 8
FC = 22
DINP = 8736
EPS = 1e-6
PAST_LEN = 1024
ROT = 16
THETA = 500000.0
OFF_Z, OFF_XBC, OFF_DT, OFF_Q, OFF_K, OFF_V, OFF_GM, OFF_GA = 0, 2048, 5120, 5152, 6176, 6432, 6688, 7712

ENGS = ("pe", "act", "dve", "pool", "sp")


class Buf:
    __slots__ = ("name", "last_w", "readers", "excl")

    def __init__(self, name="", pending=None):
        self.name = name
        self.excl = False
        self.last_w = None
        self.readers = list(pending) if pending else []


class Op:
    __slots__ = ("eng", "fn", "deps", "sig", "count", "key", "ord", "is_dma", "idx")


class DmaKey:
    def __init__(self, sem, grouped=False):
        self.sem = sem
        self.n = 0
        self.grouped = grouped


class Rec:
    def __init__(self, nc):
        self.nc = nc
        self.streams = {e: [] for e in ENGS}
        self.sems = {e: nc.alloc_semaphore("sem_" + e) for e in ENGS}
        self.final_keys = []
        self.sched = None

    def key(self, name, grouped=False):
        self.nk = getattr(self, "nk", 0) + 1
        return DmaKey(self.nc.alloc_semaphore("dk%d_%s" % (self.nk, name)), grouped)

    def op(self, eng, method, reads, writes, *args, key=None, **kwargs):
        o = Op()
        o.eng = eng
        o.fn = (method, args, kwargs)
        o.key = key
        o.is_dma = key is not None
        o.sig = False
        o.count = None
        o.idx = len(self.streams[eng])
        if key is not None:
            key.n += 1
            o.ord = key.n
        best = {}
        dmas = []
        ex = [b for b in reads if b.excl]
        if ex:
            writes = list(writes) + [b for b in ex if b not in writes]

        dk = {}

        def add(d, kind):
            if d is o:
                return
            if d.is_dma:
                c = dk.get(id(d.key))
                if c is None or d.ord > c.ord:
                    dk[id(d.key)] = d
                return
            if d.eng == eng and not o.is_dma:
                if eng == "pe" or kind != "raw":
                    return
            c = best.get(d.eng)
            if c is None or d.idx > c.idx:
                best[d.eng] = d

        for b in reads:
            if b.last_w is not None:
                add(b.last_w, "raw")
        for b in writes:
            if b.last_w is not None:
                add(b.last_w, "waw")
            for r in b.readers:
                add(r, "war")
        o.deps = list(best.values()) + list(dk.values())
        for d in best.values():
            d.sig = True
        for b in reads:
            b.readers.append(o)
        for b in writes:
            b.last_w = o
            b.readers = []
        self.streams[eng].append(o)
        if self.sched is not None:
            self.sched.tick(o)
        return o

    @staticmethod
    def pending(bufs):
        best = {}
        dk = {}
        for b in bufs:
            for d in ([b.last_w] if b.last_w is not None else []) + b.readers:
                if d.is_dma:
                    c = dk.get(id(d.key))
                    if c is None or d.ord > c.ord:
                        dk[id(d.key)] = d
                else:
                    c = best.get(d.eng)
                    if c is None or d.idx > c.idx:
                        best[d.eng] = d
        return list(dk.values()) + list(best.values())

    def emit(self):
        nc = self.nc
        for e in ENGS:
            c = 0
            for o in self.streams[e]:
                if o.sig and not o.is_dma:
                    c += 1
                    o.count = c
        sems = self.sems

        def runner(en):
            def f(e):
                waited = {}
                for o in self.streams[en]:
                    need = {}
                    for d in o.deps:
                        if d.is_dma:
                            sem = d.key.sem
                            val = 16 * (d.key.n if d.key.grouped else d.ord)
                        else:
                            sem = sems[d.eng]
                            val = d.count
                        k = id(sem)
                        if k not in need or need[k][1] < val:
                            need[k] = (sem, val)
                    items = [(s, v) for k, (s, v) in need.items() if waited.get(k, 0) < v]
                    for k, (s, v) in need.items():
                        if waited.get(k, 0) < v:
                            waited[k] = v
                    attach = None
                    if items and not o.is_dma:
                        attach = items.pop()
                    for s, v in items:
                        e.wait_ge(s, v)
                    m, a, kw = o.fn
                    ins = getattr(e, m)(*a, **kw)
                    if attach is not None:
                        ins._wait_ge(attach[0], attach[1])
                    if o.is_dma:
                        ins.then_inc(o.key.sem, 16)
                    elif o.sig:
                        ins.then_inc(sems[en], 1)
                if en == "sp":
                    for k in self.final_keys:
                        if k.n > 0:
                            e.wait_ge(k.sem, 16 * k.n)
            return f

        with nc.Block() as blk:
            blk.tensor(runner("pe"))
            blk.scalar(runner("act"))
            blk.vector(runner("dve"))
            blk.gpsimd(runner("pool"))
            blk.sync(runner("sp"))


class Sched:
    def __init__(self, ka, kb):
        self.cv = threading.Condition()
        self.turn = None
        self.alive = {"A": False, "B": False}
        self.cnt = {"A": 0, "B": 0}
        self.k = {"A": ka, "B": kb}
        self.local = threading.local()
        self.exc = None

    def me(self):
        return getattr(self.local, "name", None)

    def run(self, fa, fb):
        self.alive = {"A": True, "B": True}
        self.turn = "A"
        self.cnt = {"A": 0, "B": 0}

        def wrap(name, f):
            other = "B" if name == "A" else "A"

            def g():
                self.local.name = name
                with self.cv:
                    while self.turn != name:
                        self.cv.wait()
                try:
                    f()
                except BaseException as e:
                    self.exc = e
                finally:
                    with self.cv:
                        self.alive[name] = False
                        self.turn = other
                        self.cv.notify_all()
            return g

        ta = threading.Thread(target=wrap("A", fa))
        tb = threading.Thread(target=wrap("B", fb))
        ta.start()
        tb.start()
        ta.join()
        tb.join()
        self.turn = None
        if self.exc is not None:
            e = self.exc
            self.exc = None
            raise e

    def yield_(self):
        name = self.me()
        if name is None:
            return False
        other = "B" if name == "A" else "A"
        if not self.alive[other]:
            return False
        with self.cv:
            self.turn = other
            self.cv.notify_all()
            while self.turn != name:
                self.cv.wait()
        return True

    def tick(self, o):
        name = self.me()
        if name is None or o.is_dma or o.eng == "sp":
            return
        if name == "A":
            if o.eng != "pe":
                self.cnt["A"] += 1
        elif o.eng == "pe":
            self.cnt["B"] += 1
        if self.cnt[name] >= self.k[name]:
            self.cnt[name] = 0
            self.yield_()


class TT:
    def __init__(self, ap, nb=1, name=""):
        self.ap = ap
        self.b = [Buf(name) for _ in range(nb)]

    def renew(self, pending):
        self.b = [Buf("", pending) for _ in self.b]


def prod(xs):
    r = 1
    for x in xs:
        r *= x
    return r


class Blk:
    pass


class Tile:
    pass


class Seq:
    pass


DN_GROUPS = ((0, 8), (8, 8), (16, 6))
INTERLEAVE = True
KA = 4
KB = 8
VW = 72
DBG = set("m1,m2,m3,st,m4,m4a,m5".split(","))


def build(n_prompt, NB, n_samp=2, samp_len=32, stages="all"):
    nc = bass.Bass("TRN2", target_bir_lowering=False)
    R = Rec(nc)
    TMAX = 128 * NB
    nt = n_prompt // TMAX
    assert nt * TMAX == n_prompt
    NPOSX = n_prompt + n_samp * samp_len

    def din(name, shape, dt=F32):
        return nc.dram_tensor(name, list(shape), dt, kind="ExternalInput").ap()

    def dout(name, shape, dt=F32):
        return nc.dram_tensor(name, list(shape), dt, kind="ExternalOutput").ap()

    xp = din("xp", [n_prompt, D])
    xs = din("xs", [n_samp * samp_len, D])
    st_conv = din("st_conv", [n_samp, 3, CONV])
    st_ssm = din("st_ssm", [n_samp, DI, NS])
    ck = din("ck", [n_samp, 128, 256])
    cv = din("cv", [n_samp, 128, 256])
    w_gu = [din("w_gu1", [D, 2 * DFF]), din("w_gu2", [D, 2 * DFF])]
    w_dn = [din("w_dn1", [DFF, D]), din("w_dn2", [DFF, D])]
    w_in = din("w_in", [D, DINP])
    w_brm = din("w_brm", [DI, D])
    w_bra = din("w_bra", [D, D])
    w_o = din("w_o", [D, D])
    NPAR = PAR["_n"]
    params_d = din("params", [P, NPAR])
    NCB = CB["_n"]
    cbf_d = din("cbf", [P, NCB], BF16)
    cs_d = din("cs_t", [P, 2, NPOSX])

    yp = dout("yp", [n_prompt, D])
    ys = dout("ys", [n_samp * samp_len, D])
    conv_p = dout("conv_p", [3, CONV])
    ssm_p = dout("ssm_p", [DI, NS])
    rows_p = min(128, n_prompt)
    k_p = dout("k_p", [rows_p, 256])
    v_p = dout("v_p", [rows_p, 256])
    conv_s = dout("conv_s", [n_samp, 3, CONV])
    ssm_s = dout("ssm_s", [n_samp, DI, NS])
    k_s = dout("k_s", [n_samp * samp_len, 256])
    v_s = dout("v_s", [n_samp * samp_len, 256])

    def out_key(name):
        k = R.key(name)
        R.final_keys.append(k)
        return k

    def sb(name, shape, dt=F32):
        return nc.alloc_sbuf_tensor(name, list(shape), dt)

    def tt(name, shape, dt=F32, nb=1):
        t = sb(name, shape, dt)
        return TT(t[:], nb, name)

    class Arena:
        def __init__(self, name, nbytes):
            self.t = sb(name, [P, nbytes // 2], BF16)
            self.nbytes = nbytes
            self.off = 0
            self.hi = 0

        def carve(self, shape, dt, nb=1, name=""):
            nel = prod(shape[1:])
            esz = 2 if dt == BF16 else 4
            nbytes = (nel * esz + 31) // 32 * 32
            assert self.off + nbytes <= self.nbytes, (name, self.off, nbytes, self.nbytes)
            a = self.t[:, self.off // 2:(self.off + nel * esz) // 2]
            self.off += nbytes
            self.hi = max(self.hi, self.off)
            if dt == F32:
                a = a.bitcast(F32)
            if len(shape) == 3:
                a = a.rearrange("p (a b) -> p a b", b=shape[2])
            elif len(shape) == 4:
                a = a.rearrange("p (a b c) -> p a b c", b=shape[2], c=shape[3])
            return TT(a, nb, name)

    par = tt("par_sb", [P, NPAR])
    cbf = tt("cbf_sb", [P, NCB], BF16)
    ld_key = R.key("ld", grouped=True)
    R.op("sp", "dma_start", [], par.b, out=par.ap, in_=params_d, key=ld_key)
    R.op("sp", "dma_start", [], cbf.b, out=cbf.ap, in_=cbf_d, key=ld_key)

    def pv(name):
        o, n = PAR[name]
        return par.ap[:, o:o + n]

    def cv_(name):
        o, n = CB[name]
        return cbf.ap[:, o:o + n]

    ident = cv_("ident")
    tri = cv_("tri")
    m2 = cv_("m2")
    ones = cv_("ones")
    rm = cv_("rm")
    negm = cv_("negm").rearrange("p (h l) -> p h l", l=128)
    maskA = cv_("maskA").rearrange("p (h l) -> p h l", l=128)
    maskB = cv_("maskB").rearrange("p (h l) -> p h l", l=128)
    _oA = CB["maskA"][0]
    assert CB["maskB"][0] == _oA + 512
    maskAB = cbf.ap[:, _oA + 256:_oA + 768].rearrange("p (h l) -> p h l", l=128)
    identf = pv("identf")
    gpre = {0: pv("gpre1"), 1: pv("gpre2"), "m": pv("gmix")}
    gpost = {0: pv("gpost1"), 1: pv("gpost2"), "m": pv("gpostm")}
    gmn = pv("gmn")
    convw = pv("convw")
    convb = pv("convb")
    dtb = pv("dtb")
    dskip = pv("dskip")
    negcol = pv("negcol")

    dconst = tt("dconst", [P, 48])
    Aneg = dconst.ap[:, 0:32]
    esink = dconst.ap[:, 32:48]
    R.op("act", "activation", par.b, dconst.b, out=dconst.ap[:, 0:32], in_=pv("alog"), func=AF.Exp)
    R.op("act", "activation", par.b, dconst.b, out=dconst.ap[:, 32:48], in_=pv("sink"), func=AF.Exp)
    R.op("dve", "tensor_scalar", dconst.b, dconst.b, out=dconst.ap[:, 0:32], in0=dconst.ap[:, 0:32], scalar1=-1.0, scalar2=None, op0=ALU.mult)

    class PsPool:
        def __init__(self):
            self.banks = []
            for i in range(8):
                t = nc.alloc_psum_tensor("ps%d" % i, [P, 512], F32)
                tb = TT(t[:], 1, "ps%d" % i)
                tb.bf = t[:].bitcast(BF16)
                tb.b[0].excl = True
                tb.live = False
                self.banks.append(tb)
            self.i = 0

        def next(self):
            for attempt in range(10000):
                for _ in range(8):
                    b = self.banks[self.i]
                    self.i = (self.i + 1) % 8
                    if not b.live:
                        b.live = True
                        return b
                if R.sched is None or not R.sched.yield_():
                    break
            raise RuntimeError("psum exhausted")

        def rel(self, *bs):
            for b in bs:
                b.live = False

    psum = PsPool()

    SLOT = 4096
    units = {}
    unit_defs = {}

    NCK = 12
    cast_keys = [R.key("cast%d" % i) for i in range(NCK)]
    cast_dummy = [Buf("castd%d" % i) for i in range(NCK)]
    cast_i = [0]
    unit_order = []

    def def_unit(name, n, srcs, grp):
        unit_defs[name] = (n, srcs, grp)
        unit_order.append(name)

    def make_unit(name):
        n, srcs, grp = unit_defs[name]
        d = nc.dram_tensor("wb_" + name, [P, n], BF16, kind="Internal").ap()
        bs = []
        for dv, src in srcs:
            b = Buf(name)
            ci = cast_i[0] % NCK
            cast_i[0] += 1
            R.op("pool", "dma_start", [], [b, cast_dummy[ci]], out=dv(d), in_=src, key=cast_keys[ci])
            bs.append(b)
        units[name] = (d, bs, n)

    def get_unit(name):
        return units[name]

    def kcview(w, c0, ncols):
        return w.rearrange("(kc p) n -> p kc n", p=P)[:, :, c0:c0 + ncols]

    def v3(k):
        return lambda d: d.rearrange("p (k n) -> p k n", k=k)

    for f in range(2):
        wg = w_gu[f].rearrange("(kc p) (gu x) -> p kc gu x", p=P, gu=2)
        for u in range(11):
            srcs = []
            for gu in range(2):
                srcs.append(((lambda gu: lambda d: d.rearrange("p (kc gu x) -> p kc gu x", kc=KC, gu=2)[:, :, gu, :])(gu),
                             wg[:, :, gu, 2 * u * 128:2 * u * 128 + 256]))
            def_unit("gu%d_%d" % (f, u), KC * 512, srcs, "gu%d" % f)
        wd = w_dn[f].rearrange("(kc p) n -> p kc n", p=P)
        for half in range(2):
            for gi, (k0, nk) in enumerate(DN_GROUPS):
                def_unit("dn%d_%d_%d" % (f, half, gi), nk * 512, [(v3(nk), wd[:, k0:k0 + nk, half * 512:(half + 1) * 512])], "dn%d" % f)
    for u in range(6):
        def_unit("xbc%d" % u, KC * 512, [(v3(KC), kcview(w_in, OFF_XBC + u * 512, 512))], "win_a")
    def_unit("dtv", KC * 288,
             [(lambda d: d.rearrange("p (k n) -> p k n", k=KC)[:, :, 0:32], kcview(w_in, OFF_DT, 32)),
              (lambda d: d.rearrange("p (k n) -> p k n", k=KC)[:, :, 32:288], kcview(w_in, OFF_V, 256))], "win_a")
    for u in range(4):
        def_unit("z%d" % u, KC * 512, [(v3(KC), kcview(w_in, OFF_Z + u * 512, 512))], "win_a")
    for u in range(2):
        def_unit("q%d" % u, KC * 512, [(v3(KC), kcview(w_in, OFF_Q + u * 512, 512))], "win_b")
    def_unit("kdup", KC * 512,
             [((lambda j, t: lambda d: d.rearrange("p (k x) -> p k x", k=KC)[:, :, j * 128 + t * 64:j * 128 + t * 64 + 64])(j, t),
               kcview(w_in, OFF_K + j * 64, 64)) for j in range(4) for t in range(2)], "win_b")
    for u in range(2):
        def_unit("gm%d" % u, KC * 512, [(v3(KC), kcview(w_in, OFF_GM + u * 512, 512))], "win_b")
        def_unit("ga%d" % u, KC * 512, [(v3(KC), kcview(w_in, OFF_GA + u * 512, 512))], "win_b")
    wm = w_brm.rearrange("(kc p) n -> p kc n", p=P)
    for u in range(4):
        def_unit("brm%d" % u, 16 * 256, [(v3(16), wm[:, :, u * 256:(u + 1) * 256])], "brm")
    for u in range(2):
        def_unit("bra%d" % u, KC * 512, [(v3(KC), kcview(w_bra, u * 512, 512))], "bra")
    for u in range(2):
        def_unit("wo%d" % u, KC * 512, [(v3(KC), kcview(w_o, u * 512, 512))], "wo")

    def prepass(seq):
        for name in seq:
            if name not in units:
                make_unit(name)

    def ffn_unit_seq(f):
        s_ = []
        for u in range(11):
            s_.append("gu%d_%d" % (f, u))
        for half in range(2):
            for gi in range(3):
                s_.append("dn%d_%d_%d" % (f, half, gi))
        return s_

    def mix_unit_seq():
        s_ = []
        for u in range(6):
            s_.append("xbc%d" % u)
        s_.append("dtv")
        for u in range(4):
            s_.append("z%d" % u)
        for u in range(2):
            s_.append("q%d" % u)
        s_.append("kdup")
        for u in range(4):
            s_.append(("gm%d" if u % 2 == 0 else "ga%d") % (u // 2))
        for u in range(4):
            s_.append("brm%d" % u)
        for u in range(2):
            s_.append("bra%d" % u)
        for u in range(2):
            s_.append("wo%d" % u)
        return s_

    class WStream:
        def __init__(self, name, nslot):
            self.seq = []
            self.issued = 0
            self.cons = 0
            self.nslot = nslot
            self.look = nslot - 1
            self.ring = [tt("wslot%s%d" % (name, i), [P, SLOT], BF16) for i in range(nslot)]
            self.keys = [R.key("ring%s%d" % (name, i)) for i in range(nslot)]

        def _issue(self):
            i = self.issued
            d, bs, n = get_unit(self.seq[i])
            slot = self.ring[i % self.nslot]
            R.op("sp", "dma_start", bs, slot.b, out=slot.ap[:, 0:n], in_=d, key=self.keys[i % self.nslot])
            self.issued += 1

        def acquire(self, name):
            i = self.cons
            assert self.seq[i] == name, (self.seq[i], name)
            while self.issued < min(len(self.seq), i + 1 + self.look):
                self._issue()
            self.cons += 1
            return self.ring[i % self.nslot]

    tiles = []
    prompt = Seq()
    samp = [Seq() for _ in range(n_samp)]
    for t in range(nt):
        tl = Tile()
        tl.blocks = []
        for b in range(NB):
            k = Blk()
            k.n = 128
            k.seq = prompt
            k.pos0 = (t * NB + b) * 128
            k.first = (k.pos0 == 0)
            k.last = (k.pos0 + 128 == n_prompt)
            k.is_samp = False
            tl.blocks.append(k)
        r0 = t * TMAX
        tl.src = xp[r0:r0 + TMAX, :].rearrange("(b p) d -> p b d", p=128)
        tl.dst = yp[r0:r0 + TMAX, :].rearrange("(b p) d -> p b d", p=128)
        tl.runs = [(prompt, 0, TMAX)]
        tl.cscol = r0
        tl.is_samp = False
        tiles.append(tl)
    tl = Tile()
    tl.blocks = []
    for b in range(n_samp):
        k = Blk()
        k.n = samp_len
        k.seq = samp[b]
        k.pos0 = PAST_LEN
        k.first = True
        k.last = True
        k.is_samp = True
        k.sidx = b
        tl.blocks.append(k)
    tl.src = xs.rearrange("(b p) d -> p b d", p=samp_len)
    tl.dst = ys.rearrange("(b p) d -> p b d", p=samp_len)
    tl.runs = [(samp[b], b * samp_len, samp_len) for b in range(n_samp)]
    tl.cscol = n_prompt
    tl.is_samp = True
    tiles.append(tl)
    for tl in tiles:
        c = 0
        for bi, k in enumerate(tl.blocks):
            k.col0 = c
            k.bi = bi
            c += k.n
        tl.T = c
        tl.n = tl.blocks[0].n
        tl.nb = len(tl.blocks)

    prepass(ffn_unit_seq(0) + mix_unit_seq() + ffn_unit_seq(1))
    wsA = WStream("A", 3)
    wsB = WStream("B", 2)

    NBX = max(NB, n_samp)
    x_bufs = [tt("x%d" % i, [P, NBX, D], F32, nb=1) for i in range(2)]
    x_keys = [R.key("x%d" % i) for i in range(2)]
    xst_keys = [out_key("xst%d" % i) for i in range(2)]
    xnT = tt("xnT", [P, KC, TMAX], BF16, nb=NBX)
    uT = tt("uT", [P, KC, TMAX], BF16, nb=NBX)
    hT = tt("hT", [P, FC, TMAX], BF16, nb=FC)
    sg = [tt("sg%d" % i, [P, TMAX], F32) for i in range(2)]

    class Scr:
        def __init__(self, nm):
            self.junk = tt("junk" + nm, [P, 2048], BF16)
            self.etmp = tt("etmp" + nm, [P, 512], F32)
            self.stat = [tt("stat%s%d" % (nm, i), [P, 8], F32) for i in range(4)]
            self.i = 0

        def new_stat(self):
            s_ = self.stat[self.i % 4]
            self.i += 1
            return s_

    scrA = Scr("A")
    scrB = Scr("B")

    def load_x(ti):
        tl = tiles[ti]
        xb = x_bufs[ti % 2]
        R.op("sp", "dma_start", [], xb.b, out=xb.ap[0:tl.n, 0:tl.nb, :], in_=tl.src, key=x_keys[ti % 2])

    def store_x(ti):
        tl = tiles[ti]
        xb = x_bufs[ti % 2]
        R.op("sp", "dma_start", xb.b, [], out=tl.dst, in_=xb.ap[0:tl.n, 0:tl.nb, :], key=xst_keys[ti % 2])

    def rstd_from(ssq_ap, n, st, col, dim, extra):
        e2 = float(extra) ** 2
        R.op("act", "activation", st.b, st.b, out=st.ap[0:n, col:col + 1], in_=ssq_ap, func=AF.Sqrt, scale=1.0 / (dim * e2), bias=EPS / e2)
        R.op("dve", "reciprocal", st.b, st.b, out=st.ap[0:n, col:col + 1], in_=st.ap[0:n, col:col + 1])

    def norm_T(tl, xb, g_ap, dst, scr):
        for k in tl.blocks:
            n = k.n
            st = scr.new_stat()
            junk = scr.junk
            xin = xb.ap[0:n, k.bi, :]
            R.op("act", "activation", xb.b, junk.b + st.b, out=junk.ap[0:n, 0:D], in_=xin, func=AF.Square, accum_out=st.ap[0:n, 0:1])
            rstd_from(st.ap[0:n, 0:1], n, st, 1, D, 1.0)
            xn = junk.ap[0:n, 1024:2048]
            R.op("act", "activation", xb.b + st.b, junk.b, out=xn, in_=xin, func=AF.Copy, scale=st.ap[0:n, 1:2])
            ps = psum.next()
            ptv = ps.bf.rearrange("p (a b) -> p a b", b=128)
            for kc in range(KC):
                R.op("pe", "transpose", junk.b + cbf.b, ps.b, ptv[:, kc, 0:n], junk.ap[0:n, 1024 + kc * 128:1024 + (kc + 1) * 128], ident[0:n, 0:n])
            R.op("dve", "tensor_tensor", ps.b + par.b, [dst.b[k.bi]], out=dst.ap[:, :, k.col0:k.col0 + n], in0=ptv[:, :, 0:n],
                 in1=g_ap.unsqueeze(2).to_broadcast([P, KC, n]), op=ALU.mult)
            psum.rel(ps)

    def post_norm_add(xb, k, accs, g_ap, scale, scr):
        n = k.n
        st = scr.new_stat()
        junk = scr.junk
        for h in range(2):
            R.op("act", "activation", accs[h].b, junk.b + st.b, out=junk.ap[0:n, h * 512:(h + 1) * 512], in_=accs[h].ap[0:n, :], func=AF.Square,
                 accum_out=st.ap[0:n, h:h + 1])
        R.op("dve", "tensor_tensor", st.b, st.b, out=st.ap[0:n, 2:3], in0=st.ap[0:n, 0:1], in1=st.ap[0:n, 1:2], op=ALU.add)
        rstd_from(st.ap[0:n, 2:3], n, st, 3, D, scale)
        for h in range(2):
            et = scr.etmp
            R.op("dve", "scalar_tensor_tensor", accs[h].b + st.b + par.b, et.b, out=et.ap[0:n, :], in0=accs[h].ap[0:n, :], scalar=st.ap[0:n, 3:4],
                 in1=g_ap[0:n, h * 512:(h + 1) * 512], op0=ALU.mult, op1=ALU.mult)
            xsl = xb.ap[0:n, k.bi, h * 512:(h + 1) * 512]
            R.op("pool", "tensor_tensor", et.b + xb.b, xb.b, out=xsl, in0=et.ap[0:n, :], in1=xsl, op=ALU.add)

    def ffn(tl, xb, f):
        T = tl.T
        norm_T(tl, xb, gpre[f], xnT, scrB)
        for u in range(11):
            slot = wsB.acquire("gu%d_%d" % (f, u))
            wv = slot.ap.rearrange("p (kc gu jj c) -> p kc gu jj c", kc=KC, gu=2, jj=2)
            for jj in range(2):
                j = 2 * u + jj
                gps = psum.next()
                ups = psum.next()
                for gu, ps in ((0, gps), (1, ups)):
                    for kc in range(KC):
                        R.op("pe", "matmul", slot.b + xnT.b[0:tl.nb], ps.b, ps.ap[:, 0:T], lhsT=wv[:, kc, gu, jj, :], rhs=xnT.ap[:, kc, 0:T],
                             start=(kc == 0), stop=(kc == KC - 1))
                s = sg[j % 2]
                R.op("act", "activation", gps.b, s.b, out=s.ap[:, 0:T], in_=gps.ap[:, 0:T], func=AF.Tanh, scale=0.5)
                R.op("dve", "scalar_tensor_tensor", gps.b + s.b, s.b, out=s.ap[:, 0:T], in0=s.ap[:, 0:T], scalar=1.0, in1=gps.ap[:, 0:T],
                     op0=ALU.add, op1=ALU.mult)
                R.op("dve", "scalar_tensor_tensor", ups.b + s.b, [hT.b[j]], out=hT.ap[:, j, 0:T], in0=s.ap[:, 0:T], scalar=0.5, in1=ups.ap[:, 0:T],
                     op0=ALU.mult, op1=ALU.mult)
                psum.rel(gps, ups)
        accs = [[psum.next() for _ in tl.blocks] for _ in range(2)]
        for half in range(2):
            for gi, (k0, nk) in enumerate(DN_GROUPS):
                slot = wsB.acquire("dn%d_%d_%d" % (f, half, gi))
                wv = slot.ap[:, 0:nk * 512].rearrange("p (k n) -> p k n", k=nk)
                for kk in range(nk):
                    kc = k0 + kk
                    for k in tl.blocks:
                        R.op("pe", "matmul", slot.b + [hT.b[kc]], accs[half][k.bi].b, accs[half][k.bi].ap[0:k.n, :],
                             lhsT=hT.ap[:, kc, k.col0:k.col0 + k.n], rhs=wv[:, kk, :], start=(kc == 0), stop=(kc == FC - 1))
        for k in tl.blocks:
            a2 = [accs[0][k.bi], accs[1][k.bi]]
            post_norm_add(xb, k, a2, gpost[f], 0.5, scrB)
            psum.rel(*a2)

    do_mix = stages in ("all", "mix")
    if do_mix:
        for si, s in enumerate([prompt] + samp):
            s.halo = tt("halo%d" % si, [P, 24, 3], F32)
        hbufs = [(tt("hTa", [P, DI], F32, nb=4), tt("hTa_bf", [P, DI], BF16, nb=4))]
        prompt.h, prompt.hbf = hbufs[0]
        for i, s in enumerate(samp):
            s.h, s.hbf = hbufs[0]
        kT_all = tt("kT_all", [P, 4, (NB + 1) * 128], BF16, nb=NB + 1)
        v_all = tt("v_all", [P, NB + 1, 4, VW], BF16, nb=NB + 1)
        cacheT = [tt("cacheT%d" % i, [P, 4, 128], BF16) for i in range(n_samp)]
        cachev = [tt("cachev%d" % i, [P, 4, VW], BF16) for i in range(n_samp)]
        R.op("pool", "memset", [], prompt.halo.b, prompt.halo.ap, 0.0)
        R.op("pool", "memset", [], prompt.h.b, prompt.h.ap, 0.0)
        R.op("pool", "memset", [], prompt.hbf.b, prompt.hbf.ap, 0.0)
        R.op("pool", "memset", [], v_all.b, v_all.ap, 1.0)
        for c in cachev:
            R.op("pool", "memset", [], c.b, c.ap, 1.0)

        mx = Arena("mx", MX_BYTES[NB])
        ymT = mx.carve([P, 16, TMAX], BF16, nb=NBX, name="ymT")
        base = mx.off
        PREW = max(3 + TMAX, n_samp * (3 + samp_len))
        pre = [mx.carve([P, PREW], F32, name="pre") for _ in range(2)]
        acc = [mx.carve([P, TMAX], F32, name="acc") for _ in range(2)]
        xc = [mx.carve([P, TMAX], BF16, name="xc") for _ in range(2)]
        xh_tok = mx.carve([P, NBX, DI], BF16, nb=NBX, name="xh_tok")
        B_tok = mx.carve([P, NBX, 512], BF16, nb=NBX, name="B_tok")
        BT = mx.carve([P, 4, TMAX], BF16, name="BT")
        CT = mx.carve([P, 4, TMAX], BF16, name="CT")
        zs = mx.carve([P, NBX, DI], BF16, nb=NBX, name="zs")
        dtw = mx.carve([P, NBX, 32], F32, nb=NBX, name="dtw")
        dtA = mx.carve([P, NBX, 32], BF16, nb=NBX, name="dtA")
        xdt = mx.carve([P, DI], BF16, nb=4, name="xdt")
        xdtw = mx.carve([P, DI], BF16, nb=4, name="xdtw")
        rhs1 = [mx.carve([P, 8, 128], BF16, name="rhs1") for _ in range(2)]
        xd = mx.carve([P, DI], BF16, nb=4, name="xd")
        Eb = [mx.carve([P, 8, 128], BF16, name="E") for _ in range(2)]
        Gm = [mx.carve([P, 8, 128], BF16, name="G") for _ in range(2)]
        cbm = mx.carve([P, 4, 128], BF16, name="cbm")
        t1 = [mx.carve([P, 512], F32, name="t1") for _ in range(1)] * 2
        t2 = [mx.carve([P, 512], F32, name="t2") for _ in range(2)]
        yg = tt("yg", [P, DI], F32, nb=4)
        yn = scrA.junk
        eacs = mx.carve([P, 64], F32, name="eacs")
        hstage_ap = yg.ap.rearrange("p (c n) -> p c n", n=128)
        kvout0 = tt("kvout0", [P, 256], F32)
        M123 = pre + acc + xc + [xh_tok, B_tok, BT, CT, zs, dtw, dtA, xdt, xdtw] + rhs1 + [xd] + Eb + Gm + [cbm, t1[0]] + t2 + [yg, eacs, kvout0]
        mx.off = base
        cs = tt("cs_sb", [P, 2, TMAX], F32)
        qb = [mx.carve([P, TMAX], BF16, name="qb") for _ in range(2)]
        rt1 = [mx.carve([P, TMAX], F32, name="rt1") for _ in range(2)]
        rt2 = [mx.carve([P, TMAX], F32, name="rt2") for _ in range(2)]
        qT = mx.carve([P, 8, TMAX], BF16, nb=8, name="qT")
        PT = [mx.carve([P, 4, 128], BF16, name="PT") for _ in range(4)]
        attn_tok = [mx.carve([P, 1024], BF16, name="attn_tok") for _ in range(2)]
        den = [mx.carve([P, 8], F32, name="den") for _ in range(2)]
        attnT = mx.carve([P, 8, TMAX], BF16, nb=NBX, name="attnT")
        bt = [mx.carve([P, TMAX], F32, name="bt") for _ in range(2)]
        mixedT = mx.carve([P, 8, TMAX], BF16, nb=8, name="mixedT")
        krot_f = mx.carve([P, 4, 128], F32, name="krot_f")
        kvout1 = tt("kvout1", [P, 256], F32)
        gates = [mx.carve([P, 8, TMAX], BF16, nb=8, name="gate%d" % i) for i in range(2)]
        mixf = mx.carve([P, 8, TMAX], F32, nb=8, name="mixf")
        M456 = [cs] + qb + rt1 + rt2 + [qT] + PT + attn_tok + den + [attnT] + bt + [mixedT, krot_f, kvout1] + gates + [mixf]
        cs_key = R.key("cs")
        misc_keys = {}

        def misc_out(name, reads, **kw):
            k = out_key("o_" + name)
            R.op("sp", "dma_start", reads, [], key=k, **kw)

        def switch(frm, to):
            pend = Rec.pending([b for t_ in frm for b in t_.b])
            for t_ in to:
                t_.renew(pend)

        def bc3(ap2, a, b_):
            return ap2.unsqueeze(2).to_broadcast([ap2.shape[0], a, b_])

        def bcm(ap2, a, b_):
            return ap2.unsqueeze(1).to_broadcast([ap2.shape[0], a, b_])

        def lkey(nm):
            return R.key(nm)

        def load_halo(k):
            s = k.seq
            for r in range(3):
                R.op("sp", "dma_start", [], s.halo.b, out=s.halo.ap[:, :, r], in_=st_conv[k.sidx, r].rearrange("(c p) -> p c", p=P),
                     key=lkey("lh%d_%d" % (k.sidx, r)), allow_slow_non_contiguous=True)

        def load_seq_state_ssm(k):
            s = k.seq
            b = k.sidx
            R.op("sp", "dma_start", [], yg.b, out=hstage_ap, in_=st_ssm[b].rearrange("(c p) n -> p c n", p=P), key=lkey("ls%d" % b))
            for c4 in range(4):
                ps = psum.next()
                pv4 = ps.ap.rearrange("p (a b) -> p a b", b=128)
                for cc in range(4):
                    c = c4 * 4 + cc
                    R.op("pe", "transpose", [yg.b[c4]] + par.b, ps.b, pv4[:, cc, :], hstage_ap[:, c, :], identf)
                R.op("act", "activation", ps.b, [s.h.b[c4]], out=s.h.ap[:, c4 * 512:(c4 + 1) * 512], in_=ps.ap, func=AF.Copy)
                R.op("dve", "tensor_scalar", ps.b, [s.hbf.b[c4]], out=s.hbf.ap[:, c4 * 512:(c4 + 1) * 512], in0=ps.ap, scalar1=1.0, scalar2=None, op0=ALU.mult)
                psum.rel(ps)

        def load_seq_state_kv(k):
            b = k.sidx
            ckf = rt1[b % 2]
            cvf = rt2[b % 2]
            assert TMAX >= 256
            R.op("sp", "dma_start", [], ckf.b, out=ckf.ap[:, 0:256], in_=ck[b], key=lkey("lk%d" % b))
            R.op("sp", "dma_start", [], cvf.b, out=cvf.ap[:, 0:256], in_=cv[b], key=lkey("lv%d" % b))
            kd = attn_tok[b % 2]
            kdv = kd.ap[:, 0:512].rearrange("p (j t c) -> p j t c", j=4, t=2)
            for dd in range(2):
                R.op("dve", "tensor_copy", ckf.b, kd.b, out=kdv[:, :, dd, :], in_=ckf.ap[:, 0:256].rearrange("p (j c) -> p j c", c=64))
            ps = psum.next()
            ptv = ps.bf.rearrange("p (a b) -> p a b", b=128)
            for j in range(4):
                R.op("pe", "transpose", kd.b + cbf.b, ps.b, ptv[:, j, :], kd.ap[:, j * 128:(j + 1) * 128], ident)
            R.op("act", "activation", ps.b, cacheT[b].b, out=cacheT[b].ap, in_=ptv[:, 0:4, :], func=AF.Copy)
            psum.rel(ps)
            R.op("act", "activation", cvf.b, cachev[b].b, out=cachev[b].ap[:, :, 0:64], in_=cvf.ap[:, 0:256].rearrange("p (j c) -> p j c", c=64),
                 func=AF.Copy)

        def store_seq_state(k):
            s = k.seq
            if k.is_samp:
                cdst = conv_s[k.sidx]
                sdst = ssm_s[k.sidx]
            else:
                cdst = conv_p
                sdst = ssm_p
            for r in range(3):
                misc_out("conv%d" % r, s.halo.b, out=cdst[r].rearrange("(c p) -> p c", p=P), in_=s.halo.ap[:, :, r], allow_slow_non_contiguous=True)
            for c4 in range(4):
                ps = psum.next()
                pv4 = ps.ap.rearrange("p (a b) -> p a b", b=128)
                for cc in range(4):
                    c = c4 * 4 + cc
                    R.op("pe", "transpose", [s.h.b[c4]] + par.b, ps.b, pv4[:, cc, :], s.h.ap[:, c * 128:(c + 1) * 128], identf)
                R.op("act", "activation", ps.b, [yg.b[c4]], out=hstage_ap[:, c4 * 4:(c4 + 1) * 4, :], in_=pv4, func=AF.Copy)
                psum.rel(ps)
            misc_out("ssm", yg.b, out=sdst.rearrange("(c p) n -> p c n", p=P), in_=hstage_ap)

        def m1_xbc(tl):
            T = tl.T
            nruns = len(tl.runs)
            ln = tl.runs[0][2]
            W3 = 3 + ln
            n = tl.n
            slots = {}

            def views(c):
                pr = pre[c % 2]
                prv = pr.ap[:, 0:nruns * W3].rearrange("p (r w) -> p r w", w=W3)
                ac = acc[c % 2]
                acv = ac.ap[:, 0:T].rearrange("p (r w) -> p r w", w=ln)
                if c < 16:
                    dap, dbufs = xc[c % 2].ap[:, 0:T], xc[c % 2].b
                elif c < 20:
                    dap, dbufs = BT.ap[:, c - 16, 0:T], BT.b
                else:
                    dap, dbufs = CT.ap[:, c - 20, 0:T], CT.b
                return pr, prv, ac, acv, dap, dbufs

            def s1(c):
                u, cc = divmod(c, 4)
                if cc == 0:
                    slots[u] = wsA.acquire("xbc%d" % u)
                slot = slots[u]
                wv = slot.ap.rearrange("p (k cc c) -> p k cc c", k=KC, cc=4)
                pr, prv, ac, acv, dap, dbufs = views(c)
                ps = psum.next()
                for kc in range(KC):
                    R.op("pe", "matmul", slot.b + uT.b[0:tl.nb], ps.b, ps.ap[:, 0:T], lhsT=wv[:, kc, cc, :], rhs=uT.ap[:, kc, 0:T],
                         start=(kc == 0), stop=(kc == KC - 1))
                for r, (s_, col0, l_) in enumerate(tl.runs):
                    R.op("pool", "tensor_copy", s_.halo.b, pr.b, out=prv[:, r, 0:3], in_=s_.halo.ap[:, c, :])
                R.op("act", "activation", ps.b, pr.b, out=prv[:, :, 3:W3], in_=ps.ap[:, 0:T].rearrange("p (r w) -> p r w", w=ln), func=AF.Copy)
                psum.rel(ps)
                for r, (s_, col0, l_) in enumerate(tl.runs):
                    R.op("pool", "tensor_copy", pr.b, s_.halo.b, out=s_.halo.ap[:, c, :], in_=prv[:, r, ln:ln + 3])

            def s2(c):
                pr, prv, ac, acv, dap, dbufs = views(c)
                R.op("dve", "tensor_scalar", pr.b + par.b, ac.b, out=acv, in0=prv[:, :, 0:ln], scalar1=convw[:, c * 4:c * 4 + 1], scalar2=None, op0=ALU.mult)
                for j in range(1, 4):
                    R.op("dve", "scalar_tensor_tensor", pr.b + par.b + ac.b, ac.b, out=acv, in0=prv[:, :, j:j + ln],
                         scalar=convw[:, c * 4 + j:c * 4 + j + 1], in1=acv, op0=ALU.mult, op1=ALU.add)
                R.op("act", "activation", ac.b + par.b, dbufs, out=dap, in_=ac.ap[:, 0:T], func=AF.Silu, bias=convb[:, c:c + 1])

            def s3(c):
                if c >= 20:
                    return
                pr, prv, ac, acv, dap, dbufs = views(c)
                ps2 = psum.next()
                ptv = ps2.bf.rearrange("p (a b) -> p a b", b=128)
                for k in tl.blocks:
                    R.op("pe", "transpose", dbufs + cbf.b, ps2.b, ptv[0:n, k.bi, :], dap[:, k.col0:k.col0 + n], ident)
                if c < 16:
                    R.op("act", "activation", ps2.b, xh_tok.b[0:tl.nb], out=xh_tok.ap[0:n, 0:tl.nb, c * 128:(c + 1) * 128], in_=ptv[0:n, 0:tl.nb, :], func=AF.Copy)
                else:
                    g = c - 16
                    R.op("act", "activation", ps2.b, B_tok.b[0:tl.nb], out=B_tok.ap[0:n, 0:tl.nb, g * 128:(g + 1) * 128], in_=ptv[0:n, 0:tl.nb, :], func=AF.Copy)
                psum.rel(ps2)

            for c in range(24 + 2):
                if c < 24:
                    s1(c)
                if 0 <= c - 1 < 24:
                    s2(c - 1)
                if 0 <= c - 2 < 24:
                    s3(c - 2)

        def m2_dtv_z(tl):
            n = tl.n
            slot = wsA.acquire("dtv")
            wv = slot.ap[:, 0:KC * 288].rearrange("p (k n) -> p k n", k=KC)
            for k in tl.blocks:
                ps = psum.next()
                for kc in range(KC):
                    R.op("pe", "matmul", slot.b + [uT.b[k.bi]], ps.b, ps.ap[0:n, 0:288], lhsT=uT.ap[:, kc, k.col0:k.col0 + n], rhs=wv[:, kc, :],
                         start=(kc == 0), stop=(kc == KC - 1))
                dw = dtw.ap[0:n, k.bi, :]
                if "z1" in DBG:
                    psum.rel(ps)
                    continue
                R.op("dve", "tensor_tensor", ps.b + par.b, [dtw.b[k.bi]], out=dw, in0=ps.ap[0:n, 0:32], in1=dtb[0:n, :], op=ALU.add)
                R.op("act", "activation", [dtw.b[k.bi]], [dtw.b[k.bi]], out=dw, in_=dw, func=AF.Exp)
                R.op("act", "activation", [dtw.b[k.bi]], [dtw.b[k.bi]], out=dw, in_=dw, func=AF.Ln, bias=1.0)
                R.op("dve", "tensor_tensor", [dtw.b[k.bi]] + dconst.b, [dtA.b[k.bi]], out=dtA.ap[0:n, k.bi, :], in0=dw, in1=Aneg[0:n, :], op=ALU.mult)
                if "z2" in DBG:
                    psum.rel(ps)
                    continue
                vb = k.bi + 1
                if "z4" not in DBG:
                    R.op("act", "activation", ps.b, [v_all.b[vb]], out=v_all.ap[0:n, vb, :, 0:64], in_=ps.ap[0:n, 32:288].rearrange("p (j c) -> p j c", c=64),
                         func=AF.Copy)
                if k.last and "z5" not in DBG:
                    ko = kvout0
                    R.op("dve", "tensor_scalar", ps.b, ko.b, out=ko.ap[0:n, :], in0=ps.ap[0:n, 32:288], scalar1=1.0, scalar2=None, op0=ALU.mult)
                    if "z6" in DBG:
                        pass
                    elif k.is_samp:
                        misc_out("v", ko.b, out=v_s[k.sidx * samp_len:(k.sidx + 1) * samp_len, :], in_=ko.ap[0:n, :])
                    else:
                        misc_out("v", ko.b, out=v_p, in_=ko.ap[0:n, :])
                psum.rel(ps)
            for u in range(4):
                slot = wsA.acquire("z%d" % u)
                if "z3" in DBG:
                    continue
                wv = slot.ap.rearrange("p (k n) -> p k n", k=KC)
                for k in tl.blocks:
                    ps = psum.next()
                    for kc in range(KC):
                        R.op("pe", "matmul", slot.b + [uT.b[k.bi]], ps.b, ps.ap[0:n, :], lhsT=uT.ap[:, kc, k.col0:k.col0 + n], rhs=wv[:, kc, :],
                             start=(kc == 0), stop=(kc == KC - 1))
                    R.op("act", "activation", ps.b, [zs.b[k.bi]], out=zs.ap[0:n, k.bi, u * 512:(u + 1) * 512], in_=ps.ap[0:n, :], func=AF.Silu)
                    psum.rel(ps)

        def m3_ssd(tl, k, mid=None):
            L = k.n
            c0 = k.col0
            s = k.seq
            bi = k.bi
            dA = dtA.ap[0:L, bi, :]
            v3_ = lambda ap: ap.rearrange("p (h c) -> p h c", c=64)
            ps_s = psum.next()
            R.op("pe", "matmul", [dtA.b[bi]] + cbf.b, ps_s.b, ps_s.ap[0:L, 0:32], lhsT=tri[0:L, 0:L], rhs=dA, start=True, stop=True)
            R.op("pe", "matmul", [dtA.b[bi]] + cbf.b, ps_s.b, ps_s.ap[:, 32:64], lhsT=ones[0:L, :], rhs=dA, start=True, stop=True)
            R.op("act", "activation", ps_s.b, eacs.b, out=eacs.ap[0:L, 0:32], in_=ps_s.ap[0:L, 0:32], func=AF.Exp)
            R.op("act", "activation", ps_s.b, eacs.b, out=eacs.ap[:, 32:64], in_=ps_s.ap[:, 32:64], func=AF.Exp)
            psum.rel(ps_s)
            decay = eacs.ap[:, 32:64]
            ps_cb = psum.next()
            pcb = ps_cb.ap.rearrange("p (g l) -> p g l", l=128)
            for g in range(4):
                R.op("pe", "matmul", BT.b + CT.b, ps_cb.b, pcb[0:L, g, 0:L], lhsT=BT.ap[:, g, c0:c0 + L], rhs=CT.ap[:, g, c0:c0 + L], start=True, stop=True)
            R.op("dve", "tensor_tensor", ps_cb.b + cbf.b, cbm.b, out=cbm.ap[0:L, :, 0:L], in0=pcb[0:L, :, 0:L], in1=bcm(tri[0:L, 0:L], 4, L), op=ALU.mult)
            psum.rel(ps_cb)

            def stA(g):
                r1 = rhs1[g % 2]
                E = Eb[g % 2]
                gs = slice(g * 512, (g + 1) * 512)
                R.op("pool", "tensor_tensor", [dtA.b[bi]] + cbf.b, r1.b, out=r1.ap[0:L, :, 0:L], in0=bc3(dA[:, 8 * g:8 * g + 8], 8, L),
                     in1=bcm(tri[0:L, 0:L], 8, L), op=ALU.mult)
                R.op("pool", "tensor_tensor", [xh_tok.b[bi], dtw.b[bi]], [xdt.b[g]], out=v3_(xdt.ap[0:L, gs]),
                     in0=v3_(xh_tok.ap[0:L, bi, gs]), in1=bc3(dtw.ap[0:L, bi, 8 * g:8 * g + 8], 8, 64), op=ALU.mult)
                R.op("dve", "tensor_tensor", [xh_tok.b[bi]] + par.b, [xd.b[g]], out=v3_(xd.ap[0:L, gs]),
                     in0=v3_(xh_tok.ap[0:L, bi, gs]), in1=bc3(dskip[0:L, 8 * g:8 * g + 8], 8, 64), op=ALU.mult)
                for hh in range(2):
                    ps = psum.next()
                    pv_ = ps.ap.rearrange("p (h l) -> p h l", l=128)[0:L, :, 0:L]
                    h0 = 8 * g + 4 * hh
                    if L == 128:
                        R.op("pe", "matmul", r1.b + cbf.b, ps.b, ps.ap[:, :], lhsT=ones[0:L, 0:L], rhs=r1.ap[0:L, 4 * hh:4 * hh + 4, :].rearrange("p h l -> p (h l)"),
                             start=True, stop=False)
                        R.op("pe", "matmul", [dtA.b[bi]] + cbf.b, ps.b, ps.ap[:, :], lhsT=m2[0:L, 0:L], rhs=bc3(dA[:, h0:h0 + 4], 4, L),
                             start=False, stop=False)
                        R.op("pe", "matmul", cbf.b, ps.b, ps.ap[:, :], lhsT=ident[0:L, 0:L], rhs=negm[0:L, 0:4, :].rearrange("p h l -> p (h l)"), start=False, stop=True)
                    else:
                        for h4 in range(4):
                            o2 = pv_[:, h4, :]
                            R.op("pe", "matmul", r1.b + cbf.b, ps.b, o2, lhsT=ones[0:L, 0:L], rhs=r1.ap[0:L, 4 * hh + h4, 0:L], start=True, stop=False)
                            R.op("pe", "matmul", [dtA.b[bi]] + cbf.b, ps.b, o2, lhsT=m2[0:L, 0:L], rhs=dA[:, h0 + h4:h0 + h4 + 1].to_broadcast([L, L]),
                                 start=False, stop=False)
                            R.op("pe", "matmul", cbf.b, ps.b, o2, lhsT=ident[0:L, 0:L], rhs=negm[0:L, h4, 0:L], start=False, stop=True)
                    R.op("act", "activation", ps.b, E.b, out=E.ap[0:L, 4 * hh:4 * hh + 4, 0:L], in_=pv_, func=AF.Exp)
                    psum.rel(ps)
                R.op("pool", "tensor_tensor", [xdt.b[g]] + E.b, [xdtw.b[g]], out=v3_(xdtw.ap[0:L, gs]), in0=v3_(xdt.ap[0:L, gs]),
                     in1=E.ap[0:L, :, L - 1:L].to_broadcast([L, 8, 64]), op=ALU.mult)

            def stB(g):
                E = Eb[g % 2]
                G_ = Gm[g % 2]
                R.op("dve", "tensor_tensor", E.b + cbm.b, G_.b, out=G_.ap[0:L, :, 0:L], in0=E.ap[0:L, :, 0:L], in1=bcm(cbm.ap[0:L, g, 0:L], 8, L), op=ALU.mult)
                y_ps = psum.next()
                R.op("pe", "matmul", [xd.b[g]] + cbf.b, y_ps.b, y_ps.ap[0:L, :], lhsT=ident[0:L, 0:L], rhs=xd.ap[0:L, g * 512:(g + 1) * 512], start=True, stop=False)
                for h in range(8):
                    hh_ = 8 * g + h
                    R.op("pe", "matmul", G_.b + [xdt.b[g]], y_ps.b, y_ps.ap[0:L, h * 64:(h + 1) * 64], lhsT=G_.ap[0:L, h, 0:L], rhs=xdt.ap[0:L, hh_ * 64:(hh_ + 1) * 64],
                         start=False, stop=(h == 7))
                yi_ps = psum.next()
                R.op("pe", "matmul", CT.b + [s.hbf.b[g]], yi_ps.b, yi_ps.ap[0:L, :], lhsT=CT.ap[:, g, c0:c0 + L], rhs=s.hbf.ap[:, g * 512:(g + 1) * 512],
                     start=True, stop=True)
                a1 = t1[0]
                a2 = t2[g % 2]
                R.op("dve", "tensor_tensor", yi_ps.b + eacs.b, a1.b, out=v3_(a1.ap[0:L, :]), in0=v3_(yi_ps.ap[0:L, :]), in1=bc3(eacs.ap[0:L, 8 * g:8 * g + 8], 8, 64),
                     op=ALU.mult)
                R.op("dve", "tensor_tensor", y_ps.b + a1.b, a2.b, out=a2.ap[0:L, :], in0=y_ps.ap[0:L, :], in1=a1.ap[0:L, :], op=ALU.add)
                psum.rel(y_ps, yi_ps)
                R.op("pool", "tensor_tensor", a2.b + [zs.b[bi]], [yg.b[g]], out=yg.ap[0:L, g * 512:(g + 1) * 512], in0=a2.ap[0:L, :],
                     in1=zs.ap[0:L, bi, g * 512:(g + 1) * 512], op=ALU.mult)

            def stC(g):
                S_ps = psum.next()
                R.op("pe", "matmul", [B_tok.b[bi], xdtw.b[g]], S_ps.b, S_ps.ap[:, :], lhsT=B_tok.ap[0:L, bi, g * 128:(g + 1) * 128],
                     rhs=xdtw.ap[0:L, g * 512:(g + 1) * 512], start=True, stop=True)
                hsl = s.h.ap[:, g * 512:(g + 1) * 512]
                R.op("dve", "tensor_tensor", [s.h.b[g]] + eacs.b, [s.h.b[g]], out=v3_(hsl), in0=v3_(hsl), in1=bc3(decay[:, 8 * g:8 * g + 8], 8, 64), op=ALU.mult)
                R.op("dve", "tensor_tensor", [s.h.b[g]] + S_ps.b, [s.h.b[g]], out=hsl, in0=S_ps.ap[:, :], in1=hsl, op=ALU.add)
                psum.rel(S_ps)
                R.op("act", "activation", [s.h.b[g]], [s.hbf.b[g]], out=s.hbf.ap[:, g * 512:(g + 1) * 512], in_=hsl, func=AF.Copy)

            stA(0)
            if mid is not None:
                mid()
            for g in range(4):
                if g + 1 < 4:
                    stA(g + 1)
                stB(g)
                stC(g)

        def m3_norm(tl, k):
            L = k.n
            c0 = k.col0
            bi = k.bi
            st = scrA.new_stat()
            R.op("act", "activation", yg.b, scrA.junk.b + st.b, out=scrA.junk.ap[0:L, 0:DI], in_=yg.ap[0:L, :], func=AF.Square, accum_out=st.ap[0:L, 0:1])
            rstd_from(st.ap[0:L, 0:1], L, st, 1, DI, 1.0)
            R.op("act", "activation", yg.b + st.b, yn.b, out=yn.ap[0:L, :], in_=yg.ap[0:L, :], func=AF.Copy, scale=st.ap[0:L, 1:2])
            for i in range(2):
                ps = psum.next()
                ptv = ps.bf.rearrange("p (a b) -> p a b", b=128)
                for cc in range(8):
                    c = 8 * i + cc
                    R.op("pe", "transpose", yn.b + cbf.b, ps.b, ptv[:, cc, 0:L], yn.ap[0:L, c * 128:(c + 1) * 128], ident[0:L, 0:L])
                R.op("dve", "tensor_tensor", ps.b + par.b, [ymT.b[bi]], out=ymT.ap[:, 8 * i:8 * i + 8, c0:c0 + L], in0=ptv[:, :, 0:L],
                     in1=bc3(gmn[:, 8 * i:8 * i + 8], 8, L), op=ALU.mult)
                psum.rel(ps)

        def rope(ps, T, i, dst_ap, dst_bufs, kout_j=None):
            q_b = qb[i % 2]
            a1 = rt1[i % 2]
            a2 = rt2[i % 2]
            R.op("act", "activation", ps.b, q_b.b, out=q_b.ap[:, 0:T], in_=ps.ap[:, 0:T], func=AF.Copy)
            sw = psum.next()
            R.op("pe", "matmul", q_b.b + cbf.b, sw.b, sw.ap[:, 0:T], lhsT=rm, rhs=q_b.ap[:, 0:T], start=True, stop=True)
            R.op("dve", "tensor_tensor", ps.b + cs.b, a1.b, out=a1.ap[:, 0:T], in0=ps.ap[:, 0:T], in1=cs.ap[:, 0, 0:T], op=ALU.mult)
            R.op("dve", "tensor_tensor", sw.b + cs.b, a2.b, out=a2.ap[:, 0:T], in0=sw.ap[:, 0:T], in1=cs.ap[:, 1, 0:T], op=ALU.mult)
            psum.rel(sw)
            return a1, a2

        def m4_qkv(tl):
            T = tl.T
            n = tl.n
            R.op("sp", "dma_start", [], cs.b, out=cs.ap[:, :, 0:T], in_=cs_d[:, :, tl.cscol:tl.cscol + T], key=cs_key)
            i = 0
            for u in range(2):
                slot = wsA.acquire("q%d" % u)
                wv = slot.ap.rearrange("p (k cc c) -> p k cc c", k=KC, cc=4)
                for cc in range(4):
                    c = 4 * u + cc
                    ps = psum.next()
                    for kc in range(KC):
                        R.op("pe", "matmul", slot.b + uT.b[0:tl.nb], ps.b, ps.ap[:, 0:T], lhsT=wv[:, kc, cc, :], rhs=uT.ap[:, kc, 0:T],
                             start=(kc == 0), stop=(kc == KC - 1))
                    a1, a2 = rope(ps, T, i, None, None)
                    psum.rel(ps)
                    R.op("pool", "tensor_tensor", a1.b + a2.b, [qT.b[c]], out=qT.ap[:, c, 0:T], in0=a1.ap[:, 0:T], in1=a2.ap[:, 0:T], op=ALU.add)
                    i += 1
            slot = wsA.acquire("kdup")
            wv = slot.ap.rearrange("p (k j c) -> p k j c", k=KC, j=4)
            for j in range(4):
                ps = psum.next()
                for kc in range(KC):
                    R.op("pe", "matmul", slot.b + uT.b[0:tl.nb], ps.b, ps.ap[:, 0:T], lhsT=wv[:, kc, j, :], rhs=uT.ap[:, kc, 0:T],
                         start=(kc == 0), stop=(kc == KC - 1))
                a1, a2 = rope(ps, T, i, None, None)
                psum.rel(ps)
                kview = kT_all.ap[:, j, 128:128 + tl.nb * 128].rearrange("p (b w) -> p b w", w=128)[:, :, 0:n]
                R.op("pool", "tensor_tensor", a1.b + a2.b, kT_all.b[1:1 + tl.nb], out=kview, in0=a1.ap[:, 0:T].rearrange("p (b w) -> p b w", w=n),
                     in1=a2.ap[:, 0:T].rearrange("p (b w) -> p b w", w=n), op=ALU.add)
                if tl.blocks[-1].last:
                    if tl.is_samp:
                        lo, wd = 0, T
                    else:
                        lo, wd = T - 128, 128
                    R.op("dve", "tensor_tensor", a1.b + a2.b, krot_f.b, out=krot_f.ap[:, j, 0:wd], in0=a1.ap[:, lo:lo + wd], in1=a2.ap[:, lo:lo + wd], op=ALU.add)
                i += 1
            if tl.blocks[-1].last:
                wd = T if tl.is_samp else 128
                ps = psum.next()
                pv4 = ps.ap[:, 0:256].rearrange("p (j c) -> p j c", c=64)
                for j in range(4):
                    R.op("pe", "transpose", krot_f.b + par.b, ps.b, pv4[0:wd, j, :], krot_f.ap[0:64, j, 0:wd], identf[0:64, 0:64])
                ko = kvout1
                R.op("act", "activation", ps.b, ko.b, out=ko.ap[0:wd, :], in_=ps.ap[0:wd, 0:256], func=AF.Copy)
                psum.rel(ps)
                misc_out("k", ko.b, out=(k_s if tl.is_samp else k_p), in_=ko.ap[0:wd, :])

        def m4_attn(tl, k):
            n = k.n
            c0 = k.col0
            bi = k.bi
            keysets = []
            if k.is_samp:
                keysets.append((cacheT[k.sidx].ap, cacheT[k.sidx].b, cachev[k.sidx].ap, cachev[k.sidx].b, 128, None, None))
                ownb = (None, None)
            else:
                if not k.first:
                    keysets.append((kT_all.ap[:, :, bi * 128:(bi + 1) * 128], [kT_all.b[bi]], v_all.ap[:, bi, :, :], [v_all.b[bi]], 128, None, 0))
                ownb = (1, None)
            keysets.append((kT_all.ap[:, :, (bi + 1) * 128:(bi + 2) * 128], [kT_all.b[bi + 1]], v_all.ap[:, bi + 1, :, :], [v_all.b[bi + 1]], n, ownb[0], ownb[1]))
            nks = len(keysets)
            at = attn_tok[bi % 2]

            def scores(j):
                sbk = [psum.next(), psum.next()]
                svs = [b_.ap.rearrange("p (s q) -> p s q", q=128) for b_ in sbk]
                ptb = [PT[2 * (j % 2)], PT[2 * (j % 2) + 1]]
                for ks, ke in enumerate(keysets):
                    kap, kbufs, nk = ke[0], ke[1], ke[4]
                    for hl in range(4):
                        hf = hl % 2
                        hp = hl // 2
                        ch = (4 * j + hl) // 2
                        R.op("pe", "matmul", kbufs + [qT.b[ch]], sbk[hf].b, svs[hf][0:nk, ks * 2 + hp, 0:n], lhsT=kap[hf * 64:(hf + 1) * 64, j, 0:nk],
                             rhs=qT.ap[hf * 64:(hf + 1) * 64, ch, c0:c0 + n], start=True, stop=True)
                for hf in range(2):
                    pt = ptb[hf]
                    for ks, ke in enumerate(keysets):
                        nk, b0, b1 = ke[4], ke[5], ke[6]
                        sl = slice(2 * ks, 2 * ks + 2)
                        if b0 is None and b1 is None:
                            R.op("act", "activation", sbk[hf].b, pt.b, out=pt.ap[0:nk, sl, 0:n], in_=svs[hf][0:nk, sl, 0:n], func=AF.Exp, scale=0.125)
                        else:
                            for q0, bcol in ((0, b0), (64, b1)):
                                kw = {} if bcol is None else {"bias": negcol[0:nk, bcol:bcol + 1]}
                                R.op("act", "activation", sbk[hf].b + par.b, pt.b, out=pt.ap[0:nk, sl, q0:q0 + 64], in_=svs[hf][0:nk, sl, q0:q0 + 64],
                                     func=AF.Exp, scale=0.125, **kw)
                psum.rel(*sbk)
                return ptb

            def pv_out(j, ptb):
                o_ps = psum.next()
                ov = o_ps.ap[:, 0:260].rearrange("p (h c) -> p h c", c=65)
                for hl in range(4):
                    hf = hl % 2
                    hp = hl // 2
                    for ks, ke in enumerate(keysets):
                        vap, vbufs, nk = ke[2], ke[3], ke[4]
                        R.op("pe", "matmul", ptb[hf].b + vbufs, o_ps.b, ov[0:n, hl, :], lhsT=ptb[hf].ap[0:nk, ks * 2 + hp, 0:n], rhs=vap[0:nk, j, 0:65],
                             start=(ks == 0), stop=(ks == nks - 1))
                dn = den[j % 2]
                R.op("dve", "tensor_tensor", o_ps.b + dconst.b, dn.b, out=dn.ap[0:n, 0:4], in0=ov[0:n, :, 64], in1=esink[0:n, 4 * j:4 * j + 4], op=ALU.add)
                R.op("dve", "reciprocal", dn.b, dn.b, out=dn.ap[0:n, 4:8], in_=dn.ap[0:n, 0:4])
                R.op("dve", "tensor_tensor", o_ps.b + dn.b, at.b, out=at.ap[0:n, j * 256:(j + 1) * 256].rearrange("p (h c) -> p h c", c=64),
                     in0=ov[0:n, :, 0:64], in1=bc3(dn.ap[0:n, 4:8], 4, 64), op=ALU.mult)
                psum.rel(o_ps)

            pts = scores(0)
            for j in range(4):
                nxt = scores(j + 1) if j + 1 < 4 else None
                pv_out(j, pts)
                pts = nxt
            ps = psum.next()
            ptv = ps.bf.rearrange("p (a b) -> p a b", b=128)
            for c in range(8):
                R.op("pe", "transpose", at.b + cbf.b, ps.b, ptv[:, c, 0:n], at.ap[0:n, c * 128:(c + 1) * 128], ident[0:n, 0:n])
            R.op("act", "activation", ps.b, [attnT.b[bi]], out=attnT.ap[:, :, c0:c0 + n], in_=ptv[:, :, 0:n], func=AF.Copy)
            psum.rel(ps)

        def m4_save_prev(tl):
            if tl.is_samp:
                return
            R.op("act", "activation", [kT_all.b[NB]], [kT_all.b[0]], out=kT_all.ap[:, :, 0:128], in_=kT_all.ap[:, :, NB * 128:(NB + 1) * 128], func=AF.Copy)
            R.op("pool", "tensor_copy", [v_all.b[NB]], [v_all.b[0]], out=v_all.ap[:, 0, :, :], in_=v_all.ap[:, NB, :, :])

        def m5_branches(tl, xb):
            T = tl.T
            for u in range(4):
                nm = ("gm%d" if u % 2 == 0 else "ga%d") % (u // 2)
                slot = wsA.acquire(nm)
                wv = slot.ap.rearrange("p (k cc c) -> p k cc c", k=KC, cc=4)
                for cc in range(4):
                    fo = 4 * (u // 2) + cc
                    ps = psum.next()
                    for kc in range(KC):
                        R.op("pe", "matmul", slot.b + uT.b[0:tl.nb], ps.b, ps.ap[:, 0:T], lhsT=wv[:, kc, cc, :], rhs=uT.ap[:, kc, 0:T],
                             start=(kc == 0), stop=(kc == KC - 1))
                    gsb = gates[(u % 2)]
                    R.op("act", "activation", ps.b, [gsb.b[fo]], out=gsb.ap[:, fo, 0:T], in_=ps.ap[:, 0:T], func=AF.Sigmoid)
                    psum.rel(ps)
            for u in range(4):
                slot = wsA.acquire("brm%d" % u)
                wv = slot.ap.rearrange("p (k cc c) -> p k cc c", k=16, cc=2)
                for cc in range(2):
                    fo = 2 * u + cc
                    ps = psum.next()
                    for kc in range(16):
                        R.op("pe", "matmul", slot.b + ymT.b[0:tl.nb], ps.b, ps.ap[:, 0:T], lhsT=wv[:, kc, cc, :], rhs=ymT.ap[:, kc, 0:T],
                             start=(kc == 0), stop=(kc == 15))
                    R.op("dve", "tensor_tensor", ps.b + [gates[0].b[fo]], [mixf.b[fo]], out=mixf.ap[:, fo, 0:T], in0=ps.ap[:, 0:T], in1=gates[0].ap[:, fo, 0:T],
                         op=ALU.mult)
                    psum.rel(ps)
            for u in range(2):
                slot = wsA.acquire("bra%d" % u)
                wv = slot.ap.rearrange("p (k cc c) -> p k cc c", k=KC, cc=4)
                for cc in range(4):
                    fo = 4 * u + cc
                    ps = psum.next()
                    for kc in range(KC):
                        R.op("pe", "matmul", slot.b + attnT.b[0:tl.nb], ps.b, ps.ap[:, 0:T], lhsT=wv[:, kc, cc, :], rhs=attnT.ap[:, kc, 0:T],
                             start=(kc == 0), stop=(kc == KC - 1))
                    b_ = bt[fo % 2]
                    R.op("dve", "tensor_tensor", ps.b + [gates[1].b[fo]], b_.b, out=b_.ap[:, 0:T], in0=ps.ap[:, 0:T], in1=gates[1].ap[:, fo, 0:T], op=ALU.mult)
                    psum.rel(ps)
                    R.op("pool", "tensor_tensor", b_.b + [mixf.b[fo]], [mixedT.b[fo]], out=mixedT.ap[:, fo, 0:T], in0=b_.ap[:, 0:T], in1=mixf.ap[:, fo, 0:T],
                         op=ALU.add)
            accs = [[psum.next() for _ in tl.blocks] for _ in range(2)]
            for half in range(2):
                slot = wsA.acquire("wo%d" % half)
                wv = slot.ap.rearrange("p (k n) -> p k n", k=KC)
                for kc in range(KC):
                    for k in tl.blocks:
                        R.op("pe", "matmul", slot.b + [mixedT.b[kc]], accs[half][k.bi].b, accs[half][k.bi].ap[0:k.n, :],
                             lhsT=mixedT.ap[:, kc, k.col0:k.col0 + k.n], rhs=wv[:, kc, :], start=(kc == 0), stop=(kc == KC - 1))
            for k in tl.blocks:
                a2 = [accs[0][k.bi], accs[1][k.bi]]
                post_norm_add(xb, k, a2, gpost["m"], 1.0, scrA)
                psum.rel(*a2)


        def mixer(tl, xb):
            switch(M456, M123)
            norm_T(tl, xb, gpre["m"], uT, scrA)
            for k in tl.blocks:
                if k.is_samp and "st" in DBG:
                    load_halo(k)
            if "m1" in DBG:
                m1_xbc(tl)
            if "m2" in DBG:
                m2_dtv_z(tl)
            prevk = [None]
            for k in tl.blocks:
                if k.is_samp and "st" in DBG:
                    load_seq_state_ssm(k)

                def mid(pk=prevk[0]):
                    if pk is not None:
                        m3_norm(tl, pk)
                if k.is_samp or k.last:
                    mid()
                    m3_ssd(tl, k)
                    m3_norm(tl, k)
                    prevk[0] = None
                else:
                    m3_ssd(tl, k, mid)
                    prevk[0] = k
                if k.last and "st" in DBG:
                    store_seq_state(k)
            if prevk[0] is not None:
                m3_norm(tl, prevk[0])
            switch(M123, M456)
            for k in tl.blocks:
                if k.is_samp and "st" in DBG:
                    load_seq_state_kv(k)
            if "m4" in DBG:
                m4_qkv(tl)
            if "m4a" in DBG:
                for k in tl.blocks:
                    m4_attn(tl, k)
                m4_save_prev(tl)
            if "m5" in DBG:
                m5_branches(tl, xb)

    NT = len(tiles)
    for t in range(NT):
        wsA.seq += mix_unit_seq() if do_mix else []
    wsB.seq += ffn_unit_seq(0)
    for t in range(NT):
        if t > 0:
            wsB.seq += ffn_unit_seq(1)
        if t + 1 < NT:
            wsB.seq += ffn_unit_seq(0)
    wsB.seq += ffn_unit_seq(1)
    sched = Sched(KA, KB)
    load_x(0)
    ffn(tiles[0], x_bufs[0], 0)
    if NT > 1:
        load_x(1)
    for t in range(NT):
        tl = tiles[t]

        def fa(t=t, tl=tl):
            if do_mix:
                mixer(tl, x_bufs[t % 2])

        def fb(t=t):
            if t > 0:
                ffn(tiles[t - 1], x_bufs[(t - 1) % 2], 1)
                store_x(t - 1)
                if t + 1 < NT:
                    load_x(t + 1)
            if t + 1 < NT:
                ffn(tiles[t + 1], x_bufs[(t + 1) % 2], 0)

        if INTERLEAVE:
            R.sched = sched
            sched.run(fa, fb)
            R.sched = None
        else:
            fa()
            fb()
    ffn(tiles[NT - 1], x_bufs[(NT - 1) % 2], 1)
    store_x(NT - 1)
    R.emit()
    return nc


MX_BYTES = {2: 68352, 4: 150 * 1024}

def _layout(entries):
    d = {}
    o = 0
    for name, n in entries:
        d[name] = (o, n)
        o += n
    d["_n"] = o
    return d


PAR = _layout([("gpre1", 8), ("gpre2", 8), ("gmix", 8), ("gmn", 16), ("convw", 96), ("convb", 24),
               ("gpost1", 1024), ("gpost2", 1024), ("gpostm", 1024), ("dtb", 32), ("alog", 32), ("dskip", 32),
               ("sink", 16), ("identf", 128), ("negcol", 2)])
CB = _layout([("ident", 128), ("tri", 128), ("m2", 128), ("ones", 128), ("rm", 128), ("negm", 1024), ("maskA", 512), ("maskB", 512)])


def pack_params(inp):
    a = np.zeros((P, PAR["_n"]), np.float32)

    def put(name, v):
        o, n = PAR[name]
        a[:, o:o + n] = v

    def chunked(v):
        return np.asarray(v, np.float32).reshape(-1, P).T

    def bc(v):
        return np.broadcast_to(np.asarray(v, np.float32).reshape(1, -1), (P, np.asarray(v).size))

    put("gpre1", chunked(inp["ffn1_pre_g"][0]))
    put("gpre2", chunked(inp["ffn2_pre_g"][0]))
    put("gmix", chunked(inp["mix_pre_g"][0]))
    put("gmn", chunked(inp["m_norm_g"][0]))
    cw = np.asarray(inp["conv_w"][0], np.float32)
    put("convw", cw.reshape(4, 24, P).transpose(2, 1, 0).reshape(P, 96))
    put("convb", chunked(inp["conv_b"][0]))
    put("gpost1", bc(inp["ffn1_post_g"][0]))
    put("gpost2", bc(inp["ffn2_post_g"][0]))
    put("gpostm", bc(inp["mix_post_g"][0]))
    put("dtb", bc(inp["dt_bias"][0]))
    put("alog", bc(inp["a_log"][0]))
    put("dskip", bc(inp["d_skip"][0]))
    put("sink", bc(inp["attn_sink"][0]))
    put("identf", np.eye(P, dtype=np.float32))
    ncol = np.zeros((P, 2), np.float32)
    ncol[:64, 0] = -30000.0
    ncol[64:, 1] = -30000.0
    put("negcol", ncol)
    return a


def pack_consts():
    a = np.zeros((P, CB["_n"]), np.float32)

    def put(name, v):
        o, n = CB[name]
        a[:, o:o + n] = v.reshape(P, n)

    k = np.arange(P)[:, None]
    l = np.arange(P)[None, :]
    put("ident", np.eye(P))
    put("tri", (k <= l).astype(np.float32))
    put("m2", -(k <= l).astype(np.float32))
    put("ones", np.ones((P, P)))
    rmm = np.zeros((P, P), np.float32)
    for hb in (0, 64):
        for d2 in range(8):
            rmm[hb + d2 + 8, hb + d2] = 1.0
            rmm[hb + d2, hb + d2 + 8] = 1.0
    put("rm", rmm)
    ng = np.where(l < k, -30000.0, 0.0).astype(np.float32)
    put("negm", np.broadcast_to(ng[:, None, :], (P, 8, P)))
    mA = np.ones((P, P), np.float32)
    mA[:64, 64:] = 0.0
    mB = np.ones((P, P), np.float32)
    mB[64:, :64] = 0.0
    put("maskA", np.broadcast_to(mA[:, None, :], (P, 4, P)))
    put("maskB", np.broadcast_to(mB[:, None, :], (P, 4, P)))
    return a.astype(ml_dtypes.bfloat16)


def rope_tables(n_prompt, samp_len):
    pos = np.concatenate([np.arange(n_prompt), PAST_LEN + np.arange(samp_len)]).astype(np.float32)
    inv = (np.float32(THETA) ** (-np.arange(0, ROT, 2, dtype=np.float32) / np.float32(ROT))).astype(np.float32)
    ang = (pos[:, None] * inv[None, :]).astype(np.float32)
    c = np.cos(ang).astype(np.float32).T
    s = np.sin(ang).astype(np.float32).T
    cos_t = np.ones((P, pos.size), np.float32)
    sin_t = np.zeros((P, pos.size), np.float32)
    for hb in (0, 64):
        cos_t[hb:hb + 8] = c
        cos_t[hb + 8:hb + 16] = c
        sin_t[hb:hb + 8] = -s
        sin_t[hb + 8:hb + 16] = s
    return cos_t, sin_t


N_CORES = 8
NB_DEFAULT = 2
_cache = {}


def make_in_maps(inp, n_prompt, n_cores):
    f = lambda a: np.ascontiguousarray(np.asarray(a, dtype=np.float32))
    params = pack_params(inp)
    cbf = pack_consts()
    cos_t, sin_t = rope_tables(n_prompt, 32)
    cs = np.stack([np.concatenate([cos_t, cos_t[:, n_prompt:]], axis=1), np.concatenate([sin_t, sin_t[:, n_prompt:]], axis=1)], axis=1)
    cs = np.ascontiguousarray(cs, dtype=np.float32)
    shared = {
        "w_gu1": f(inp["ffn1_w_gu"][0]), "w_gu2": f(inp["ffn2_w_gu"][0]),
        "w_dn1": f(inp["ffn1_w_down"][0]), "w_dn2": f(inp["ffn2_w_down"][0]),
        "w_in": f(inp["w_in"][0]), "w_brm": f(inp["w_br_m"][0]), "w_bra": f(inp["w_br_a"][0]), "w_o": f(inp["w_o"][0]),
        "params": params, "cbf": cbf, "cs_t": cs,
    }
    maps = []
    for c in range(n_cores):
        m = dict(shared)
        m["xp"] = f(inp["x_prompt"][c])
        sl = slice(2 * c, 2 * c + 2)
        m["xs"] = f(inp["x_sample"][sl]).reshape(64, D)
        m["st_conv"] = f(inp["state_conv"][0][sl])
        m["st_ssm"] = f(inp["state_ssm"][0][sl]).reshape(2, DI, NS)
        m["ck"] = f(inp["cache_k"][0][sl]).reshape(2, 128, 256)
        m["cv"] = f(inp["cache_v"][0][sl]).reshape(2, 128, 256)
        maps.append(m)
    return maps


def gather(results, n_prompt, n_cores):
    g = lambda k: [np.asarray(r[k], dtype=np.float32) for r in results]
    rows = min(128, n_prompt)
    yp = np.stack(g("yp")).reshape(n_cores, n_prompt, D)
    ys = np.concatenate([a.reshape(2, 32, D) for a in g("ys")], 0)
    conv_p = np.stack(g("conv_p")).reshape(1, n_cores, 3, CONV)
    ssm_p = np.stack(g("ssm_p")).reshape(1, n_cores, NH, HP, NS)
    k_p = np.stack(g("k_p")).reshape(1, n_cores, rows, 4, 64)
    v_p = np.stack(g("v_p")).reshape(1, n_cores, rows, 4, 64)
    conv_s = np.concatenate(g("conv_s"), 0).reshape(1, 2 * n_cores, 3, CONV)
    ssm_s = np.concatenate(g("ssm_s"), 0).reshape(1, 2 * n_cores, NH, HP, NS)
    k_s = np.concatenate([a.reshape(2, 32, 4, 64) for a in g("k_s")], 0).reshape(1, 2 * n_cores, 32, 4, 64)
    v_s = np.concatenate([a.reshape(2, 32, 4, 64) for a in g("v_s")], 0).reshape(1, 2 * n_cores, 32, 4, 64)
    return (yp, ys, conv_p, ssm_p, k_p, v_p, conv_s, ssm_s, k_s, v_s)


def kernel(**inputs):
    n_prompt = int(np.asarray(inputs["x_prompt"]).shape[1])
    n_cores = int(np.asarray(inputs["x_prompt"]).shape[0])
    nc = build(n_prompt, NB_DEFAULT)
    maps = make_in_maps(inputs, n_prompt, n_cores)
    res = run_bass_kernel_spmd(nc, maps, core_ids=list(range(n_cores)))
    return gather(res.results, n_prompt, n_cores)
```

```python
import math
import threading
import numpy as np
import ml_dtypes
import concourse.bass as bass
import concourse.mybir as mybir
from concourse.bass_utils import run_bass_kernel_spmd

F32 = mybir.dt.float32
BF16 = mybir.dt.bfloat16
AF = mybir.ActivationFunctionType
ALU = mybir.AluOpType
AX = mybir.AxisListType
P = 128

D = 1024
DFF = 2816
DI = 2048
CONV = 3072
NH = 32
HP = 64
NS = 128
NG = 4
KC = 8
FC = 22
DINP = 8736
EPS = 1e-6
PAST_LEN = 1024
ROT = 16
THETA = 500000.0
OFF_Z, OFF_XBC, OFF_DT, OFF_Q, OFF_K, OFF_V, OFF_GM, OFF_GA = 0, 2048, 5120, 5152, 6176, 6432, 6688, 7712

ENGS = ("pe", "act", "dve", "pool", "sp")


class Buf:
    __slots__ = ("name", "last_w", "readers", "excl")

    def __init__(self, name="", pending=None):
        self.name = name
        self.excl = False
        self.last_w = None
        self.readers = list(pending) if pending else []


class Op:
    __slots__ = ("eng", "fn", "deps", "sig", "count", "key", "ord", "is_dma", "idx")


class DmaKey:
    def __init__(self, sem, grouped=False):
        self.sem = sem
        self.n = 0
        self.grouped = grouped


class Rec:
    def __init__(self, nc):
        self.nc = nc
        self.streams = {e: [] for e in ENGS}
        self.sems = {e: nc.alloc_semaphore("sem_" + e) for e in ENGS}
        self.final_keys = []
        self.sched = None

    def key(self, name, grouped=False):
        self.nk = getattr(self, "nk", 0) + 1
        return DmaKey(self.nc.alloc_semaphore("dk%d_%s" % (self.nk, name)), grouped)

    def op(self, eng, method, reads, writes, *args, key=None, **kwargs):
        o = Op()
        o.eng = eng
        o.fn = (method, args, kwargs)
        o.key = key
        o.is_dma = key is not None
        o.sig = False
        o.count = None
        o.idx = len(self.streams[eng])
        if key is not None:
            key.n += 1
            o.ord = key.n
        best = {}
        dmas = []
        ex = [b for b in reads if b.excl]
        if ex:
            writes = list(writes) + [b for b in ex if b not in writes]

        dk = {}

        def add(d, kind):
            if d is o:
                return
            if d.is_dma:
                c = dk.get(id(d.key))
                if c is None or d.ord > c.ord:
                    dk[id(d.key)] = d
                return
            if d.eng == eng and not o.is_dma:
                if eng == "pe" or kind != "raw":
                    return
            c = best.get(d.eng)
            if c is None or d.idx > c.idx:
                best[d.eng] = d

        for b in reads:
            if b.last_w is not None:
                add(b.last_w, "raw")
        for b in writes:
            if b.last_w is not None:
                add(b.last_w, "waw")
            for r in b.readers:
                add(r, "war")
        o.deps = list(best.values()) + list(dk.values())
        for d in best.values():
            d.sig = True
        for b in reads:
            b.readers.append(o)
        for b in writes:
            b.last_w = o
            b.readers = []
        self.streams[eng].append(o)
        if self.sched is not None:
            self.sched.tick(o)
        return o

    @staticmethod
    def pending(bufs):
        best = {}
        dk = {}
        for b in bufs:
            for d in ([b.last_w] if b.last_w is not None else []) + b.readers:
                if d.is_dma:
                    c = dk.get(id(d.key))
                    if c is None or d.ord > c.ord:
                        dk[id(d.key)] = d
                else:
                    c = best.get(d.eng)
                    if c is None or d.idx > c.idx:
                        best[d.eng] = d
        return list(dk.values()) + list(best.values())

    def emit(self):
        nc = self.nc
        for e in ENGS:
            c = 0
            for o in self.streams[e]:
                if o.sig and not o.is_dma:
                    c += 1
                    o.count = c
        sems = self.sems

        def runner(en):
            def f(e):
                waited = {}
                for o in self.streams[en]:
                    need = {}
                    for d in o.deps:
                        if d.is_dma:
                            sem = d.key.sem
                            val = 16 * (d.key.n if d.key.grouped else d.ord)
                        else:
                            sem = sems[d.eng]
                            val = d.count
                        k = id(sem)
                        if k not in need or need[k][1] < val:
                            need[k] = (sem, val)
                    items = [(s, v) for k, (s, v) in need.items() if waited.get(k, 0) < v]
                    for k, (s, v) in need.items():
                        if waited.get(k, 0) < v:
                            waited[k] = v
                    attach = None
                    if items and not o.is_dma:
                        attach = items.pop()
                    for s, v in items:
                        e.wait_ge(s, v)
                    m, a, kw = o.fn
                    ins = getattr(e, m)(*a, **kw)
                    if attach is not None:
                        ins._wait_ge(attach[0], attach[1])
                    if o.is_dma:
                        ins.then_inc(o.key.sem, 16)
                    elif o.sig:
                        ins.then_inc(sems[en], 1)
                if en == "sp":
                    for k in self.final_keys:
                        if k.n > 0:
                            e.wait_ge(k.sem, 16 * k.n)
            return f

        with nc.Block() as blk:
            blk.tensor(runner("pe"))
            blk.scalar(runner("act"))
            blk.vector(runner("dve"))
            blk.gpsimd(runner("pool"))
            blk.sync(runner("sp"))


class Sched:
    def __init__(self, ka, kb):
        self.cv = threading.Condition()
        self.turn = None
        self.alive = {"A": False, "B": False}
        self.cnt = {"A": 0, "B": 0}
        self.k = {"A": ka, "B": kb}
        self.local = threading.local()
        self.exc = None

    def me(self):
        return getattr(self.local, "name", None)

    def run(self, fa, fb):
        self.alive = {"A": True, "B": True}
        self.turn = "A"
        self.cnt = {"A": 0, "B": 0}

        def wrap(name, f):
            other = "B" if name == "A" else "A"

            def g():
                self.local.name = name
                with self.cv:
                    while self.turn != name:
                        self.cv.wait()
                try:
                    f()
                except BaseException as e:
                    self.exc = e
                finally:
                    with self.cv:
                        self.alive[name] = False
                        self.turn = other
                        self.cv.notify_all()
            return g

        ta = threading.Thread(target=wrap("A", fa))
        tb = threading.Thread(target=wrap("B", fb))
        ta.start()
        tb.start()
        ta.join()
        tb.join()
        self.turn = None
        if self.exc is not None:
            e = self.exc
            self.exc = None
            raise e

    def yield_(self):
        name = self.me()
        if name is None:
            return False
        other = "B" if name == "A" else "A"
        if not self.alive[other]:
            return False
        with self.cv:
            self.turn = other
            self.cv.notify_all()
            while self.turn != name:
                self.cv.wait()
        return True

    def tick(self, o):
        name = self.me()
        if name is None or o.is_dma or o.eng == "sp":
            return
        if name == "A":
            if o.eng != "pe":
                self.cnt["A"] += 1
        elif o.eng == "pe":
            self.cnt["B"] += 1
        if self.cnt[name] >= self.k[name]:
            self.cnt[name] = 0
            self.yield_()


class TT:
    def __init__(self, ap, nb=1, name=""):
        self.ap = ap
        self.b = [Buf(name) for _ in range(nb)]

    def renew(self, pending):
        self.b = [Buf("", pending) for _ in self.b]


def prod(xs):
    r = 1
    for x in xs:
        r *= x
    return r


class Blk:
    pass


class Tile:
    pass


class Seq:
    pass


DN_GROUPS = ((0, 8), (8, 8), (16, 6))
INTERLEAVE = True
KA = 4
KB = 8
PAUSE1 = 4
PAUSE2 = 12
VW = 72
DBG = set("m1,m2,m3,st,m4,m4a,m5".split(","))


def build(n_prompt, NB, n_samp=2, samp_len=32, stages="all"):
    nc = bass.Bass("TRN2", target_bir_lowering=False)
    R = Rec(nc)
    TMAX = 128 * NB
    nt = n_prompt // TMAX
    assert nt * TMAX == n_prompt
    NPOSX = n_prompt + n_samp * samp_len

    def din(name, shape, dt=F32):
        return nc.dram_tensor(name, list(shape), dt, kind="ExternalInput").ap()

    def dout(name, shape, dt=F32):
        return nc.dram_tensor(name, list(shape), dt, kind="ExternalOutput").ap()

    xp = din("xp", [n_prompt, D])
    xs = din("xs", [n_samp * samp_len, D])
    st_conv = din("st_conv", [n_samp, 3, CONV])
    st_ssm = din("st_ssm", [n_samp, DI, NS])
    ck = din("ck", [n_samp, 128, 256])
    cv = din("cv", [n_samp, 128, 256])
    w_gu = [din("w_gu1", [D, 2 * DFF]), din("w_gu2", [D, 2 * DFF])]
    w_dn = [din("w_dn1", [DFF, D]), din("w_dn2", [DFF, D])]
    w_in = din("w_in", [D, DINP])
    w_brm = din("w_brm", [DI, D])
    w_bra = din("w_bra", [D, D])
    w_o = din("w_o", [D, D])
    NPAR = PAR["_n"]
    params_d = din("params", [P, NPAR])
    NCB = CB["_n"]
    cbf_d = din("cbf", [P, NCB], BF16)
    cs_d = din("cs_t", [P, 2, NPOSX])

    yp = dout("yp", [n_prompt, D])
    ys = dout("ys", [n_samp * samp_len, D])
    conv_p = dout("conv_p", [3, CONV])
    ssm_p = dout("ssm_p", [DI, NS])
    rows_p = min(128, n_prompt)
    k_p = dout("k_p", [rows_p, 256])
    v_p = dout("v_p", [rows_p, 256])
    conv_s = dout("conv_s", [n_samp, 3, CONV])
    ssm_s = dout("ssm_s", [n_samp, DI, NS])
    k_s = dout("k_s", [n_samp * samp_len, 256])
    v_s = dout("v_s", [n_samp * samp_len, 256])

    def out_key(name):
        k = R.key(name)
        R.final_keys.append(k)
        return k

    def sb(name, shape, dt=F32):
        return nc.alloc_sbuf_tensor(name, list(shape), dt)

    def tt(name, shape, dt=F32, nb=1):
        t = sb(name, shape, dt)
        return TT(t[:], nb, name)

    class Arena:
        def __init__(self, name, nbytes):
            self.t = sb(name, [P, nbytes // 2], BF16)
            self.nbytes = nbytes
            self.off = 0
            self.hi = 0

        def carve(self, shape, dt, nb=1, name=""):
            nel = prod(shape[1:])
            esz = 2 if dt == BF16 else 4
            nbytes = (nel * esz + 31) // 32 * 32
            assert self.off + nbytes <= self.nbytes, (name, self.off, nbytes, self.nbytes)
            a = self.t[:, self.off // 2:(self.off + nel * esz) // 2]
            self.off += nbytes
            self.hi = max(self.hi, self.off)
            if dt == F32:
                a = a.bitcast(F32)
            if len(shape) == 3:
                a = a.rearrange("p (a b) -> p a b", b=shape[2])
            elif len(shape) == 4:
                a = a.rearrange("p (a b c) -> p a b c", b=shape[2], c=shape[3])
            return TT(a, nb, name)

    par = tt("par_sb", [P, NPAR])
    cbf = tt("cbf_sb", [P, NCB], BF16)
    ld_key = R.key("ld", grouped=True)
    R.op("sp", "dma_start", [], par.b, out=par.ap, in_=params_d, key=ld_key)
    R.op("sp", "dma_start", [], cbf.b, out=cbf.ap, in_=cbf_d, key=ld_key)

    def pv(name):
        o, n = PAR[name]
        return par.ap[:, o:o + n]

    def cv_(name):
        o, n = CB[name]
        return cbf.ap[:, o:o + n]

    ident = cv_("ident")
    tri = cv_("tri")
    m2 = cv_("m2")
    ones = cv_("ones")
    rm = cv_("rm")
    negm = cv_("negm").rearrange("p (h l) -> p h l", l=128)
    maskA = cv_("maskA").rearrange("p (h l) -> p h l", l=128)
    maskB = cv_("maskB").rearrange("p (h l) -> p h l", l=128)
    _oA = CB["maskA"][0]
    assert CB["maskB"][0] == _oA + 512
    maskAB = cbf.ap[:, _oA + 256:_oA + 768].rearrange("p (h l) -> p h l", l=128)
    identf = pv("identf")
    gpre = {0: pv("gpre1"), 1: pv("gpre2"), "m": pv("gmix")}
    gpost = {0: pv("gpost1"), 1: pv("gpost2"), "m": pv("gpostm")}
    gmn = pv("gmn")
    convw = pv("convw")
    convb = pv("convb")
    dtb = pv("dtb")
    dskip = pv("dskip")
    negcol = pv("negcol")

    dconst = tt("dconst", [P, 48])
    Aneg = dconst.ap[:, 0:32]
    esink = dconst.ap[:, 32:48]
    R.op("act", "activation", par.b, dconst.b, out=dconst.ap[:, 0:32], in_=pv("alog"), func=AF.Exp)
    R.op("act", "activation", par.b, dconst.b, out=dconst.ap[:, 32:48], in_=pv("sink"), func=AF.Exp)
    R.op("dve", "tensor_scalar", dconst.b, dconst.b, out=dconst.ap[:, 0:32], in0=dconst.ap[:, 0:32], scalar1=-1.0, scalar2=None, op0=ALU.mult)

    class PsPool:
        def __init__(self):
            self.banks = []
            for i in range(8):
                t = nc.alloc_psum_tensor("ps%d" % i, [P, 512], F32)
                tb = TT(t[:], 1, "ps%d" % i)
                tb.bf = t[:].bitcast(BF16)
                tb.b[0].excl = True
                tb.live = False
                self.banks.append(tb)
            self.i = 0

        def next(self):
            for attempt in range(10000):
                for _ in range(8):
                    b = self.banks[self.i]
                    self.i = (self.i + 1) % 8
                    if not b.live:
                        b.live = True
                        return b
                if R.sched is None or not R.sched.yield_():
                    break
            raise RuntimeError("psum exhausted")

        def rel(self, *bs):
            for b in bs:
                b.live = False

    psum = PsPool()

    SLOT = 4096
    units = {}
    unit_defs = {}

    NCK = 12
    cast_keys = [R.key("cast%d" % i) for i in range(NCK)]
    cast_dummy = [Buf("castd%d" % i) for i in range(NCK)]
    cast_i = [0]
    unit_order = []

    def def_unit(name, n, srcs, grp):
        unit_defs[name] = (n, srcs, grp)
        unit_order.append(name)

    def make_unit(name):
        n, srcs, grp = unit_defs[name]
        d = nc.dram_tensor("wb_" + name, [P, n], BF16, kind="Internal").ap()
        bs = []
        for dv, src in srcs:
            b = Buf(name)
            ci = cast_i[0] % NCK
            cast_i[0] += 1
            R.op("pool", "dma_start", [], [b, cast_dummy[ci]], out=dv(d), in_=src, key=cast_keys[ci])
            bs.append(b)
        units[name] = (d, bs, n)

    def get_unit(name):
        return units[name]

    def kcview(w, c0, ncols):
        return w.rearrange("(kc p) n -> p kc n", p=P)[:, :, c0:c0 + ncols]

    def v3(k):
        return lambda d: d.rearrange("p (k n) -> p k n", k=k)

    for f in range(2):
        wg = w_gu[f].rearrange("(kc p) (gu x) -> p kc gu x", p=P, gu=2)
        for u in range(11):
            srcs = []
            for gu in range(2):
                srcs.append(((lambda gu: lambda d: d.rearrange("p (kc gu x) -> p kc gu x", kc=KC, gu=2)[:, :, gu, :])(gu),
                             wg[:, :, gu, 2 * u * 128:2 * u * 128 + 256]))
            def_unit("gu%d_%d" % (f, u), KC * 512, srcs, "gu%d" % f)
        wd = w_dn[f].rearrange("(kc p) n -> p kc n", p=P)
        for half in range(2):
            for gi, (k0, nk) in enumerate(DN_GROUPS):
                def_unit("dn%d_%d_%d" % (f, half, gi), nk * 512, [(v3(nk), wd[:, k0:k0 + nk, half * 512:(half + 1) * 512])], "dn%d" % f)
    for u in range(6):
        def_unit("xbc%d" % u, KC * 512, [(v3(KC), kcview(w_in, OFF_XBC + u * 512, 512))], "win_a")
    def_unit("dtv", KC * 288,
             [(lambda d: d.rearrange("p (k n) -> p k n", k=KC)[:, :, 0:32], kcview(w_in, OFF_DT, 32)),
              (lambda d: d.rearrange("p (k n) -> p k n", k=KC)[:, :, 32:288], kcview(w_in, OFF_V, 256))], "win_a")
    for u in range(4):
        def_unit("z%d" % u, KC * 512, [(v3(KC), kcview(w_in, OFF_Z + u * 512, 512))], "win_a")
    for u in range(2):
        def_unit("q%d" % u, KC * 512, [(v3(KC), kcview(w_in, OFF_Q + u * 512, 512))], "win_b")
    def_unit("kdup", KC * 512,
             [((lambda j, t: lambda d: d.rearrange("p (k x) -> p k x", k=KC)[:, :, j * 128 + t * 64:j * 128 + t * 64 + 64])(j, t),
               kcview(w_in, OFF_K + j * 64, 64)) for j in range(4) for t in range(2)], "win_b")
    for u in range(2):
        def_unit("gm%d" % u, KC * 512, [(v3(KC), kcview(w_in, OFF_GM + u * 512, 512))], "win_b")
        def_unit("ga%d" % u, KC * 512, [(v3(KC), kcview(w_in, OFF_GA + u * 512, 512))], "win_b")
    wm = w_brm.rearrange("(kc p) n -> p kc n", p=P)
    for u in range(4):
        def_unit("brm%d" % u, 16 * 256, [(v3(16), wm[:, :, u * 256:(u + 1) * 256])], "brm")
    for u in range(2):
        def_unit("bra%d" % u, KC * 512, [(v3(KC), kcview(w_bra, u * 512, 512))], "bra")
    for u in range(2):
        def_unit("wo%d" % u, KC * 512, [(v3(KC), kcview(w_o, u * 512, 512))], "wo")

    def prepass(seq):
        for name in seq:
            if name not in units:
                make_unit(name)

    def ffn_unit_seq(f):
        s_ = []
        for u in range(11):
            s_.append("gu%d_%d" % (f, u))
        for half in range(2):
            for gi in range(3):
                s_.append("dn%d_%d_%d" % (f, half, gi))
        return s_

    def mix_unit_seq():
        s_ = []
        for u in range(6):
            s_.append("xbc%d" % u)
        s_.append("dtv")
        for u in range(4):
            s_.append("z%d" % u)
        for u in range(2):
            s_.append("q%d" % u)
        s_.append("kdup")
        for u in range(4):
            s_.append(("gm%d" if u % 2 == 0 else "ga%d") % (u // 2))
        for u in range(4):
            s_.append("brm%d" % u)
        for u in range(2):
            s_.append("bra%d" % u)
        for u in range(2):
            s_.append("wo%d" % u)
        return s_

    class WStream:
        def __init__(self, name, nslot):
            self.seq = []
            self.issued = 0
            self.cons = 0
            self.nslot = nslot
            self.look = nslot - 1
            self.ring = [tt("wslot%s%d" % (name, i), [P, SLOT], BF16) for i in range(nslot)]
            self.keys = [R.key("ring%s%d" % (name, i)) for i in range(nslot)]

        def _issue(self):
            i = self.issued
            d, bs, n = get_unit(self.seq[i])
            slot = self.ring[i % self.nslot]
            R.op("sp", "dma_start", bs, slot.b, out=slot.ap[:, 0:n], in_=d, key=self.keys[i % self.nslot])
            self.issued += 1

        def acquire(self, name):
            i = self.cons
            assert self.seq[i] == name, (self.seq[i], name)
            while self.issued < min(len(self.seq), i + 1 + self.look):
                self._issue()
            self.cons += 1
            return self.ring[i % self.nslot]

    tiles = []
    prompt = Seq()
    samp = [Seq() for _ in range(n_samp)]
    for t in range(nt):
        tl = Tile()
        tl.blocks = []
        for b in range(NB):
            k = Blk()
            k.n = 128
            k.seq = prompt
            k.pos0 = (t * NB + b) * 128
            k.first = (k.pos0 == 0)
            k.last = (k.pos0 + 128 == n_prompt)
            k.is_samp = False
            tl.blocks.append(k)
        r0 = t * TMAX
        tl.src = xp[r0:r0 + TMAX, :].rearrange("(b p) d -> p b d", p=128)
        tl.dst = yp[r0:r0 + TMAX, :].rearrange("(b p) d -> p b d", p=128)
        tl.runs = [(prompt, 0, TMAX)]
        tl.cscol = r0
        tl.is_samp = False
        tiles.append(tl)
    tl = Tile()
    tl.blocks = []
    for b in range(n_samp):
        k = Blk()
        k.n = samp_len
        k.seq = samp[b]
        k.pos0 = PAST_LEN
        k.first = True
        k.last = True
        k.is_samp = True
        k.sidx = b
        tl.blocks.append(k)
    tl.src = xs.rearrange("(b p) d -> p b d", p=samp_len)
    tl.dst = ys.rearrange("(b p) d -> p b d", p=samp_len)
    tl.runs = [(samp[b], b * samp_len, samp_len) for b in range(n_samp)]
    tl.cscol = n_prompt
    tl.is_samp = True
    tiles.append(tl)
    for tl in tiles:
        c = 0
        for bi, k in enumerate(tl.blocks):
            k.col0 = c
            k.bi = bi
            c += k.n
        tl.T = c
        tl.n = tl.blocks[0].n
        tl.nb = len(tl.blocks)

    prepass(ffn_unit_seq(0) + mix_unit_seq() + ffn_unit_seq(1))
    wsA = WStream("A", 3)
    wsB = WStream("B", 2)

    NBX = max(NB, n_samp)
    x_bufs = [tt("x%d" % i, [P, NBX, D], F32, nb=1) for i in range(2)]
    x_keys = [R.key("x%d" % i) for i in range(2)]
    xst_keys = [out_key("xst%d" % i) for i in range(2)]
    xnT = tt("xnT", [P, KC, TMAX], BF16, nb=NBX)
    uT = tt("uT", [P, KC, TMAX], BF16, nb=NBX)
    hT = tt("hT", [P, FC, TMAX], BF16, nb=FC)
    sg = [tt("sg%d" % i, [P, TMAX], F32) for i in range(2)]

    class Scr:
        def __init__(self, nm):
            self.junk = tt("junk" + nm, [P, 2048], BF16)
            self.etmp = tt("etmp" + nm, [P, 512], F32)
            self.stat = [tt("stat%s%d" % (nm, i), [P, 8], F32) for i in range(4)]
            self.i = 0

        def new_stat(self):
            s_ = self.stat[self.i % 4]
            self.i += 1
            return s_

    scrA = Scr("A")
    scrB = Scr("B")

    def load_x(ti):
        tl = tiles[ti]
        xb = x_bufs[ti % 2]
        R.op("sp", "dma_start", [], xb.b, out=xb.ap[0:tl.n, 0:tl.nb, :], in_=tl.src, key=x_keys[ti % 2])

    def store_x(ti):
        tl = tiles[ti]
        xb = x_bufs[ti % 2]
        R.op("sp", "dma_start", xb.b, [], out=tl.dst, in_=xb.ap[0:tl.n, 0:tl.nb, :], key=xst_keys[ti % 2])

    def rstd_from(ssq_ap, n, st, col, dim, extra):
        e2 = float(extra) ** 2
        R.op("act", "activation", st.b, st.b, out=st.ap[0:n, col:col + 1], in_=ssq_ap, func=AF.Sqrt, scale=1.0 / (dim * e2), bias=EPS / e2)
        R.op("dve", "reciprocal", st.b, st.b, out=st.ap[0:n, col:col + 1], in_=st.ap[0:n, col:col + 1])

    def norm_T(tl, xb, g_ap, dst, scr):
        for k in tl.blocks:
            n = k.n
            st = scr.new_stat()
            junk = scr.junk
            xin = xb.ap[0:n, k.bi, :]
            R.op("act", "activation", xb.b, junk.b + st.b, out=junk.ap[0:n, 0:D], in_=xin, func=AF.Square, accum_out=st.ap[0:n, 0:1])
            rstd_from(st.ap[0:n, 0:1], n, st, 1, D, 1.0)
            xn = junk.ap[0:n, 1024:2048]
            R.op("act", "activation", xb.b + st.b, junk.b, out=xn, in_=xin, func=AF.Copy, scale=st.ap[0:n, 1:2])
            ps = psum.next()
            ptv = ps.bf.rearrange("p (a b) -> p a b", b=128)
            for kc in range(KC):
                R.op("pe", "transpose", junk.b + cbf.b, ps.b, ptv[:, kc, 0:n], junk.ap[0:n, 1024 + kc * 128:1024 + (kc + 1) * 128], ident[0:n, 0:n])
            R.op("dve", "tensor_tensor", ps.b + par.b, [dst.b[k.bi]], out=dst.ap[:, :, k.col0:k.col0 + n], in0=ptv[:, :, 0:n],
                 in1=g_ap.unsqueeze(2).to_broadcast([P, KC, n]), op=ALU.mult)
            psum.rel(ps)

    def post_norm_add(xb, k, accs, g_ap, scale, scr):
        n = k.n
        st = scr.new_stat()
        junk = scr.junk
        for h in range(2):
            R.op("act", "activation", accs[h].b, junk.b + st.b, out=junk.ap[0:n, h * 512:(h + 1) * 512], in_=accs[h].ap[0:n, :], func=AF.Square,
                 accum_out=st.ap[0:n, h:h + 1])
        R.op("dve", "tensor_tensor", st.b, st.b, out=st.ap[0:n, 2:3], in0=st.ap[0:n, 0:1], in1=st.ap[0:n, 1:2], op=ALU.add)
        rstd_from(st.ap[0:n, 2:3], n, st, 3, D, scale)
        for h in range(2):
            et = scr.etmp
            R.op("dve", "scalar_tensor_tensor", accs[h].b + st.b + par.b, et.b, out=et.ap[0:n, :], in0=accs[h].ap[0:n, :], scalar=st.ap[0:n, 3:4],
                 in1=g_ap[0:n, h * 512:(h + 1) * 512], op0=ALU.mult, op1=ALU.mult)
            xsl = xb.ap[0:n, k.bi, h * 512:(h + 1) * 512]
            R.op("pool", "tensor_tensor", et.b + xb.b, xb.b, out=xsl, in0=et.ap[0:n, :], in1=xsl, op=ALU.add)

    def ffn(tl, xb, f):
        T = tl.T
        norm_T(tl, xb, gpre[f], xnT, scrB)
        for u in range(11):
            slot = wsB.acquire("gu%d_%d" % (f, u))
            wv = slot.ap.rearrange("p (kc gu jj c) -> p kc gu jj c", kc=KC, gu=2, jj=2)
            for jj in range(2):
                j = 2 * u + jj
                gps = psum.next()
                ups = psum.next()
                for gu, ps in ((0, gps), (1, ups)):
                    for kc in range(KC):
                        R.op("pe", "matmul", slot.b + xnT.b[0:tl.nb], ps.b, ps.ap[:, 0:T], lhsT=wv[:, kc, gu, jj, :], rhs=xnT.ap[:, kc, 0:T],
                             start=(kc == 0), stop=(kc == KC - 1))
                s = sg[j % 2]
                R.op("act", "activation", gps.b, s.b, out=s.ap[:, 0:T], in_=gps.ap[:, 0:T], func=AF.Tanh, scale=0.5)
                R.op("dve", "scalar_tensor_tensor", gps.b + s.b, s.b, out=s.ap[:, 0:T], in0=s.ap[:, 0:T], scalar=1.0, in1=gps.ap[:, 0:T],
                     op0=ALU.add, op1=ALU.mult)
                R.op("dve", "scalar_tensor_tensor", ups.b + s.b, [hT.b[j]], out=hT.ap[:, j, 0:T], in0=s.ap[:, 0:T], scalar=0.5, in1=ups.ap[:, 0:T],
                     op0=ALU.mult, op1=ALU.mult)
                psum.rel(gps, ups)
        accs = [[psum.next() for _ in tl.blocks] for _ in range(2)]
        for half in range(2):
            for gi, (k0, nk) in enumerate(DN_GROUPS):
                slot = wsB.acquire("dn%d_%d_%d" % (f, half, gi))
                wv = slot.ap[:, 0:nk * 512].rearrange("p (k n) -> p k n", k=nk)
                for kk in range(nk):
                    kc = k0 + kk
                    for k in tl.blocks:
                        R.op("pe", "matmul", slot.b + [hT.b[kc]], accs[half][k.bi].b, accs[half][k.bi].ap[0:k.n, :],
                             lhsT=hT.ap[:, kc, k.col0:k.col0 + k.n], rhs=wv[:, kk, :], start=(kc == 0), stop=(kc == FC - 1))
        for k in tl.blocks:
            a2 = [accs[0][k.bi], accs[1][k.bi]]
            post_norm_add(xb, k, a2, gpost[f], 0.5, scrB)
            psum.rel(*a2)

    do_mix = stages in ("all", "mix")
    if do_mix:
        for si, s in enumerate([prompt] + samp):
            s.halo = tt("halo%d" % si, [P, 24, 3], F32)
        hbufs = [(tt("hTa", [P, DI], F32, nb=4), tt("hTa_bf", [P, DI], BF16, nb=4))]
        prompt.h, prompt.hbf = hbufs[0]
        for i, s in enumerate(samp):
            s.h, s.hbf = hbufs[0]
        kT_all = tt("kT_all", [P, 4, (NB + 1) * 128], BF16, nb=NB + 1)
        v_all = tt("v_all", [P, NB + 1, 4, VW], BF16, nb=NB + 1)
        cacheT = [tt("cacheT%d" % i, [P, 4, 128], BF16) for i in range(n_samp)]
        cachev = [tt("cachev%d" % i, [P, 4, VW], BF16) for i in range(n_samp)]
        R.op("pool", "memset", [], prompt.halo.b, prompt.halo.ap, 0.0)
        R.op("pool", "memset", [], prompt.h.b, prompt.h.ap, 0.0)
        R.op("pool", "memset", [], prompt.hbf.b, prompt.hbf.ap, 0.0)
        R.op("pool", "memset", [], v_all.b, v_all.ap, 1.0)
        for c in cachev:
            R.op("pool", "memset", [], c.b, c.ap, 1.0)

        mx = Arena("mx", MX_BYTES[NB])
        ymT = mx.carve([P, 16, TMAX], BF16, nb=NBX, name="ymT")
        base = mx.off
        PREW = max(3 + TMAX, n_samp * (3 + samp_len))
        pre = [mx.carve([P, PREW], F32, name="pre") for _ in range(2)]
        acc = [mx.carve([P, TMAX], F32, name="acc") for _ in range(2)]
        xc = [mx.carve([P, TMAX], BF16, name="xc") for _ in range(2)]
        xh_tok = mx.carve([P, NBX, DI], BF16, nb=NBX, name="xh_tok")
        B_tok = mx.carve([P, NBX, 512], BF16, nb=NBX, name="B_tok")
        BT = mx.carve([P, 4, TMAX], BF16, name="BT")
        CT = mx.carve([P, 4, TMAX], BF16, name="CT")
        zs = mx.carve([P, NBX, DI], BF16, nb=NBX, name="zs")
        dtw = mx.carve([P, NBX, 32], F32, nb=NBX, name="dtw")
        dtA = mx.carve([P, NBX, 32], BF16, nb=NBX, name="dtA")
        xdt = mx.carve([P, DI], BF16, nb=4, name="xdt")
        xdtw = mx.carve([P, DI], BF16, nb=4, name="xdtw")
        rhs1 = [mx.carve([P, 8, 128], BF16, name="rhs1") for _ in range(2)]
        xd = mx.carve([P, DI], BF16, nb=4, name="xd")
        Eb = [mx.carve([P, 8, 128], BF16, name="E") for _ in range(2)]
        Gm = [mx.carve([P, 8, 128], BF16, name="G") for _ in range(2)]
        cbm = mx.carve([P, 4, 128], BF16, name="cbm")
        t1 = [mx.carve([P, 512], F32, name="t1") for _ in range(1)] * 2
        t2 = [mx.carve([P, 512], F32, name="t2") for _ in range(2)]
        yg = tt("yg", [P, DI], F32, nb=4)
        yn = scrA.junk
        eacs = mx.carve([P, 64], F32, name="eacs")
        hstage_ap = yg.ap.rearrange("p (c n) -> p c n", n=128)
        kvout0 = tt("kvout0", [P, 256], F32)
        M123 = pre + acc + xc + [xh_tok, B_tok, BT, CT, zs, dtw, dtA, xdt, xdtw] + rhs1 + [xd] + Eb + Gm + [cbm, t1[0]] + t2 + [yg, eacs, kvout0]
        mx.off = base
        cs = tt("cs_sb", [P, 2, TMAX], F32)
        qb = [mx.carve([P, TMAX], BF16, name="qb") for _ in range(2)]
        rt1 = [mx.carve([P, TMAX], F32, name="rt1") for _ in range(2)]
        rt2 = [mx.carve([P, TMAX], F32, name="rt2") for _ in range(2)]
        qT = mx.carve([P, 8, TMAX], BF16, nb=8, name="qT")
        PT = [mx.carve([P, 4, 128], BF16, name="PT") for _ in range(4)]
        attn_tok = [mx.carve([P, 1024], BF16, name="attn_tok") for _ in range(2)]
        den = [mx.carve([P, 8], F32, name="den") for _ in range(2)]
        attnT = mx.carve([P, 8, TMAX], BF16, nb=NBX, name="attnT")
        bt = [mx.carve([P, TMAX], F32, name="bt") for _ in range(2)]
        mixedT = mx.carve([P, 8, TMAX], BF16, nb=8, name="mixedT")
        krot_f = mx.carve([P, 4, 128], F32, name="krot_f")
        kvout1 = tt("kvout1", [P, 256], F32)
        gates = [mx.carve([P, 8, TMAX], BF16, nb=8, name="gate%d" % i) for i in range(2)]
        mixf = mx.carve([P, 8, TMAX], F32, nb=8, name="mixf")
        M456 = [cs] + qb + rt1 + rt2 + [qT] + PT + attn_tok + den + [attnT] + bt + [mixedT, krot_f, kvout1] + gates + [mixf]
        cs_key = R.key("cs")
        misc_keys = {}

        def misc_out(name, reads, **kw):
            k = out_key("o_" + name)
            R.op("sp", "dma_start", reads, [], key=k, **kw)

        def switch(frm, to):
            pend = Rec.pending([b for t_ in frm for b in t_.b])
            for t_ in to:
                t_.renew(pend)

        def bc3(ap2, a, b_):
            return ap2.unsqueeze(2).to_broadcast([ap2.shape[0], a, b_])

        def bcm(ap2, a, b_):
            return ap2.unsqueeze(1).to_broadcast([ap2.shape[0], a, b_])

        def lkey(nm):
            return R.key(nm)

        def load_halo(k):
            s = k.seq
            for r in range(3):
                R.op("sp", "dma_start", [], s.halo.b, out=s.halo.ap[:, :, r], in_=st_conv[k.sidx, r].rearrange("(c p) -> p c", p=P),
                     key=lkey("lh%d_%d" % (k.sidx, r)), allow_slow_non_contiguous=True)

        def load_seq_state_ssm(k):
            s = k.seq
            b = k.sidx
            R.op("sp", "dma_start", [], yg.b, out=hstage_ap, in_=st_ssm[b].rearrange("(c p) n -> p c n", p=P), key=lkey("ls%d" % b))
            for c4 in range(4):
                ps = psum.next()
                pv4 = ps.ap.rearrange("p (a b) -> p a b", b=128)
                for cc in range(4):
                    c = c4 * 4 + cc
                    R.op("pe", "transpose", [yg.b[c4]] + par.b, ps.b, pv4[:, cc, :], hstage_ap[:, c, :], identf)
                R.op("act", "activation", ps.b, [s.h.b[c4]], out=s.h.ap[:, c4 * 512:(c4 + 1) * 512], in_=ps.ap, func=AF.Copy)
                R.op("dve", "tensor_scalar", ps.b, [s.hbf.b[c4]], out=s.hbf.ap[:, c4 * 512:(c4 + 1) * 512], in0=ps.ap, scalar1=1.0, scalar2=None, op0=ALU.mult)
                psum.rel(ps)

        def load_seq_state_kv(k):
            b = k.sidx
            ckf = rt1[b % 2]
            cvf = rt2[b % 2]
            assert TMAX >= 256
            R.op("sp", "dma_start", [], ckf.b, out=ckf.ap[:, 0:256], in_=ck[b], key=lkey("lk%d" % b))
            R.op("sp", "dma_start", [], cvf.b, out=cvf.ap[:, 0:256], in_=cv[b], key=lkey("lv%d" % b))
            kd = attn_tok[b % 2]
            kdv = kd.ap[:, 0:512].rearrange("p (j t c) -> p j t c", j=4, t=2)
            for dd in range(2):
                R.op("dve", "tensor_copy", ckf.b, kd.b, out=kdv[:, :, dd, :], in_=ckf.ap[:, 0:256].rearrange("p (j c) -> p j c", c=64))
            ps = psum.next()
            ptv = ps.bf.rearrange("p (a b) -> p a b", b=128)
            for j in range(4):
                R.op("pe", "transpose", kd.b + cbf.b, ps.b, ptv[:, j, :], kd.ap[:, j * 128:(j + 1) * 128], ident)
            R.op("act", "activation", ps.b, cacheT[b].b, out=cacheT[b].ap, in_=ptv[:, 0:4, :], func=AF.Copy)
            psum.rel(ps)
            R.op("act", "activation", cvf.b, cachev[b].b, out=cachev[b].ap[:, :, 0:64], in_=cvf.ap[:, 0:256].rearrange("p (j c) -> p j c", c=64),
                 func=AF.Copy)

        def store_seq_state(k):
            s = k.seq
            if k.is_samp:
                cdst = conv_s[k.sidx]
                sdst = ssm_s[k.sidx]
            else:
                cdst = conv_p
                sdst = ssm_p
            for r in range(3):
                misc_out("conv%d" % r, s.halo.b, out=cdst[r].rearrange("(c p) -> p c", p=P), in_=s.halo.ap[:, :, r], allow_slow_non_contiguous=True)
            for c4 in range(4):
                ps = psum.next()
                pv4 = ps.ap.rearrange("p (a b) -> p a b", b=128)
                for cc in range(4):
                    c = c4 * 4 + cc
                    R.op("pe", "transpose", [s.h.b[c4]] + par.b, ps.b, pv4[:, cc, :], s.h.ap[:, c * 128:(c + 1) * 128], identf)
                R.op("act", "activation", ps.b, [yg.b[c4]], out=hstage_ap[:, c4 * 4:(c4 + 1) * 4, :], in_=pv4, func=AF.Copy)
                psum.rel(ps)
            misc_out("ssm", yg.b, out=sdst.rearrange("(c p) n -> p c n", p=P), in_=hstage_ap)

        def m1_xbc(tl):
            T = tl.T
            nruns = len(tl.runs)
            ln = tl.runs[0][2]
            W3 = 3 + ln
            n = tl.n
            slots = {}

            def views(c):
                pr = pre[c % 2]
                prv = pr.ap[:, 0:nruns * W3].rearrange("p (r w) -> p r w", w=W3)
                ac = acc[c % 2]
                acv = ac.ap[:, 0:T].rearrange("p (r w) -> p r w", w=ln)
                if c < 16:
                    dap, dbufs = xc[c % 2].ap[:, 0:T], xc[c % 2].b
                elif c < 20:
                    dap, dbufs = BT.ap[:, c - 16, 0:T], BT.b
                else:
                    dap, dbufs = CT.ap[:, c - 20, 0:T], CT.b
                return pr, prv, ac, acv, dap, dbufs

            def s1(c):
                u, cc = divmod(c, 4)
                if cc == 0:
                    slots[u] = wsA.acquire("xbc%d" % u)
                slot = slots[u]
                wv = slot.ap.rearrange("p (k cc c) -> p k cc c", k=KC, cc=4)
                pr, prv, ac, acv, dap, dbufs = views(c)
                ps = psum.next()
                for kc in range(KC):
                    R.op("pe", "matmul", slot.b + uT.b[0:tl.nb], ps.b, ps.ap[:, 0:T], lhsT=wv[:, kc, cc, :], rhs=uT.ap[:, kc, 0:T],
                         start=(kc == 0), stop=(kc == KC - 1))
                for r, (s_, col0, l_) in enumerate(tl.runs):
                    R.op("pool", "tensor_copy", s_.halo.b, pr.b, out=prv[:, r, 0:3], in_=s_.halo.ap[:, c, :])
                R.op("act", "activation", ps.b, pr.b, out=prv[:, :, 3:W3], in_=ps.ap[:, 0:T].rearrange("p (r w) -> p r w", w=ln), func=AF.Copy)
                psum.rel(ps)
                for r, (s_, col0, l_) in enumerate(tl.runs):
                    R.op("pool", "tensor_copy", pr.b, s_.halo.b, out=s_.halo.ap[:, c, :], in_=prv[:, r, ln:ln + 3])

            def s2(c):
                pr, prv, ac, acv, dap, dbufs = views(c)
                R.op("dve", "tensor_scalar", pr.b + par.b, ac.b, out=acv, in0=prv[:, :, 0:ln], scalar1=convw[:, c * 4:c * 4 + 1], scalar2=None, op0=ALU.mult)
                for j in range(1, 4):
                    R.op("dve", "scalar_tensor_tensor", pr.b + par.b + ac.b, ac.b, out=acv, in0=prv[:, :, j:j + ln],
                         scalar=convw[:, c * 4 + j:c * 4 + j + 1], in1=acv, op0=ALU.mult, op1=ALU.add)
                R.op("act", "activation", ac.b + par.b, dbufs, out=dap, in_=ac.ap[:, 0:T], func=AF.Silu, bias=convb[:, c:c + 1])

            def s3(c):
                if c >= 20:
                    return
                pr, prv, ac, acv, dap, dbufs = views(c)
                ps2 = psum.next()
                ptv = ps2.bf.rearrange("p (a b) -> p a b", b=128)
                for k in tl.blocks:
                    R.op("pe", "transpose", dbufs + cbf.b, ps2.b, ptv[0:n, k.bi, :], dap[:, k.col0:k.col0 + n], ident)
                if c < 16:
                    R.op("act", "activation", ps2.b, xh_tok.b[0:tl.nb], out=xh_tok.ap[0:n, 0:tl.nb, c * 128:(c + 1) * 128], in_=ptv[0:n, 0:tl.nb, :], func=AF.Copy)
                else:
                    g = c - 16
                    R.op("act", "activation", ps2.b, B_tok.b[0:tl.nb], out=B_tok.ap[0:n, 0:tl.nb, g * 128:(g + 1) * 128], in_=ptv[0:n, 0:tl.nb, :], func=AF.Copy)
                psum.rel(ps2)

            for c in range(24 + 2):
                if c < 24:
                    s1(c)
                if 0 <= c - 1 < 24:
                    s2(c - 1)
                if 0 <= c - 2 < 24:
                    s3(c - 2)

        def m2_dtv_z(tl):
            n = tl.n
            slot = wsA.acquire("dtv")
            wv = slot.ap[:, 0:KC * 288].rearrange("p (k n) -> p k n", k=KC)
            for k in tl.blocks:
                ps = psum.next()
                for kc in range(KC):
                    R.op("pe", "matmul", slot.b + [uT.b[k.bi]], ps.b, ps.ap[0:n, 0:288], lhsT=uT.ap[:, kc, k.col0:k.col0 + n], rhs=wv[:, kc, :],
                         start=(kc == 0), stop=(kc == KC - 1))
                dw = dtw.ap[0:n, k.bi, :]
                if "z1" in DBG:
                    psum.rel(ps)
                    continue
                R.op("dve", "tensor_tensor", ps.b + par.b, [dtw.b[k.bi]], out=dw, in0=ps.ap[0:n, 0:32], in1=dtb[0:n, :], op=ALU.add)
                R.op("act", "activation", [dtw.b[k.bi]], [dtw.b[k.bi]], out=dw, in_=dw, func=AF.Exp)
                R.op("act", "activation", [dtw.b[k.bi]], [dtw.b[k.bi]], out=dw, in_=dw, func=AF.Ln, bias=1.0)
                R.op("dve", "tensor_tensor", [dtw.b[k.bi]] + dconst.b, [dtA.b[k.bi]], out=dtA.ap[0:n, k.bi, :], in0=dw, in1=Aneg[0:n, :], op=ALU.mult)
                if "z2" in DBG:
                    psum.rel(ps)
                    continue
                vb = k.bi + 1
                if "z4" not in DBG:
                    R.op("act", "activation", ps.b, [v_all.b[vb]], out=v_all.ap[0:n, vb, :, 0:64], in_=ps.ap[0:n, 32:288].rearrange("p (j c) -> p j c", c=64),
                         func=AF.Copy)
                if k.last and "z5" not in DBG:
                    ko = kvout0
                    R.op("dve", "tensor_scalar", ps.b, ko.b, out=ko.ap[0:n, :], in0=ps.ap[0:n, 32:288], scalar1=1.0, scalar2=None, op0=ALU.mult)
                    if "z6" in DBG:
                        pass
                    elif k.is_samp:
                        misc_out("v", ko.b, out=v_s[k.sidx * samp_len:(k.sidx + 1) * samp_len, :], in_=ko.ap[0:n, :])
                    else:
                        misc_out("v", ko.b, out=v_p, in_=ko.ap[0:n, :])
                psum.rel(ps)
            for u in range(4):
                slot = wsA.acquire("z%d" % u)
                if "z3" in DBG:
                    continue
                wv = slot.ap.rearrange("p (k n) -> p k n", k=KC)
                for k in tl.blocks:
                    ps = psum.next()
                    for kc in range(KC):
                        R.op("pe", "matmul", slot.b + [uT.b[k.bi]], ps.b, ps.ap[0:n, :], lhsT=uT.ap[:, kc, k.col0:k.col0 + n], rhs=wv[:, kc, :],
                             start=(kc == 0), stop=(kc == KC - 1))
                    R.op("act", "activation", ps.b, [zs.b[k.bi]], out=zs.ap[0:n, k.bi, u * 512:(u + 1) * 512], in_=ps.ap[0:n, :], func=AF.Silu)
                    psum.rel(ps)

        def m3_ssd(tl, k, mid=None):
            L = k.n
            c0 = k.col0
            s = k.seq
            bi = k.bi
            dA = dtA.ap[0:L, bi, :]
            v3_ = lambda ap: ap.rearrange("p (h c) -> p h c", c=64)
            ps_s = psum.next()
            R.op("pe", "matmul", [dtA.b[bi]] + cbf.b, ps_s.b, ps_s.ap[0:L, 0:32], lhsT=tri[0:L, 0:L], rhs=dA, start=True, stop=True)
            R.op("pe", "matmul", [dtA.b[bi]] + cbf.b, ps_s.b, ps_s.ap[:, 32:64], lhsT=ones[0:L, :], rhs=dA, start=True, stop=True)
            R.op("act", "activation", ps_s.b, eacs.b, out=eacs.ap[0:L, 0:32], in_=ps_s.ap[0:L, 0:32], func=AF.Exp)
            R.op("act", "activation", ps_s.b, eacs.b, out=eacs.ap[:, 32:64], in_=ps_s.ap[:, 32:64], func=AF.Exp)
            psum.rel(ps_s)
            decay = eacs.ap[:, 32:64]
            ps_cb = psum.next()
            pcb = ps_cb.ap.rearrange("p (g l) -> p g l", l=128)
            for g in range(4):
                R.op("pe", "matmul", BT.b + CT.b, ps_cb.b, pcb[0:L, g, 0:L], lhsT=BT.ap[:, g, c0:c0 + L], rhs=CT.ap[:, g, c0:c0 + L], start=True, stop=True)
            R.op("dve", "tensor_tensor", ps_cb.b + cbf.b, cbm.b, out=cbm.ap[0:L, :, 0:L], in0=pcb[0:L, :, 0:L], in1=bcm(tri[0:L, 0:L], 4, L), op=ALU.mult)
            psum.rel(ps_cb)

            def stA(g):
                r1 = rhs1[g % 2]
                E = Eb[g % 2]
                gs = slice(g * 512, (g + 1) * 512)
                R.op("pool", "tensor_tensor", [dtA.b[bi]] + cbf.b, r1.b, out=r1.ap[0:L, :, 0:L], in0=bc3(dA[:, 8 * g:8 * g + 8], 8, L),
                     in1=bcm(tri[0:L, 0:L], 8, L), op=ALU.mult)
                R.op("pool", "tensor_tensor", [xh_tok.b[bi], dtw.b[bi]], [xdt.b[g]], out=v3_(xdt.ap[0:L, gs]),
                     in0=v3_(xh_tok.ap[0:L, bi, gs]), in1=bc3(dtw.ap[0:L, bi, 8 * g:8 * g + 8], 8, 64), op=ALU.mult)
                R.op("dve", "tensor_tensor", [xh_tok.b[bi]] + par.b, [xd.b[g]], out=v3_(xd.ap[0:L, gs]),
                     in0=v3_(xh_tok.ap[0:L, bi, gs]), in1=bc3(dskip[0:L, 8 * g:8 * g + 8], 8, 64), op=ALU.mult)
                for hh in range(2):
                    ps = psum.next()
                    pv_ = ps.ap.rearrange("p (h l) -> p h l", l=128)[0:L, :, 0:L]
                    h0 = 8 * g + 4 * hh
                    if L == 128:
                        R.op("pe", "matmul", r1.b + cbf.b, ps.b, ps.ap[:, :], lhsT=ones[0:L, 0:L], rhs=r1.ap[0:L, 4 * hh:4 * hh + 4, :].rearrange("p h l -> p (h l)"),
                             start=True, stop=False)
                        R.op("pe", "matmul", [dtA.b[bi]] + cbf.b, ps.b, ps.ap[:, :], lhsT=m2[0:L, 0:L], rhs=bc3(dA[:, h0:h0 + 4], 4, L),
                             start=False, stop=False)
                        R.op("pe", "matmul", cbf.b, ps.b, ps.ap[:, :], lhsT=ident[0:L, 0:L], rhs=negm[0:L, 0:4, :].rearrange("p h l -> p (h l)"), start=False, stop=True)
                    else:
                        for h4 in range(4):
                            o2 = pv_[:, h4, :]
                            R.op("pe", "matmul", r1.b + cbf.b, ps.b, o2, lhsT=ones[0:L, 0:L], rhs=r1.ap[0:L, 4 * hh + h4, 0:L], start=True, stop=False)
                            R.op("pe", "matmul", [dtA.b[bi]] + cbf.b, ps.b, o2, lhsT=m2[0:L, 0:L], rhs=dA[:, h0 + h4:h0 + h4 + 1].to_broadcast([L, L]),
                                 start=False, stop=False)
                            R.op("pe", "matmul", cbf.b, ps.b, o2, lhsT=ident[0:L, 0:L], rhs=negm[0:L, h4, 0:L], start=False, stop=True)
                    R.op("act", "activation", ps.b, E.b, out=E.ap[0:L, 4 * hh:4 * hh + 4, 0:L], in_=pv_, func=AF.Exp)
                    psum.rel(ps)
                R.op("pool", "tensor_tensor", [xdt.b[g]] + E.b, [xdtw.b[g]], out=v3_(xdtw.ap[0:L, gs]), in0=v3_(xdt.ap[0:L, gs]),
                     in1=E.ap[0:L, :, L - 1:L].to_broadcast([L, 8, 64]), op=ALU.mult)

            def stB(g):
                E = Eb[g % 2]
                G_ = Gm[g % 2]
                R.op("dve", "tensor_tensor", E.b + cbm.b, G_.b, out=G_.ap[0:L, :, 0:L], in0=E.ap[0:L, :, 0:L], in1=bcm(cbm.ap[0:L, g, 0:L], 8, L), op=ALU.mult)
                y_ps = psum.next()
                R.op("pe", "matmul", [xd.b[g]] + cbf.b, y_ps.b, y_ps.ap[0:L, :], lhsT=ident[0:L, 0:L], rhs=xd.ap[0:L, g * 512:(g + 1) * 512], start=True, stop=False)
                for h in range(8):
                    hh_ = 8 * g + h
                    R.op("pe", "matmul", G_.b + [xdt.b[g]], y_ps.b, y_ps.ap[0:L, h * 64:(h + 1) * 64], lhsT=G_.ap[0:L, h, 0:L], rhs=xdt.ap[0:L, hh_ * 64:(hh_ + 1) * 64],
                         start=False, stop=(h == 7))
                yi_ps = psum.next()
                R.op("pe", "matmul", CT.b + [s.hbf.b[g]], yi_ps.b, yi_ps.ap[0:L, :], lhsT=CT.ap[:, g, c0:c0 + L], rhs=s.hbf.ap[:, g * 512:(g + 1) * 512],
                     start=True, stop=True)
                a1 = t1[0]
                a2 = t2[g % 2]
                R.op("dve", "tensor_tensor", yi_ps.b + eacs.b, a1.b, out=v3_(a1.ap[0:L, :]), in0=v3_(yi_ps.ap[0:L, :]), in1=bc3(eacs.ap[0:L, 8 * g:8 * g + 8], 8, 64),
                     op=ALU.mult)
                R.op("dve", "tensor_tensor", y_ps.b + a1.b, a2.b, out=a2.ap[0:L, :], in0=y_ps.ap[0:L, :], in1=a1.ap[0:L, :], op=ALU.add)
                psum.rel(y_ps, yi_ps)
                R.op("pool", "tensor_tensor", a2.b + [zs.b[bi]], [yg.b[g]], out=yg.ap[0:L, g * 512:(g + 1) * 512], in0=a2.ap[0:L, :],
                     in1=zs.ap[0:L, bi, g * 512:(g + 1) * 512], op=ALU.mult)

            def stC(g):
                S_ps = psum.next()
                R.op("pe", "matmul", [B_tok.b[bi], xdtw.b[g]], S_ps.b, S_ps.ap[:, :], lhsT=B_tok.ap[0:L, bi, g * 128:(g + 1) * 128],
                     rhs=xdtw.ap[0:L, g * 512:(g + 1) * 512], start=True, stop=True)
                hsl = s.h.ap[:, g * 512:(g + 1) * 512]
                R.op("dve", "tensor_tensor", [s.h.b[g]] + eacs.b, [s.h.b[g]], out=v3_(hsl), in0=v3_(hsl), in1=bc3(decay[:, 8 * g:8 * g + 8], 8, 64), op=ALU.mult)
                R.op("dve", "tensor_tensor", [s.h.b[g]] + S_ps.b, [s.h.b[g]], out=hsl, in0=S_ps.ap[:, :], in1=hsl, op=ALU.add)
                psum.rel(S_ps)
                R.op("act", "activation", [s.h.b[g]], [s.hbf.b[g]], out=s.hbf.ap[:, g * 512:(g + 1) * 512], in_=hsl, func=AF.Copy)

            stA(0)
            if mid is not None:
                mid()
            for g in range(4):
                if g + 1 < 4:
                    stA(g + 1)
                stB(g)
                stC(g)

        def m3_norm(tl, k):
            L = k.n
            c0 = k.col0
            bi = k.bi
            st = scrA.new_stat()
            R.op("act", "activation", yg.b, scrA.junk.b + st.b, out=scrA.junk.ap[0:L, 0:DI], in_=yg.ap[0:L, :], func=AF.Square, accum_out=st.ap[0:L, 0:1])
            rstd_from(st.ap[0:L, 0:1], L, st, 1, DI, 1.0)
            R.op("act", "activation", yg.b + st.b, yn.b, out=yn.ap[0:L, :], in_=yg.ap[0:L, :], func=AF.Copy, scale=st.ap[0:L, 1:2])
            for i in range(2):
                ps = psum.next()
                ptv = ps.bf.rearrange("p (a b) -> p a b", b=128)
                for cc in range(8):
                    c = 8 * i + cc
                    R.op("pe", "transpose", yn.b + cbf.b, ps.b, ptv[:, cc, 0:L], yn.ap[0:L, c * 128:(c + 1) * 128], ident[0:L, 0:L])
                R.op("dve", "tensor_tensor", ps.b + par.b, [ymT.b[bi]], out=ymT.ap[:, 8 * i:8 * i + 8, c0:c0 + L], in0=ptv[:, :, 0:L],
                     in1=bc3(gmn[:, 8 * i:8 * i + 8], 8, L), op=ALU.mult)
                psum.rel(ps)

        def rope(ps, T, i, dst_ap, dst_bufs, kout_j=None):
            q_b = qb[i % 2]
            a1 = rt1[i % 2]
            a2 = rt2[i % 2]
            R.op("act", "activation", ps.b, q_b.b, out=q_b.ap[:, 0:T], in_=ps.ap[:, 0:T], func=AF.Copy)
            sw = psum.next()
            R.op("pe", "matmul", q_b.b + cbf.b, sw.b, sw.ap[:, 0:T], lhsT=rm, rhs=q_b.ap[:, 0:T], start=True, stop=True)
            R.op("dve", "tensor_tensor", ps.b + cs.b, a1.b, out=a1.ap[:, 0:T], in0=ps.ap[:, 0:T], in1=cs.ap[:, 0, 0:T], op=ALU.mult)
            R.op("dve", "tensor_tensor", sw.b + cs.b, a2.b, out=a2.ap[:, 0:T], in0=sw.ap[:, 0:T], in1=cs.ap[:, 1, 0:T], op=ALU.mult)
            psum.rel(sw)
            return a1, a2

        def m4_qkv(tl):
            T = tl.T
            n = tl.n
            R.op("sp", "dma_start", [], cs.b, out=cs.ap[:, :, 0:T], in_=cs_d[:, :, tl.cscol:tl.cscol + T], key=cs_key)
            i = 0
            for u in range(2):
                slot = wsA.acquire("q%d" % u)
                wv = slot.ap.rearrange("p (k cc c) -> p k cc c", k=KC, cc=4)
                for cc in range(4):
                    c = 4 * u + cc
                    ps = psum.next()
                    for kc in range(KC):
                        R.op("pe", "matmul", slot.b + uT.b[0:tl.nb], ps.b, ps.ap[:, 0:T], lhsT=wv[:, kc, cc, :], rhs=uT.ap[:, kc, 0:T],
                             start=(kc == 0), stop=(kc == KC - 1))
                    a1, a2 = rope(ps, T, i, None, None)
                    psum.rel(ps)
                    R.op("pool", "tensor_tensor", a1.b + a2.b, [qT.b[c]], out=qT.ap[:, c, 0:T], in0=a1.ap[:, 0:T], in1=a2.ap[:, 0:T], op=ALU.add)
                    i += 1
            slot = wsA.acquire("kdup")
            wv = slot.ap.rearrange("p (k j c) -> p k j c", k=KC, j=4)
            for j in range(4):
                ps = psum.next()
                for kc in range(KC):
                    R.op("pe", "matmul", slot.b + uT.b[0:tl.nb], ps.b, ps.ap[:, 0:T], lhsT=wv[:, kc, j, :], rhs=uT.ap[:, kc, 0:T],
                         start=(kc == 0), stop=(kc == KC - 1))
                a1, a2 = rope(ps, T, i, None, None)
                psum.rel(ps)
                kview = kT_all.ap[:, j, 128:128 + tl.nb * 128].rearrange("p (b w) -> p b w", w=128)[:, :, 0:n]
                R.op("pool", "tensor_tensor", a1.b + a2.b, kT_all.b[1:1 + tl.nb], out=kview, in0=a1.ap[:, 0:T].rearrange("p (b w) -> p b w", w=n),
                     in1=a2.ap[:, 0:T].rearrange("p (b w) -> p b w", w=n), op=ALU.add)
                if tl.blocks[-1].last:
                    if tl.is_samp:
                        lo, wd = 0, T
                    else:
                        lo, wd = T - 128, 128
                    R.op("dve", "tensor_tensor", a1.b + a2.b, krot_f.b, out=krot_f.ap[:, j, 0:wd], in0=a1.ap[:, lo:lo + wd], in1=a2.ap[:, lo:lo + wd], op=ALU.add)
                i += 1
            if tl.blocks[-1].last:
                wd = T if tl.is_samp else 128
                ps = psum.next()
                pv4 = ps.ap[:, 0:256].rearrange("p (j c) -> p j c", c=64)
                for j in range(4):
                    R.op("pe", "transpose", krot_f.b + par.b, ps.b, pv4[0:wd, j, :], krot_f.ap[0:64, j, 0:wd], identf[0:64, 0:64])
                ko = kvout1
                R.op("act", "activation", ps.b, ko.b, out=ko.ap[0:wd, :], in_=ps.ap[0:wd, 0:256], func=AF.Copy)
                psum.rel(ps)
                misc_out("k", ko.b, out=(k_s if tl.is_samp else k_p), in_=ko.ap[0:wd, :])

        def m4_attn(tl, k):
            n = k.n
            c0 = k.col0
            bi = k.bi
            keysets = []
            if k.is_samp:
                keysets.append((cacheT[k.sidx].ap, cacheT[k.sidx].b, cachev[k.sidx].ap, cachev[k.sidx].b, 128, None, None))
                ownb = (None, None)
            else:
                if not k.first:
                    keysets.append((kT_all.ap[:, :, bi * 128:(bi + 1) * 128], [kT_all.b[bi]], v_all.ap[:, bi, :, :], [v_all.b[bi]], 128, None, 0))
                ownb = (1, None)
            keysets.append((kT_all.ap[:, :, (bi + 1) * 128:(bi + 2) * 128], [kT_all.b[bi + 1]], v_all.ap[:, bi + 1, :, :], [v_all.b[bi + 1]], n, ownb[0], ownb[1]))
            nks = len(keysets)
            at = attn_tok[bi % 2]

            def scores(j):
                sbk = [psum.next(), psum.next()]
                svs = [b_.ap.rearrange("p (s q) -> p s q", q=128) for b_ in sbk]
                ptb = [PT[2 * (j % 2)], PT[2 * (j % 2) + 1]]
                for ks, ke in enumerate(keysets):
                    kap, kbufs, nk = ke[0], ke[1], ke[4]
                    for hl in range(4):
                        hf = hl % 2
                        hp = hl // 2
                        ch = (4 * j + hl) // 2
                        R.op("pe", "matmul", kbufs + [qT.b[ch]], sbk[hf].b, svs[hf][0:nk, ks * 2 + hp, 0:n], lhsT=kap[hf * 64:(hf + 1) * 64, j, 0:nk],
                             rhs=qT.ap[hf * 64:(hf + 1) * 64, ch, c0:c0 + n], start=True, stop=True)
                for hf in range(2):
                    pt = ptb[hf]
                    for ks, ke in enumerate(keysets):
                        nk, b0, b1 = ke[4], ke[5], ke[6]
                        sl = slice(2 * ks, 2 * ks + 2)
                        if b0 is None and b1 is None:
                            R.op("act", "activation", sbk[hf].b, pt.b, out=pt.ap[0:nk, sl, 0:n], in_=svs[hf][0:nk, sl, 0:n], func=AF.Exp, scale=0.125)
                        else:
                            for q0, bcol in ((0, b0), (64, b1)):
                                kw = {} if bcol is None else {"bias": negcol[0:nk, bcol:bcol + 1]}
                                R.op("act", "activation", sbk[hf].b + par.b, pt.b, out=pt.ap[0:nk, sl, q0:q0 + 64], in_=svs[hf][0:nk, sl, q0:q0 + 64],
                                     func=AF.Exp, scale=0.125, **kw)
                psum.rel(*sbk)
                return ptb

            def pv_out(j, ptb):
                o_ps = psum.next()
                ov = o_ps.ap[:, 0:260].rearrange("p (h c) -> p h c", c=65)
                for hl in range(4):
                    hf = hl % 2
                    hp = hl // 2
                    for ks, ke in enumerate(keysets):
                        vap, vbufs, nk = ke[2], ke[3], ke[4]
                        R.op("pe", "matmul", ptb[hf].b + vbufs, o_ps.b, ov[0:n, hl, :], lhsT=ptb[hf].ap[0:nk, ks * 2 + hp, 0:n], rhs=vap[0:nk, j, 0:65],
                             start=(ks == 0), stop=(ks == nks - 1))
                dn = den[j % 2]
                R.op("dve", "tensor_tensor", o_ps.b + dconst.b, dn.b, out=dn.ap[0:n, 0:4], in0=ov[0:n, :, 64], in1=esink[0:n, 4 * j:4 * j + 4], op=ALU.add)
                R.op("dve", "reciprocal", dn.b, dn.b, out=dn.ap[0:n, 4:8], in_=dn.ap[0:n, 0:4])
                R.op("dve", "tensor_tensor", o_ps.b + dn.b, at.b, out=at.ap[0:n, j * 256:(j + 1) * 256].rearrange("p (h c) -> p h c", c=64),
                     in0=ov[0:n, :, 0:64], in1=bc3(dn.ap[0:n, 4:8], 4, 64), op=ALU.mult)
                psum.rel(o_ps)

            pts = scores(0)
            for j in range(4):
                nxt = scores(j + 1) if j + 1 < 4 else None
                pv_out(j, pts)
                pts = nxt
            ps = psum.next()
            ptv = ps.bf.rearrange("p (a b) -> p a b", b=128)
            for c in range(8):
                R.op("pe", "transpose", at.b + cbf.b, ps.b, ptv[:, c, 0:n], at.ap[0:n, c * 128:(c + 1) * 128], ident[0:n, 0:n])
            R.op("act", "activation", ps.b, [attnT.b[bi]], out=attnT.ap[:, :, c0:c0 + n], in_=ptv[:, :, 0:n], func=AF.Copy)
            psum.rel(ps)

        def m4_save_prev(tl):
            if tl.is_samp:
                return
            R.op("act", "activation", [kT_all.b[NB]], [kT_all.b[0]], out=kT_all.ap[:, :, 0:128], in_=kT_all.ap[:, :, NB * 128:(NB + 1) * 128], func=AF.Copy)
            R.op("pool", "tensor_copy", [v_all.b[NB]], [v_all.b[0]], out=v_all.ap[:, 0, :, :], in_=v_all.ap[:, NB, :, :])

        def m5_branches(tl, xb):
            T = tl.T
            for u in range(4):
                nm = ("gm%d" if u % 2 == 0 else "ga%d") % (u // 2)
                slot = wsA.acquire(nm)
                wv = slot.ap.rearrange("p (k cc c) -> p k cc c", k=KC, cc=4)
                for cc in range(4):
                    fo = 4 * (u // 2) + cc
                    ps = psum.next()
                    for kc in range(KC):
                        R.op("pe", "matmul", slot.b + uT.b[0:tl.nb], ps.b, ps.ap[:, 0:T], lhsT=wv[:, kc, cc, :], rhs=uT.ap[:, kc, 0:T],
                             start=(kc == 0), stop=(kc == KC - 1))
                    gsb = gates[(u % 2)]
                    R.op("act", "activation", ps.b, [gsb.b[fo]], out=gsb.ap[:, fo, 0:T], in_=ps.ap[:, 0:T], func=AF.Sigmoid)
                    psum.rel(ps)
            for u in range(4):
                slot = wsA.acquire("brm%d" % u)
                wv = slot.ap.rearrange("p (k cc c) -> p k cc c", k=16, cc=2)
                for cc in range(2):
                    fo = 2 * u + cc
                    ps = psum.next()
                    for kc in range(16):
                        R.op("pe", "matmul", slot.b + ymT.b[0:tl.nb], ps.b, ps.ap[:, 0:T], lhsT=wv[:, kc, cc, :], rhs=ymT.ap[:, kc, 0:T],
                             start=(kc == 0), stop=(kc == 15))
                    R.op("dve", "tensor_tensor", ps.b + [gates[0].b[fo]], [mixf.b[fo]], out=mixf.ap[:, fo, 0:T], in0=ps.ap[:, 0:T], in1=gates[0].ap[:, fo, 0:T],
                         op=ALU.mult)
                    psum.rel(ps)
            for u in range(2):
                slot = wsA.acquire("bra%d" % u)
                wv = slot.ap.rearrange("p (k cc c) -> p k cc c", k=KC, cc=4)
                for cc in range(4):
                    fo = 4 * u + cc
                    ps = psum.next()
                    for kc in range(KC):
                        R.op("pe", "matmul", slot.b + attnT.b[0:tl.nb], ps.b, ps.ap[:, 0:T], lhsT=wv[:, kc, cc, :], rhs=attnT.ap[:, kc, 0:T],
                             start=(kc == 0), stop=(kc == KC - 1))
                    b_ = bt[fo % 2]
                    R.op("dve", "tensor_tensor", ps.b + [gates[1].b[fo]], b_.b, out=b_.ap[:, 0:T], in0=ps.ap[:, 0:T], in1=gates[1].ap[:, fo, 0:T], op=ALU.mult)
                    psum.rel(ps)
                    R.op("pool", "tensor_tensor", b_.b + [mixf.b[fo]], [mixedT.b[fo]], out=mixedT.ap[:, fo, 0:T], in0=b_.ap[:, 0:T], in1=mixf.ap[:, fo, 0:T],
                         op=ALU.add)
            accs = [[psum.next() for _ in tl.blocks] for _ in range(2)]
            for half in range(2):
                slot = wsA.acquire("wo%d" % half)
                wv = slot.ap.rearrange("p (k n) -> p k n", k=KC)
                for kc in range(KC):
                    for k in tl.blocks:
                        R.op("pe", "matmul", slot.b + [mixedT.b[kc]], accs[half][k.bi].b, accs[half][k.bi].ap[0:k.n, :],
                             lhsT=mixedT.ap[:, kc, k.col0:k.col0 + k.n], rhs=wv[:, kc, :], start=(kc == 0), stop=(kc == KC - 1))
            for k in tl.blocks:
                a2 = [accs[0][k.bi], accs[1][k.bi]]
                post_norm_add(xb, k, a2, gpost["m"], 1.0, scrA)
                psum.rel(*a2)


        def mixer(tl, xb):
            switch(M456, M123)
            norm_T(tl, xb, gpre["m"], uT, scrA)
            for k in tl.blocks:
                if k.is_samp and "st" in DBG:
                    load_halo(k)
            if "m1" in DBG:
                m1_xbc(tl)
            if "m2" in DBG:
                m2_dtv_z(tl)
            prevk = [None]
            for k in tl.blocks:
                if k.is_samp and "st" in DBG:
                    load_seq_state_ssm(k)

                def mid(pk=prevk[0]):
                    if pk is not None:
                        m3_norm(tl, pk)
                if k.is_samp or k.last:
                    mid()
                    m3_ssd(tl, k)
                    m3_norm(tl, k)
                    prevk[0] = None
                else:
                    m3_ssd(tl, k, mid)
                    prevk[0] = k
                if k.last and "st" in DBG:
                    store_seq_state(k)
            if prevk[0] is not None:
                m3_norm(tl, prevk[0])
            switch(M123, M456)
            for k in tl.blocks:
                if k.is_samp and "st" in DBG:
                    load_seq_state_kv(k)
            if "m4" in DBG:
                m4_qkv(tl)
            if "m4a" in DBG:
                for k in tl.blocks:
                    m4_attn(tl, k)
                m4_save_prev(tl)
            if "m5" in DBG:
                m5_branches(tl, xb)

    NT = len(tiles)
    for t in range(NT):
        wsA.seq += mix_unit_seq() if do_mix else []
    wsB.seq += ffn_unit_seq(0)
    for t in range(NT):
        if t > 0:
            wsB.seq += ffn_unit_seq(1)
        if t + 1 < NT:
            wsB.seq += ffn_unit_seq(0)
    wsB.seq += ffn_unit_seq(1)
    sched = Sched(KA, KB)
    load_x(0)
    ffn(tiles[0], x_bufs[0], 0)
    if NT > 1:
        load_x(1)
    for t in range(NT):
        tl = tiles[t]

        def fa(t=t, tl=tl):
            if do_mix:
                mixer(tl, x_bufs[t % 2])

        def pause(n):
            for _ in range(n):
                if not sched.yield_():
                    break

        def fb(t=t):
            if t > 0:
                pause(PAUSE1)
                ffn(tiles[t - 1], x_bufs[(t - 1) % 2], 1)
                store_x(t - 1)
                if t + 1 < NT:
                    load_x(t + 1)
                    pause(PAUSE2)
            if t + 1 < NT:
                ffn(tiles[t + 1], x_bufs[(t + 1) % 2], 0)

        if INTERLEAVE:
            R.sched = sched
            sched.run(fa, fb)
            R.sched = None
        else:
            fa()
            fb()
    ffn(tiles[NT - 1], x_bufs[(NT - 1) % 2], 1)
    store_x(NT - 1)
    R.emit()
    return nc


MX_BYTES = {2: 68352, 4: 150 * 1024}

def _layout(entries):
    d = {}
    o = 0
    for name, n in entries:
        d[name] = (o, n)
        o += n
    d["_n"] = o
    return d


PAR = _layout([("gpre1", 8), ("gpre2", 8), ("gmix", 8), ("gmn", 16), ("convw", 96), ("convb", 24),
               ("gpost1", 1024), ("gpost2", 1024), ("gpostm", 1024), ("dtb", 32), ("alog", 32), ("dskip", 32),
               ("sink", 16), ("identf", 128), ("negcol", 2)])
CB = _layout([("ident", 128), ("tri", 128), ("m2", 128), ("ones", 128), ("rm", 128), ("negm", 1024), ("maskA", 512), ("maskB", 512)])


def pack_params(inp):
    a = np.zeros((P, PAR["_n"]), np.float32)

    def put(name, v):
        o, n = PAR[name]
        a[:, o:o + n] = v

    def chunked(v):
        return np.asarray(v, np.float32).reshape(-1, P).T

    def bc(v):
        return np.broadcast_to(np.asarray(v, np.float32).reshape(1, -1), (P, np.asarray(v).size))

    put("gpre1", chunked(inp["ffn1_pre_g"][0]))
    put("gpre2", chunked(inp["ffn2_pre_g"][0]))
    put("gmix", chunked(inp["mix_pre_g"][0]))
    put("gmn", chunked(inp["m_norm_g"][0]))
    cw = np.asarray(inp["conv_w"][0], np.float32)
    put("convw", cw.reshape(4, 24, P).transpose(2, 1, 0).reshape(P, 96))
    put("convb", chunked(inp["conv_b"][0]))
    put("gpost1", bc(inp["ffn1_post_g"][0]))
    put("gpost2", bc(inp["ffn2_post_g"][0]))
    put("gpostm", bc(inp["mix_post_g"][0]))
    put("dtb", bc(inp["dt_bias"][0]))
    put("alog", bc(inp["a_log"][0]))
    put("dskip", bc(inp["d_skip"][0]))
    put("sink", bc(inp["attn_sink"][0]))
    put("identf", np.eye(P, dtype=np.float32))
    ncol = np.zeros((P, 2), np.float32)
    ncol[:64, 0] = -30000.0
    ncol[64:, 1] = -30000.0
    put("negcol", ncol)
    return a


def pack_consts():
    a = np.zeros((P, CB["_n"]), np.float32)

    def put(name, v):
        o, n = CB[name]
        a[:, o:o + n] = v.reshape(P, n)

    k = np.arange(P)[:, None]
    l = np.arange(P)[None, :]
    put("ident", np.eye(P))
    put("tri", (k <= l).astype(np.float32))
    put("m2", -(k <= l).astype(np.float32))
    put("ones", np.ones((P, P)))
    rmm = np.zeros((P, P), np.float32)
    for hb in (0, 64):
        for d2 in range(8):
            rmm[hb + d2 + 8, hb + d2] = 1.0
            rmm[hb + d2, hb + d2 + 8] = 1.0
    put("rm", rmm)
    ng = np.where(l < k, -30000.0, 0.0).astype(np.float32)
    put("negm", np.broadcast_to(ng[:, None, :], (P, 8, P)))
    mA = np.ones((P, P), np.float32)
    mA[:64, 64:] = 0.0
    mB = np.ones((P, P), np.float32)
    mB[64:, :64] = 0.0
    put("maskA", np.broadcast_to(mA[:, None, :], (P, 4, P)))
    put("maskB", np.broadcast_to(mB[:, None, :], (P, 4, P)))
    return a.astype(ml_dtypes.bfloat16)


def rope_tables(n_prompt, samp_len):
    pos = np.concatenate([np.arange(n_prompt), PAST_LEN + np.arange(samp_len)]).astype(np.float32)
    inv = (np.float32(THETA) ** (-np.arange(0, ROT, 2, dtype=np.float32) / np.float32(ROT))).astype(np.float32)
    ang = (pos[:, None] * inv[None, :]).astype(np.float32)
    c = np.cos(ang).astype(np.float32).T
    s = np.sin(ang).astype(np.float32).T
    cos_t = np.ones((P, pos.size), np.float32)
    sin_t = np.zeros((P, pos.size), np.float32)
    for hb in (0, 64):
        cos_t[hb:hb + 8] = c
        cos_t[hb + 8:hb + 16] = c
        sin_t[hb:hb + 8] = -s
        sin_t[hb + 8:hb + 16] = s
    return cos_t, sin_t


N_CORES = 8
NB_DEFAULT = 2
_cache = {}


def make_in_maps(inp, n_prompt, n_cores):
    f = lambda a: np.ascontiguousarray(np.asarray(a, dtype=np.float32))
    params = pack_params(inp)
    cbf = pack_consts()
    cos_t, sin_t = rope_tables(n_prompt, 32)
    cs = np.stack([np.concatenate([cos_t, cos_t[:, n_prompt:]], axis=1), np.concatenate([sin_t, sin_t[:, n_prompt:]], axis=1)], axis=1)
    cs = np.ascontiguousarray(cs, dtype=np.float32)
    shared = {
        "w_gu1": f(inp["ffn1_w_gu"][0]), "w_gu2": f(inp["ffn2_w_gu"][0]),
        "w_dn1": f(inp["ffn1_w_down"][0]), "w_dn2": f(inp["ffn2_w_down"][0]),
        "w_in": f(inp["w_in"][0]), "w_brm": f(inp["w_br_m"][0]), "w_bra": f(inp["w_br_a"][0]), "w_o": f(inp["w_o"][0]),
        "params": params, "cbf": cbf, "cs_t": cs,
    }
    maps = []
    for c in range(n_cores):
        m = dict(shared)
        m["xp"] = f(inp["x_prompt"][c])
        sl = slice(2 * c, 2 * c + 2)
        m["xs"] = f(inp["x_sample"][sl]).reshape(64, D)
        m["st_conv"] = f(inp["state_conv"][0][sl])
        m["st_ssm"] = f(inp["state_ssm"][0][sl]).reshape(2, DI, NS)
        m["ck"] = f(inp["cache_k"][0][sl]).reshape(2, 128, 256)
        m["cv"] = f(inp["cache_v"][0][sl]).reshape(2, 128, 256)
        maps.append(m)
    return maps


def gather(results, n_prompt, n_cores):
    g = lambda k: [np.asarray(r[k], dtype=np.float32) for r in results]
    rows = min(128, n_prompt)
    yp = np.stack(g("yp")).reshape(n_cores, n_prompt, D)
    ys = np.concatenate([a.reshape(2, 32, D) for a in g("ys")], 0)
    conv_p = np.stack(g("conv_p")).reshape(1, n_cores, 3, CONV)
    ssm_p = np.stack(g("ssm_p")).reshape(1, n_cores, NH, HP, NS)
    k_p = np.stack(g("k_p")).reshape(1, n_cores, rows, 4, 64)
    v_p = np.stack(g("v_p")).reshape(1, n_cores, rows, 4, 64)
    conv_s = np.concatenate(g("conv_s"), 0).reshape(1, 2 * n_cores, 3, CONV)
    ssm_s = np.concatenate(g("ssm_s"), 0).reshape(1, 2 * n_cores, NH, HP, NS)
    k_s = np.concatenate([a.reshape(2, 32, 4, 64) for a in g("k_s")], 0).reshape(1, 2 * n_cores, 32, 4, 64)
    v_s = np.concatenate([a.reshape(2, 32, 4, 64) for a in g("v_s")], 0).reshape(1, 2 * n_cores, 32, 4, 64)
    return (yp, ys, conv_p, ssm_p, k_p, v_p, conv_s, ssm_s, k_s, v_s)


def kernel(**inputs):
    n_prompt = int(np.asarray(inputs["x_prompt"]).shape[1])
    n_cores = int(np.asarray(inputs["x_prompt"]).shape[0])
    nc = build(n_prompt, NB_DEFAULT)
    maps = make_in_maps(inputs, n_prompt, n_cores)
    res = run_bass_kernel_spmd(nc, maps, core_ids=list(range(n_cores)))
    return gather(res.results, n_prompt, n_cores)
```
